# Optimizing a Trainium2 kernel written in Bass

```python
import math
import jax, jax.numpy as jnp
from jax import lax
import numpy as np

D_MODEL = 4096
BATCH = 4
SEQ = 2048
DEPTH = 2
DEC_BATCH = 128
DEC_SEQ = 1
PAST_LEN = 16384
PAGE_SIZE = 128

N_BRANCH = 3
BRANCH_DIM = D_MODEL // 2
RWKV_HEAD = 64
RWKV_HEADS = BRANCH_DIM // RWKV_HEAD
RWKV_LORA_W = max(32, int(round(1.8 * BRANCH_DIM ** 0.5 / 32)) * 32)
RWKV_LORA_A = max(32, int(round(1.8 * BRANCH_DIM ** 0.5 / 32)) * 32)
RWKV_LORA_G = max(32, int(round(0.6 * BRANCH_DIM ** 0.8 / 32)) * 32)
RWKV_COLS = 3 * BRANCH_DIM + RWKV_LORA_W + RWKV_LORA_A + RWKV_LORA_G
RWKV_LN_EPS = 64e-5
SSM_HEADDIM = 64
SSM_HEADS = BRANCH_DIM // SSM_HEADDIM
SSM_GROUPS = 4
SSM_STATE = 128
SSM_CONV = 4
SSM_CHUNK = 64
CONV_CH = BRANCH_DIM + 2 * SSM_GROUPS * SSM_STATE
SSM_COLS = BRANCH_DIM + CONV_CH + SSM_HEADS
GLA_HEADS = 4
GLA_KDIM = BRANCH_DIM // 2
GLA_VDIM = BRANCH_DIM
GLA_DK = GLA_KDIM // GLA_HEADS
GLA_DV = GLA_VDIM // GLA_HEADS
GLA_LORA = 16
GLA_TAU = 16.0
GLA_CHUNK = 64
GLA_COLS = 2 * GLA_KDIM + 2 * GLA_VDIM + GLA_LORA
IN_COLS = RWKV_COLS + SSM_COLS + GLA_COLS + N_BRANCH * D_MODEL
D_FF = 4 * D_MODEL
PLE_DIM = 256
EPS = 1e-6

kernel_name = "hybrid_rwkv7_mamba2_gla_decode_step"


def _rms(x, eps=EPS):
    xf = x.astype(jnp.float32)
    return xf * lax.rsqrt(jnp.mean(xf * xf, axis=-1, keepdims=True) + eps)


def rms_norm(x, g):
    return (_rms(x) * g.astype(jnp.float32)).astype(x.dtype)


def causal_conv(u, buf, w, b):
    L = u.shape[1]
    cat = jnp.concatenate([buf.astype(u.dtype), u], axis=1)
    out = b + sum(cat[:, j:j + L] * w[j] for j in range(w.shape[0]))
    return out, cat[:, L:]


def _pad_chunks(t, c):
    L = t.shape[1]
    n = -(-L // c)
    t = jnp.pad(t, [(0, 0), (0, n * c - L)] + [(0, 0)] * (t.ndim - 2))
    return t.reshape(t.shape[0], n, c, *t.shape[2:])


def rwkv7_scan(r, log_w, k, v, a, b, s0):
    def step(s, inp):
        r_t, lw_t, k_t, v_t, a_t, b_t = inp
        sa = jnp.einsum('bhvk,bhk->bhv', s, a_t)
        s = (s * jnp.exp(lw_t)[:, :, None, :] + sa[..., None] * b_t[:, :, None, :]
             + v_t[..., None] * k_t[:, :, None, :])
        return s, jnp.einsum('bhvk,bhk->bhv', s, r_t)
    xs = tuple(jnp.moveaxis(t, 1, 0) for t in (r, log_w, k, v, a, b))
    s_t, o = lax.scan(step, s0.astype(jnp.float32), xs)
    return jnp.moveaxis(o, 0, 1), s_t


def rwkv7_mix(u, shift_prev, s0, mu, w0, w2, a0, a2, g2, k_k, k_a, r_k, ln_w, ln_b):
    Bt, L, _ = u.shape
    prev = jnp.concatenate([shift_prev[:, None].astype(u.dtype), u[:, :-1]], axis=1)
    xs = u + (prev - u) * mu
    new_shift = u[:, -1]
    r, k, v, wl, al, gl = jnp.split(
        xs, [BRANCH_DIM, 2 * BRANCH_DIM, 3 * BRANCH_DIM,
             3 * BRANCH_DIM + RWKV_LORA_W, 3 * BRANCH_DIM + RWKV_LORA_W + RWKV_LORA_A], axis=-1)
    w = -jax.nn.softplus(-(w0 + jnp.tanh(wl) @ w2)) - 0.5
    log_w = -jnp.exp(w)
    a = jax.nn.sigmoid(a0 + al @ a2)
    g = jax.nn.sigmoid(gl) @ g2
    hd = lambda t: t.reshape(Bt, L, RWKV_HEADS, RWKV_HEAD)
    kk = hd(k * k_k)
    kk = kk / jnp.maximum(jnp.sqrt(jnp.sum(kk * kk, axis=-1, keepdims=True)), 1e-12)
    k = k * (1.0 + (a - 1.0) * k_a)
    r, k, v, a, log_w = hd(r), hd(k), hd(v), hd(a), hd(log_w)
    o, s_t = rwkv7_scan(r, log_w, k, v, -kk, kk * a, s0)
    mean = jnp.mean(o, axis=-1, keepdims=True)
    var = jnp.mean(jnp.square(o - mean), axis=-1, keepdims=True)
    o = ((o - mean) * lax.rsqrt(var + RWKV_LN_EPS)).reshape(Bt, L, BRANCH_DIM) * ln_w + ln_b
    bonus = jnp.sum(r * k * r_k, axis=-1, keepdims=True) * v
    o = (o + bonus.reshape(Bt, L, BRANCH_DIM)) * g
    return o, new_shift, s_t


def ssd_chunked(x, dt, a, bm, cm, h0):
    Bt, L = x.shape[:2]
    c = min(SSM_CHUNK, L)
    xdt = _pad_chunks(x * dt[..., None], c)
    la = _pad_chunks(dt * a, c)
    bm, cm = _pad_chunks(bm, c), _pad_chunks(cm, c)
    cum = jnp.cumsum(la, axis=2)
    seg = cum[:, :, :, None] - cum[:, :, None, :]
    causal = jnp.tril(jnp.ones((c, c), dtype=bool))[:, :, None, None]
    lmat = jnp.exp(jnp.where(causal, seg, -jnp.inf))
    cb = jnp.einsum('bctgn,bcsgn->bctsg', cm, bm)
    y_diag = jnp.einsum('bctsg,bctsge,bcsgep->bctgep', cb, lmat, xdt)
    decay_to_end = jnp.exp(cum[:, :, -1:] - cum)
    chunk_state = jnp.einsum('bcsgn,bcsge,bcsgep->bcgepn', bm, decay_to_end, xdt)
    chunk_decay = jnp.exp(cum[:, :, -1])

    def step(h, inp):
        st, dec = inp
        return h * dec[..., None, None] + st, h
    h_t, h_prev = lax.scan(step, h0.astype(jnp.float32),
                           (jnp.moveaxis(chunk_state, 1, 0), jnp.moveaxis(chunk_decay, 1, 0)))
    h_prev = jnp.moveaxis(h_prev, 0, 1)
    y_off = jnp.einsum('bctgn,bcgepn,bctge->bctgep', cm, h_prev, jnp.exp(cum))
    n = cum.shape[1]
    y = (y_diag + y_off).reshape(Bt, n * c, *x.shape[2:])[:, :L]
    return y, h_t


def mamba2_mix(u, conv_buf, h0, conv_w, conv_b, dt_bias, a_log, d_skip, norm_w):
    Bt, L, _ = u.shape
    e = SSM_HEADS // SSM_GROUPS
    z, xbc, dt = jnp.split(u, [BRANCH_DIM, BRANCH_DIM + CONV_CH], axis=-1)
    xbc, new_buf = causal_conv(xbc, conv_buf, conv_w, conv_b)
    xbc = jax.nn.silu(xbc)
    xs, bm, cm = jnp.split(xbc, [BRANCH_DIM, BRANCH_DIM + SSM_GROUPS * SSM_STATE], axis=-1)
    xs = xs.reshape(Bt, L, SSM_GROUPS, e, SSM_HEADDIM)
    bm = bm.reshape(Bt, L, SSM_GROUPS, SSM_STATE)
    cm = cm.reshape(Bt, L, SSM_GROUPS, SSM_STATE)
    dt = jax.nn.softplus(dt + dt_bias).reshape(Bt, L, SSM_GROUPS, e)
    a = -jnp.exp(a_log.astype(jnp.float32)).reshape(SSM_GROUPS, e)
    y, h_t = ssd_chunked(xs, dt, a, bm, cm,
                         h0.reshape(Bt, SSM_GROUPS, e, SSM_HEADDIM, SSM_STATE))
    y = y + d_skip.reshape(SSM_GROUPS, e, 1) * xs
    y = y.reshape(Bt, L, BRANCH_DIM) * jax.nn.silu(z)
    y = _rms(y.reshape(Bt, L, SSM_GROUPS, BRANCH_DIM // SSM_GROUPS)).reshape(Bt, L, BRANCH_DIM) * norm_w
    return y, new_buf, h_t.reshape(Bt, SSM_HEADS, SSM_HEADDIM, SSM_STATE)


def gla_chunked(q, k, v, lg, s0):
    Bt, L = q.shape[:2]
    c = min(GLA_CHUNK, L)
    q, k, v, lg = (_pad_chunks(t, c) for t in (q, k, v, lg))
    b = jnp.cumsum(lg, axis=2)
    b_last = b[:, :, -1:]
    q_in = q * jnp.exp(b)
    k_in = k * jnp.exp(-b)
    k_end = k * jnp.exp(b_last - b)
    causal = jnp.tril(jnp.ones((c, c), dtype=bool))
    att = jnp.where(causal, jnp.einsum('bcthk,bcshk->bchts', q_in, k_in), 0.0)
    o_intra = jnp.einsum('bchts,bcshv->bcthv', att, v)
    chunk_state = jnp.einsum('bcshk,bcshv->bchkv', k_end, v)
    chunk_decay = jnp.exp(b_last[:, :, 0])

    def step(s, inp):
        st, dec = inp
        return s * dec[..., None] + st, s
    s_t, s_prev = lax.scan(step, s0.astype(jnp.float32),
                           (jnp.moveaxis(chunk_state, 1, 0), jnp.moveaxis(chunk_decay, 1, 0)))
    s_prev = jnp.moveaxis(s_prev, 0, 1)
    o = o_intra + jnp.einsum('bcthk,bchkv->bcthv', q_in, s_prev)
    n = o.shape[1]
    return o.reshape(Bt, n * c, *o.shape[3:])[:, :L], s_t


def gla_mix(u, s0, f_up, f_bias, norm_w):
    Bt, L, _ = u.shape
    q, k, v, g, fl = jnp.split(
        u, [GLA_KDIM, 2 * GLA_KDIM, 2 * GLA_KDIM + GLA_VDIM, 2 * GLA_KDIM + 2 * GLA_VDIM], axis=-1)
    lg = jax.nn.log_sigmoid(fl @ f_up + f_bias) / GLA_TAU
    q = q.reshape(Bt, L, GLA_HEADS, GLA_DK) * (GLA_DK ** -0.5)
    k = k.reshape(Bt, L, GLA_HEADS, GLA_DK)
    v = v.reshape(Bt, L, GLA_HEADS, GLA_DV)
    lg = lg.reshape(Bt, L, GLA_HEADS, GLA_DK)
    o, s_t = gla_chunked(q, k, v, lg, s0)
    o = (_rms(o) * norm_w).reshape(Bt, L, GLA_VDIM) * jax.nn.silu(g)
    return o, s_t


def decoder_layer(x, ple, s_rwkv, s_shift, s_ssm, s_conv, s_gla,
                  norm_mix, w_in, rw_mu, rw_w0, rw_w2, rw_a0, rw_a2, rw_g2, rw_kk, rw_ka, rw_rk,
                  rw_ln_w, rw_ln_b, ssm_conv_w, ssm_conv_b, ssm_dt_bias, ssm_a_log, ssm_d, ssm_norm,
                  gla_f_up, gla_f_bias, gla_norm, w_branch, w_out, norm_ffn, w_ff1, w_ff2,
                  norm_ple, w_ple_gate, w_ple_proj):
    h = rms_norm(x, norm_mix)
    u = (h @ w_in).astype(jnp.float32)
    o1 = RWKV_COLS
    o2 = o1 + SSM_COLS
    o3 = o2 + GLA_COLS
    u_rw, u_ssm, u_gla, u_gate = jnp.split(u, [o1, o2, o3], axis=-1)
    y_rw, new_shift, new_rwkv = rwkv7_mix(u_rw, s_shift, s_rwkv, rw_mu, rw_w0, rw_w2, rw_a0, rw_a2,
                                          rw_g2, rw_kk, rw_ka, rw_rk, rw_ln_w, rw_ln_b)
    y_ssm, new_conv, new_ssm = mamba2_mix(u_ssm, s_conv, s_ssm, ssm_conv_w, ssm_conv_b, ssm_dt_bias,
                                          ssm_a_log, ssm_d, ssm_norm)
    y_gla, new_gla = gla_mix(u_gla, s_gla, gla_f_up, gla_f_bias, gla_norm)
    gates = jax.nn.sigmoid(u_gate).reshape(*u_gate.shape[:-1], N_BRANCH, D_MODEL).astype(x.dtype)
    merged = sum(gates[..., j, :] * (o.astype(x.dtype) @ w_branch[j])
                 for j, o in enumerate((y_rw, y_ssm, y_gla)))
    x = x + merged @ w_out
    hf = rms_norm(x, norm_ffn)
    x = x + jnp.square(jax.nn.relu(hf @ w_ff1)) @ w_ff2
    x = x + (ple @ w_ple_proj) * jax.nn.sigmoid(rms_norm(x, norm_ple) @ w_ple_gate)
    return x, (new_rwkv, new_shift, new_ssm, new_conv, new_gla)


def setup_inputs(seed: int = 0) -> dict:
    key = jax.random.key(seed)
    keys = iter(jax.random.split(key, 64))
    nrm = lambda shape, s=1.0: jax.random.normal(next(keys), shape, jnp.float32) * s
    uni = lambda shape, lo, hi: jax.random.uniform(next(keys), shape, jnp.float32, lo, hi)
    gain = lambda shape: 1.0 + nrm(shape, 0.02)
    dt0 = jnp.exp(uni((DEPTH, SSM_HEADS), math.log(1e-3), math.log(1e-1)))
    return {
        "x_prompt": nrm((BATCH, SEQ, D_MODEL)),
        "x_sample": nrm((DEC_BATCH, DEC_SEQ, D_MODEL)),
        "p_prompt": nrm((DEPTH, BATCH, SEQ, PLE_DIM)),
        "p_sample": nrm((DEPTH, DEC_BATCH, DEC_SEQ, PLE_DIM)),
        "state_rwkv": nrm((DEPTH, DEC_BATCH, RWKV_HEADS, RWKV_HEAD, RWKV_HEAD), 0.3),
        "state_shift": nrm((DEPTH, DEC_BATCH, RWKV_COLS)),
        "state_ssm": nrm((DEPTH, DEC_BATCH, SSM_HEADS, SSM_HEADDIM, SSM_STATE), 0.3),
        "state_conv": nrm((DEPTH, DEC_BATCH, SSM_CONV - 1, CONV_CH)),
        "state_gla": nrm((DEPTH, DEC_BATCH, GLA_HEADS, GLA_DK, GLA_DV), 0.3),
        "norm_mix": gain((DEPTH, D_MODEL)),
        "w_in": nrm((DEPTH, D_MODEL, IN_COLS), D_MODEL ** -0.5),
        "rw_mu": uni((DEPTH, RWKV_COLS), 0.0, 1.0),
        "rw_w0": uni((DEPTH, BRANCH_DIM), -6.0, 0.0),
        "rw_w2": nrm((DEPTH, RWKV_LORA_W, BRANCH_DIM), 0.1),
        "rw_a0": nrm((DEPTH, BRANCH_DIM), 0.1),
        "rw_a2": nrm((DEPTH, RWKV_LORA_A, BRANCH_DIM), 0.1),
        "rw_g2": nrm((DEPTH, RWKV_LORA_G, BRANCH_DIM), RWKV_LORA_G ** -0.5),
        "rw_kk": 0.85 + nrm((DEPTH, BRANCH_DIM), 0.05),
        "rw_ka": 1.0 + nrm((DEPTH, BRANCH_DIM), 0.05),
        "rw_rk": nrm((DEPTH, RWKV_HEADS, RWKV_HEAD), 0.1),
        "rw_ln_w": gain((DEPTH, BRANCH_DIM)),
        "rw_ln_b": nrm((DEPTH, BRANCH_DIM), 0.02),
        "ssm_conv_w": nrm((DEPTH, SSM_CONV, CONV_CH), SSM_CONV ** -0.5),
        "ssm_conv_b": nrm((DEPTH, CONV_CH), 0.02),
        "ssm_dt_bias": dt0 + jnp.log(-jnp.expm1(-dt0)),
        "ssm_a_log": jnp.log(uni((DEPTH, SSM_HEADS), 1.0, 16.0)),
        "ssm_d": 1.0 + nrm((DEPTH, SSM_HEADS), 0.1),
        "ssm_norm": gain((DEPTH, BRANCH_DIM)),
        "gla_f_up": nrm((DEPTH, GLA_LORA, GLA_KDIM), GLA_LORA ** -0.5),
        "gla_f_bias": nrm((DEPTH, GLA_KDIM), 0.5),
        "gla_norm": gain((DEPTH, GLA_DV)),
        "w_branch": nrm((DEPTH, N_BRANCH, BRANCH_DIM, D_MODEL), BRANCH_DIM ** -0.5),
        "w_out": nrm((DEPTH, D_MODEL, D_MODEL), D_MODEL ** -0.5),
        "norm_ffn": gain((DEPTH, D_MODEL)),
        "w_ff1": nrm((DEPTH, D_MODEL, D_FF), D_MODEL ** -0.5),
        "w_ff2": nrm((DEPTH, D_FF, D_MODEL), D_FF ** -0.5),
        "norm_ple": gain((DEPTH, D_MODEL)),
        "w_ple_gate": nrm((DEPTH, D_MODEL, D_MODEL), D_MODEL ** -0.5),
        "w_ple_proj": nrm((DEPTH, PLE_DIM, D_MODEL), PLE_DIM ** -0.5),
        "norm_final": gain((D_MODEL,)),
    }


def reference(x_prompt, x_sample, p_prompt, p_sample, state_rwkv, state_shift, state_ssm,
              state_conv, state_gla, norm_mix, w_in, rw_mu, rw_w0, rw_w2, rw_a0, rw_a2, rw_g2,
              rw_kk, rw_ka, rw_rk, rw_ln_w, rw_ln_b, ssm_conv_w, ssm_conv_b, ssm_dt_bias,
              ssm_a_log, ssm_d, ssm_norm, gla_f_up, gla_f_bias, gla_norm, w_branch, w_out,
              norm_ffn, w_ff1, w_ff2, norm_ple, w_ple_gate, w_ple_proj, norm_final):
    layer_w = (norm_mix, w_in, rw_mu, rw_w0, rw_w2, rw_a0, rw_a2, rw_g2, rw_kk, rw_ka, rw_rk,
               rw_ln_w, rw_ln_b, ssm_conv_w, ssm_conv_b, ssm_dt_bias, ssm_a_log, ssm_d, ssm_norm,
               gla_f_up, gla_f_bias, gla_norm, w_branch, w_out, norm_ffn, w_ff1, w_ff2,
               norm_ple, w_ple_gate, w_ple_proj)

    def trunk(x, ple, s_rwkv, s_shift, s_ssm, s_conv, s_gla):
        per_layer = []
        for i in range(DEPTH):
            x, st = decoder_layer(x, ple[i], s_rwkv[i], s_shift[i], s_ssm[i], s_conv[i], s_gla[i],
                                  *(w[i] for w in layer_w))
            per_layer.append(st)
        new = [jnp.stack([st[j] for st in per_layer]) for j in range(5)]
        return rms_norm(x, norm_final), new

    bp = x_prompt.shape[0]
    z = lambda *shape: jnp.zeros((DEPTH, bp) + shape, jnp.float32)
    y_prompt, (rwkv_p, shift_p, ssm_p, conv_p, gla_p) = trunk(
        x_prompt, p_prompt,
        z(RWKV_HEADS, RWKV_HEAD, RWKV_HEAD), z(RWKV_COLS),
        z(SSM_HEADS, SSM_HEADDIM, SSM_STATE), z(SSM_CONV - 1, CONV_CH),
        z(GLA_HEADS, GLA_DK, GLA_DV))
    y_sample, (rwkv_s, shift_s, ssm_s, conv_s, gla_s) = trunk(
        x_sample, p_sample, state_rwkv, state_shift, state_ssm, state_conv, state_gla)
    return (y_prompt, y_sample, rwkv_p, rwkv_s, shift_p, shift_s, ssm_p, ssm_s,
            conv_p, conv_s, gla_p, gla_s)
```

```python
import numpy as np
import concourse.bass as bass
import concourse.mybir as mybir
from concourse.bass_utils import run_bass_kernel_spmd
from contextlib import ExitStack

F32 = mybir.dt.float32
BF16 = mybir.dt.bfloat16
AF = mybir.ActivationFunctionType
ALU = mybir.AluOpType
AX = mybir.AxisListType


class PG:
    def __init__(self, nc, es):
        self.nc = nc
        self.es = es
        self.E = {'pe': nc.tensor, 'dve': nc.vector, 'act': nc.scalar,
                  'pool': nc.gpsimd, 'sp': nc.sync}
        self.sem = {}
        self.cnt = {}
        for e in ('pe', 'dve', 'act', 'pool'):
            self.sem[e] = es.enter_context(nc.semaphore('s_' + e))
            self.cnt[e] = 0
        self.dq = {}
        for q, n in (('sp', 16), ('pool', 8), ('act', 6)):
            names = []
            for i in range(n):
                nm = 'd_%s%d' % (q, i)
                self.sem[nm] = es.enter_context(nc.semaphore(nm))
                self.cnt[nm] = 0
                names.append(nm)
            self.dq[q] = names
        self.rr = {'sp': 0, 'pool': 0, 'act': 0}
        self.waited = {e: {} for e in self.E}
        self.lastw = {}
        self.rd = {}
        self.nops = 0
        self._uid = 0

    def uid(self, p='t'):
        self._uid += 1
        return '%s%d' % (p, self._uid)

    def sb(self, es, shape, dt, name=None):
        return es.enter_context(self.nc.sbuf_tensor(self.uid('sb_' + (name or '')), list(shape), dt))

    def ps(self, es, shape, dt=F32, name=None):
        return es.enter_context(self.nc.psum_tensor(self.uid('pz_' + (name or '')), list(shape), dt))

    def _wait(self, e, toks):
        need = {}
        for (s, v) in toks:
            if s == e and e == 'pe':
                continue
            if v > need.get(s, 0):
                need[s] = v
        w = self.waited[e]
        for s, v in need.items():
            if w.get(s, 0) < v:
                self.E[e].wait_ge(self.sem[s], v)
                w[s] = v

    def _deps(self, reads, writes):
        toks = []
        for k in reads:
            t = self.lastw.get(k)
            if t:
                toks.append(t)
        for k in writes:
            t = self.lastw.get(k)
            if t:
                toks.append(t)
            r = self.rd.get(k)
            if r:
                toks.extend(r.items())
        return toks

    def _commit(self, tok, reads, writes):
        for k in reads:
            d = self.rd.setdefault(k, {})
            if d.get(tok[0], 0) < tok[1]:
                d[tok[0]] = tok[1]
        for k in writes:
            self.lastw[k] = tok
            self.rd[k] = {}

    def op(self, e, fn, reads=(), writes=()):
        self._wait(e, self._deps(reads, writes))
        inst = fn(self.E[e])
        self.cnt[e] += 1
        inst.then_inc(self.sem[e], 1)
        self._commit((e, self.cnt[e]), reads, writes)
        self.nops += 1

    def dma(self, q, out, in_, reads=(), writes=(), **kw):
        names = self.dq[q]
        s = names[self.rr[q] % len(names)]
        self.rr[q] += 1
        toks = self._deps(reads, writes)
        if self.cnt[s] > 0:
            toks.append((s, self.cnt[s]))
        self._wait(q, toks)
        inst = self.E[q].dma_start(out=out, in_=in_, **kw)
        self.cnt[s] += 16
        inst.then_inc(self.sem[s], 16)
        self._commit((s, self.cnt[s]), reads, writes)
        self.nops += 1

    def barrier(self):
        toks = [(s, c) for s, c in self.cnt.items() if c > 0]
        for e in self.E:
            self._wait(e, toks)


def token_tiles(T, w=512):
    out = []
    t = 0
    while t < T:
        out.append((t, min(w, T - t)))
        t += w
    return out


class Ctx:
    def __init__(self, pg, es, T, KCX=32):
        self.pg = pg
        self.T = T
        nc = pg.nc
        self.XTflat = pg.sb(es, [128, max(KCX * T, 128 * 512)], BF16, 'XT')
        self.XT = self.XTflat[:, 0:KCX * T].rearrange("p (k t) -> p k t", t=T)
        self.NW = 4
        self.wbuf = [pg.sb(es, [128, 16, 256], BF16, 'wb%d' % i) for i in range(self.NW)]
        self.wi = 0
        self.NB = 8
        self.psb = [pg.ps(es, [128, 512], F32, 'psb%d' % i) for i in range(self.NB)]
        self.bi = 0
        self.NS = 4
        self.stg = [pg.sb(es, [128, 512], F32, 'stg%d' % i) for i in range(self.NS)]
        self.si = 0
        self.aux = [pg.sb(es, [128, 512], F32, 'aux%d' % i) for i in range(self.NS)]
        self.auxb = [pg.sb(es, [128, 512], BF16, 'auxb%d' % i) for i in range(self.NS)]
        self.ai = 0
        self.ost = [pg.sb(es, [128, 512], BF16, 'ost%d' % i) for i in range(self.NS)]
        self.oi = 0
        self.ones = pg.sb(es, [128, 128], F32, 'ones')
        pg.op('dve', lambda e: e.memset(self.ones[:], 1.0), writes=[('ones',)])
        self.epsc = pg.sb(es, [128, 1], F32, 'epsc')
        pg.op('dve', lambda e: e.memset(self.epsc[:], 1e-6), writes=[('epsc',)])

    def bank(self):
        b = self.bi % self.NB
        self.bi += 1
        return b


def phase_norm(cx, xT, gcol, D, eps, tiles=None, dst=None):
    pg = cx.pg
    KC = D // 128
    for (t0, tw) in (tiles or token_tiles(cx.T)):
        b = cx.bank()
        for kc in range(KC):
            s = cx.si % cx.NS
            cx.si += 1
            pg.dma('sp', cx.stg[s][:, 0:tw], xT[kc * 128:(kc + 1) * 128, t0:t0 + tw],
                   reads=[('xT', kc, t0)], writes=[('stg', s)])
            a = cx.ai % cx.NS
            cx.ai += 1
            pg.op('act', lambda e, s=s, a=a: e.activation(out=cx.aux[a][:, 0:tw], in_=cx.stg[s][:, 0:tw],
                                                         func=AF.Square),
                  reads=[('stg', s)], writes=[('aux', a)])
            pg.op('pe', lambda e, a=a, kc=kc: e.matmul(cx.psb[b][:, 0:tw], cx.ones[:], cx.aux[a][:, 0:tw],
                                                      start=(kc == 0), stop=(kc == KC - 1)),
                  reads=[('aux', a), ('ones',)], writes=[('ps', b)])
        a = cx.ai % cx.NS
        cx.ai += 1
        pg.op('act', lambda e: e.activation(out=cx.aux[a][:, 0:tw], in_=cx.psb[b][:, 0:tw],
                                            func=AF.Sqrt, bias=cx.epsc[:, 0:1], scale=1.0 / D),
              reads=[('ps', b), ('epsc',)], writes=[('aux', a)])
        pg.op('dve', lambda e: e.reciprocal(out=cx.aux[a][:, 0:tw], in_=cx.aux[a][:, 0:tw]),
              reads=[('aux', a)], writes=[('aux', a)])
        for kc in range(KC):
            s = cx.si % cx.NS
            cx.si += 1
            pg.dma('sp', cx.stg[s][:, 0:tw], xT[kc * 128:(kc + 1) * 128, t0:t0 + tw],
                   reads=[('xT', kc, t0)], writes=[('stg', s)])
            if dst is None:
                pg.op('dve', lambda e, s=s, kc=kc: e.scalar_tensor_tensor(
                    out=cx.XT[:, kc, t0:t0 + tw], in0=cx.stg[s][:, 0:tw], scalar=gcol[:, kc:kc + 1],
                    in1=cx.aux[a][:, 0:tw], op0=ALU.mult, op1=ALU.mult),
                    reads=[('stg', s), ('aux', a), ('gam',)], writes=[('XT',)])
            else:
                pg.op('dve', lambda e, s=s, kc=kc: e.scalar_tensor_tensor(
                    out=cx.stg[s][:, 0:tw], in0=cx.stg[s][:, 0:tw], scalar=gcol[:, kc:kc + 1],
                    in1=cx.aux[a][:, 0:tw], op0=ALU.mult, op1=ALU.mult),
                    reads=[('stg', s), ('aux', a), ('gam',)], writes=[('stg', s)])
                pg.dma('sp', dst[kc * 128:(kc + 1) * 128, t0:t0 + tw], cx.stg[s][:, 0:tw],
                       reads=[('stg', s)], writes=[('yT', kc, t0)])


def col_groups(segs, gw=256):
    out = []
    for (s0, w) in segs:
        c = 0
        while c < w:
            g = min(gw, w - c)
            blocks = []
            o = 0
            while o < g:
                blocks.append((o, min(128, g - o)))
                o += 128
            out.append((s0 + c, g, blocks))
            c += g
    return out


def linear_A(cx, W, K, segs, epi, xkey=('XT',), tiles=None):
    pg = cx.pg
    KC = K // 128
    KP = (KC + 15) // 16
    groups = col_groups(segs)
    tiles = tiles or token_tiles(cx.T)
    loaded = {}
    order = [(gi, kp) for gi in range(len(groups)) for kp in range(KP)]
    state = {'next': 0}

    def issue_loads(upto):
        while state['next'] < len(order) and state['next'] < upto:
            gi, kp = order[state['next']]
            c0, gw, _ = groups[gi]
            slot = cx.wi % cx.NW
            cx.wi += 1
            k0 = kp * 16
            nk = min(16, KC - k0)
            src = W[k0 * 128:(k0 + nk) * 128, c0:c0 + gw].rearrange("(kc p) c -> p kc c", p=128)
            pg.dma('pool', cx.wbuf[slot][:, 0:nk, 0:gw], src, reads=[], writes=[('w', slot)])
            loaded[(gi, kp)] = slot
            state['next'] += 1

    for gi, (c0, gw, blocks) in enumerate(groups):
        issue_loads((gi + 1) * KP + min(cx.NW - KP, KP))
        for (t0, tw) in tiles:
            for (off, cw) in blocks:
                b = cx.bank()
                for kc in range(KC):
                    slot = loaded[(gi, kc // 16)]
                    pg.op('pe', lambda e, slot=slot, kc=kc: e.matmul(
                        cx.psb[b][0:cw, 0:tw], cx.wbuf[slot][:, kc % 16, off:off + cw],
                        cx.XT[:, kc, t0:t0 + tw], start=(kc == 0), stop=(kc == KC - 1)),
                        reads=[('w', slot), xkey], writes=[('ps', b)])
                epi(c0 + off, cw, t0, tw, cx.psb[b][0:cw, 0:tw], ('ps', b))


def make_epi(cx, dst, dkey, func=AF.Identity, mul=None, mulkey=None, add=None, addkey=None,
             square=False, rowoff=0):
    pg = cx.pg
    out_bf = (dst.dtype == BF16)

    def epi(c0, cw, t0, tw, ps, pskey):
        r0 = c0 - rowoff
        simple = (mul is None and add is None and not square)
        if simple:
            if out_bf:
                o = cx.oi % cx.NS
                cx.oi += 1
                ot, okey = cx.ost[o], ('ost', o)
            else:
                o = cx.si % cx.NS
                cx.si += 1
                ot, okey = cx.stg[o], ('stg', o)
            pg.op('act', lambda e: e.activation(out=ot[0:cw, 0:tw], in_=ps, func=func),
                  reads=[pskey], writes=[okey])
            pg.dma('sp', dst[r0:r0 + cw, t0:t0 + tw], ot[0:cw, 0:tw], reads=[okey],
                   writes=[(dkey, r0, t0)])
            return
        s = cx.si % cx.NS
        cx.si += 1
        st, skey = cx.stg[s], ('stg', s)
        pg.op('act', lambda e: e.activation(out=st[0:cw, 0:tw], in_=ps, func=func),
              reads=[pskey], writes=[skey])
        steps = []
        if mul is not None:
            steps.append(('mul', mul, mulkey))
        if add is not None:
            steps.append(('add', add, addkey))
        if square:
            steps.append(('sq', None, None))
        for i, (kind, src, skey2) in enumerate(steps):
            last = (i == len(steps) - 1)
            if last and out_bf:
                o = cx.oi % cx.NS
                cx.oi += 1
                ot, okey = cx.ost[o], ('ost', o)
            else:
                ot, okey = st, skey
            if kind == 'sq':
                pg.op('dve', lambda e, ot=ot: e.tensor_tensor(out=ot[0:cw, 0:tw], in0=st[0:cw, 0:tw],
                                                              in1=st[0:cw, 0:tw], op=ALU.mult),
                      reads=[skey], writes=[okey])
            else:
                a = cx.ai % cx.NS
                cx.ai += 1
                if src.dtype == BF16:
                    at, akey = cx.auxb[a], ('auxb', a)
                else:
                    at, akey = cx.aux[a], ('aux', a)
                pg.dma('sp', at[0:cw, 0:tw], src[r0:r0 + cw, t0:t0 + tw],
                       reads=[(skey2, r0, t0)], writes=[akey])
                opx = ALU.mult if kind == 'mul' else ALU.add
                pg.op('dve', lambda e, ot=ot, at=at, opx=opx: e.tensor_tensor(
                    out=ot[0:cw, 0:tw], in0=st[0:cw, 0:tw], in1=at[0:cw, 0:tw], op=opx),
                    reads=[skey, akey], writes=[okey])
        pg.dma('sp', dst[r0:r0 + cw, t0:t0 + tw], ot[0:cw, 0:tw], reads=[okey],
               writes=[(dkey, r0, t0)])
    return epi


class Ring:
    def __init__(self, pg, es, n, shape, dt, name):
        self.t = [pg.sb(es, shape, dt, '%s%d' % (name, i)) for i in range(n)]
        self.name = name
        self.n = n
        self.i = 0

    def next(self):
        j = self.i % self.n
        self.i += 1
        return self.t[j], (self.name, j)


class PRing:
    def __init__(self, pg, es, n, shape, dt, name):
        self.t = [pg.ps(es, shape, dt, '%s%d' % (name, i)) for i in range(n)]
        self.name = name
        self.n = n
        self.i = 0

    def next(self):
        j = self.i % self.n
        self.i += 1
        return self.t[j], (self.name, j)


def load_cols(pg, es, dram, n, name, q='sp', dt=F32):
    t = pg.sb(es, [128, n], dt, name)
    pg.dma(q, t[:], dram, writes=[(name,)])
    return t


BD = 2048
RW_COLS = 6592


def rwkv_prep(pg, L, NS, urwT, shiftT, cw, lw, scr):
    T = L + NS
    with ExitStack() as es:
        mu = load_cols(pg, es, cw['mu'], 52, 'c_mu')
        w0 = load_cols(pg, es, cw['w0'], 16, 'c_w0')
        a0 = load_cols(pg, es, cw['a0'], 16, 'c_a0')
        kkc = load_cols(pg, es, cw['kk'], 16, 'c_kk')
        kac = load_cols(pg, es, cw['ka'], 16, 'c_ka')
        rkc = load_cols(pg, es, cw['rk'], 16, 'c_rk')
        nw0 = pg.sb(es, [128, 16], F32, 'c_nw0')
        omka = pg.sb(es, [128, 16], F32, 'c_omka')
        pg.op('dve', lambda e: e.tensor_scalar(out=nw0[:], in0=w0[:], scalar1=-1.0, scalar2=None, op0=ALU.mult),
              reads=[('c_w0',)], writes=[('c_nw0',)])
        pg.op('dve', lambda e: e.tensor_scalar(out=omka[:], in0=kac[:], scalar1=-1.0, scalar2=1.0,
                                               op0=ALU.mult, op1=ALU.add),
              reads=[('c_ka',)], writes=[('c_omka',)])
        cst = pg.sb(es, [128, 4], F32, 'c_cst')
        pg.op('dve', lambda e: e.memset(cst[:, 0:1], 1.0), writes=[('c_cst',)])
        pg.op('dve', lambda e: e.memset(cst[:, 1:2], -0.5), reads=[('c_cst',)], writes=[('c_cst',)])
        bones = pg.sb(es, [128, 128], F32, 'bones')
        pg.op('dve', lambda e: e.memset(bones[:], 0.0), writes=[('bones',)])
        pg.op('dve', lambda e: e.memset(bones[0:64, 0:64], 1.0), reads=[('bones',)], writes=[('bones',)])
        pg.op('dve', lambda e: e.memset(bones[64:128, 64:128], 1.0), reads=[('bones',)], writes=[('bones',)])
        w2 = pg.sb(es, [96, 2048], BF16, 'l_w2')
        a2 = pg.sb(es, [96, 2048], BF16, 'l_a2')
        g2 = pg.sb(es, [128, 2, 2048], BF16, 'l_g2')
        pg.dma('pool', w2[:], lw['w2'], writes=[('l_w2',)])
        pg.dma('pool', a2[:], lw['a2'], writes=[('l_a2',)])
        pg.dma('pool', g2[:], lw['g2'].rearrange("(c p) n -> p c n", p=128), writes=[('l_g2',)])

        inr = Ring(pg, es, 6, [128, 513], F32, 'rin')
        tmp = Ring(pg, es, 10, [128, 512], F32, 'rtm')
        lor = Ring(pg, es, 2, [128, 4, 512], BF16, 'rlo')
        pp = PRing(pg, es, 6, [128, 512], F32, 'rpp')

        def load_shift(row0, nr, t0, tw, mucol):
            it, ik = inr.next()
            if t0 >= L:
                pg.dma('sp', it[0:nr, 1:tw + 1], urwT[row0:row0 + nr, t0:t0 + tw], writes=[ik])
                pt, pk = inr.next()
                pg.dma('sp', pt[0:nr, 0:tw], shiftT[row0:row0 + nr, 0:tw], writes=[pk])
                prev, pkeys = pt[0:nr, 0:tw], [pk]
            else:
                if t0 == 0:
                    pg.op('pool', lambda e: e.memset(it[0:nr, 0:1], 0.0), writes=[ik])
                    pg.dma('sp', it[0:nr, 1:tw + 1], urwT[row0:row0 + nr, 0:tw], reads=[ik], writes=[ik])
                else:
                    pg.dma('sp', it[0:nr, 0:tw + 1], urwT[row0:row0 + nr, t0 - 1:t0 + tw], writes=[ik])
                prev, pkeys = it[0:nr, 0:tw], []
            u = it[0:nr, 1:tw + 1]
            dt_, dk = tmp.next()
            pg.op('dve', lambda e: e.tensor_tensor(out=dt_[0:nr, 0:tw], in0=prev, in1=u, op=ALU.subtract),
                  reads=[ik] + pkeys, writes=[dk])
            pg.op('dve', lambda e: e.scalar_tensor_tensor(out=dt_[0:nr, 0:tw], in0=dt_[0:nr, 0:tw], scalar=mucol,
                                                          in1=u, op0=ALU.mult, op1=ALU.add),
                  reads=[ik, dk, ('c_mu',)], writes=[dk])
            return dt_, dk

        def store(name, hp, t0, tw, tl, tk):
            pg.dma('sp', scr[name][hp * 128:(hp + 1) * 128, t0:t0 + tw], tl[:, 0:tw], reads=[tk],
                   writes=[(name, hp, t0)])

        tiles = token_tiles(L) + ([(L, NS)] if NS else [])
        for (t0, tw) in tiles:
            lt, lk = lor.next()
            x, xk_ = load_shift(6144, 96, t0, tw, mu[0:96, 48:49])
            pg.op('act', lambda e: e.activation(out=lt[0:96, 0, 0:tw], in_=x[0:96, 0:tw], func=AF.Tanh),
                  reads=[xk_], writes=[lk])
            x, xk_ = load_shift(6240, 96, t0, tw, mu[0:96, 49:50])
            pg.op('act', lambda e: e.activation(out=lt[0:96, 1, 0:tw], in_=x[0:96, 0:tw], func=AF.Identity),
                  reads=[xk_], writes=[lk])
            for j in range(2):
                x, xk_ = load_shift(6336 + j * 128, 128, t0, tw, mu[:, 50 + j:51 + j])
                pg.op('act', lambda e, j=j: e.activation(out=lt[:, 2 + j, 0:tw], in_=x[:, 0:tw], func=AF.Sigmoid),
                      reads=[xk_], writes=[lk])
            for hp in range(16):
                xr, xrk = load_shift(hp * 128, 128, t0, tw, mu[:, hp:hp + 1])
                xk, xkk = load_shift(2048 + hp * 128, 128, t0, tw, mu[:, 16 + hp:17 + hp])
                xv, xvk = load_shift(4096 + hp * 128, 128, t0, tw, mu[:, 32 + hp:33 + hp])
                store('r', hp, t0, tw, xr, xrk)
                store('v', hp, t0, tw, xv, xvk)
                cs = slice(hp * 128, (hp + 1) * 128)
                p1, p1k = pp.next()
                pg.op('pe', lambda e: e.matmul(p1[:, 0:tw], w2[:, cs], lt[0:96, 0, 0:tw], start=True, stop=True),
                      reads=[('l_w2',), lk], writes=[p1k])
                e1, e1k = tmp.next()
                pg.op('act', lambda e: e.activation(out=e1[:, 0:tw], in_=p1[:, 0:tw], func=AF.Exp,
                                                    bias=nw0[:, hp:hp + 1], scale=-1.0),
                      reads=[p1k, ('c_nw0',)], writes=[e1k])
                pg.op('act', lambda e: e.activation(out=e1[:, 0:tw], in_=e1[:, 0:tw], func=AF.Ln,
                                                    bias=cst[:, 0:1], scale=1.0),
                      reads=[e1k, ('c_cst',)], writes=[e1k])
                pg.op('act', lambda e: e.activation(out=e1[:, 0:tw], in_=e1[:, 0:tw], func=AF.Exp,
                                                    bias=cst[:, 1:2], scale=-1.0),
                      reads=[e1k, ('c_cst',)], writes=[e1k])
                if 'lw' in scr:
                    pg.op('dve', lambda e: e.tensor_scalar(out=e1[:, 0:tw], in0=e1[:, 0:tw], scalar1=-1.0, scalar2=None,
                                                           op0=ALU.mult), reads=[e1k], writes=[e1k])
                    store('lw', hp, t0, tw, e1, e1k)
                else:
                    pg.op('act', lambda e: e.activation(out=e1[:, 0:tw], in_=e1[:, 0:tw], func=AF.Exp, scale=-1.0),
                          reads=[e1k], writes=[e1k])
                    store('dec', hp, t0, tw, e1, e1k)
                p2, p2k = pp.next()
                pg.op('pe', lambda e: e.matmul(p2[:, 0:tw], a2[:, cs], lt[0:96, 1, 0:tw], start=True, stop=True),
                      reads=[('l_a2',), lk], writes=[p2k])
                lr, lrk = tmp.next()
                pg.op('act', lambda e: e.activation(out=lr[:, 0:tw], in_=p2[:, 0:tw], func=AF.Sigmoid,
                                                    bias=a0[:, hp:hp + 1], scale=1.0),
                      reads=[p2k, ('c_a0',)], writes=[lrk])
                p3, p3k = pp.next()
                for j in range(2):
                    pg.op('pe', lambda e, j=j: e.matmul(p3[:, 0:tw], g2[:, j, cs], lt[:, 2 + j, 0:tw],
                                                        start=(j == 0), stop=(j == 1)),
                          reads=[('l_g2',), lk], writes=[p3k])
                gt, gk = tmp.next()
                pg.op('act', lambda e: e.activation(out=gt[:, 0:tw], in_=p3[:, 0:tw], func=AF.Identity),
                      reads=[p3k], writes=[gk])
                store('g', hp, t0, tw, gt, gk)
                kk, kkk = tmp.next()
                pg.op('dve', lambda e: e.tensor_scalar(out=kk[:, 0:tw], in0=xk[:, 0:tw], scalar1=kkc[:, hp:hp + 1],
                                                       scalar2=None, op0=ALU.mult),
                      reads=[xkk, ('c_kk',)], writes=[kkk])
                sq, sqk = tmp.next()
                pg.op('pool', lambda e: e.tensor_tensor(out=sq[:, 0:tw], in0=kk[:, 0:tw], in1=kk[:, 0:tw], op=ALU.mult),
                      reads=[kkk], writes=[sqk])
                p4, p4k = pp.next()
                pg.op('pe', lambda e: e.matmul(p4[:, 0:tw], bones[:], sq[:, 0:tw], start=True, stop=True),
                      reads=[('bones',), sqk], writes=[p4k])
                pg.op('act', lambda e: e.activation(out=sq[:, 0:tw], in_=p4[:, 0:tw], func=AF.Sqrt),
                      reads=[p4k], writes=[sqk])
                pg.op('dve', lambda e: e.tensor_scalar(out=sq[:, 0:tw], in0=sq[:, 0:tw], scalar1=1e-12, scalar2=None,
                                                       op0=ALU.max),
                      reads=[sqk], writes=[sqk])
                pg.op('dve', lambda e: e.reciprocal(out=sq[:, 0:tw], in_=sq[:, 0:tw]), reads=[sqk], writes=[sqk])
                pg.op('dve', lambda e: e.tensor_tensor(out=kk[:, 0:tw], in0=kk[:, 0:tw], in1=sq[:, 0:tw], op=ALU.mult),
                      reads=[kkk, sqk], writes=[kkk])
                store('an', hp, t0, tw, kk, kkk)
                pg.op('dve', lambda e: e.scalar_tensor_tensor(out=sq[:, 0:tw], in0=kk[:, 0:tw], scalar=-1.0,
                                                              in1=lr[:, 0:tw], op0=ALU.mult, op1=ALU.mult),
                      reads=[kkk, lrk], writes=[sqk])
                store('bn', hp, t0, tw, sq, sqk)
                pg.op('dve', lambda e: e.tensor_scalar(out=lr[:, 0:tw], in0=lr[:, 0:tw], scalar1=kac[:, hp:hp + 1],
                                                       scalar2=omka[:, hp:hp + 1], op0=ALU.mult, op1=ALU.add),
                      reads=[lrk, ('c_ka',), ('c_omka',)], writes=[lrk])
                pg.op('dve', lambda e: e.tensor_tensor(out=lr[:, 0:tw], in0=lr[:, 0:tw], in1=xk[:, 0:tw], op=ALU.mult),
                      reads=[lrk, xkk], writes=[lrk])
                store('kh', hp, t0, tw, lr, lrk)
                rr_, rrk = tmp.next()
                pg.op('dve', lambda e: e.scalar_tensor_tensor(out=rr_[:, 0:tw], in0=xr[:, 0:tw], scalar=rkc[:, hp:hp + 1],
                                                              in1=lr[:, 0:tw], op0=ALU.mult, op1=ALU.mult),
                      reads=[xrk, lrk, ('c_rk',)], writes=[rrk])
                p5, p5k = pp.next()
                pg.op('pe', lambda e: e.matmul(p5[:, 0:tw], bones[:], rr_[:, 0:tw], start=True, stop=True),
                      reads=[('bones',), rrk], writes=[p5k])
                pg.op('dve', lambda e: e.tensor_tensor(out=rr_[:, 0:tw], in0=p5[:, 0:tw], in1=xv[:, 0:tw], op=ALU.mult),
                      reads=[p5k, xvk], writes=[rrk])
                store('bonus', hp, t0, tw, rr_, rrk)
        pg.barrier()


def rwkv_scan(pg, L, NS, scr, cst, st_in, st_out_p, st_out_s, oT):
    T = L + NS
    names = ['dec', 'an', 'bn', 'kh', 'r', 'v']
    CH = 128
    with ExitStack() as es:
        bones = pg.sb(es, [128, 128], BF16, 'sc_bones')
        istack = pg.sb(es, [128, 64], BF16, 'sc_ist')
        sel2 = pg.sb(es, [128, 2], BF16, 'sc_sel2')
        pg.dma('sp', bones[:], cst['bones_bf'], writes=[('sc_bones',)])
        pg.dma('sp', istack[:], cst['istack'], writes=[('sc_ist',)])
        pg.dma('sp', sel2[:], cst['sel2'], writes=[('sc_sel2',)])
        qr = {n: Ring(pg, es, 2, [128, 16, CH], F32, 'sq_' + n) for n in names}
        STb = [pg.sb(es, [128, 16, 64], F32, 'ST%d' % i) for i in range(2)]
        tmpr = Ring(pg, es, 2, [128, 16, 64], BF16, 'sc_tmp')
        tmp2r = Ring(pg, es, 2, [128, 16, 64], BF16, 'sc_tmp2')
        t1r = Ring(pg, es, 2, [128, 16, 64], F32, 'sc_t1')
        t2r = Ring(pg, es, 2, [128, 16, 64], F32, 'sc_t2')
        dvr = Ring(pg, es, 4, [128, 16, 64], BF16, 'sc_dv')
        osb = Ring(pg, es, 2, [64, 32, CH], F32, 'sc_osb')
        ps_sa = PRing(pg, es, 1, [128, 16, 64], F32, 'ps_sa')
        ps_v = PRing(pg, es, 2, [128, 16, 64], F32, 'ps_v')
        ps_o = PRing(pg, es, 2, [64, 32, 16], F32, 'ps_o')

        def bc(t, tt):
            return t[:, :, tt:tt + 1].to_broadcast([128, 16, 64])

        def step(ST, stk, q, qk, tt, po, pok, oi):
            dv, dvk = dvr.next()
            pg.op('pool', lambda e: e.tensor_tensor(out=dv[:], in0=istack[:].unsqueeze(1).to_broadcast([128, 16, 64]),
                                                    in1=bc(q['v'], tt), op=ALU.mult),
                  reads=[('sc_ist',), qk['v']], writes=[dvk])
            pv, pvk = ps_v.next()
            for h in range(2):
                pg.op('pe', lambda e, h=h: e.matmul(pv[:, h * 8:(h + 1) * 8, :], bones[:], dv[:, h * 8:(h + 1) * 8, :],
                                                    start=True, stop=True),
                      reads=[('sc_bones',), dvk], writes=[pvk])
            tm, tmk = tmpr.next()
            pg.op('dve', lambda e: e.tensor_tensor(out=tm[:], in0=ST[:], in1=bc(q['an'], tt), op=ALU.mult),
                  reads=[stk, qk['an']], writes=[tmk])
            psa, psak = ps_sa.next()
            for h in range(2):
                pg.op('pe', lambda e, h=h: e.matmul(psa[:, h * 8:(h + 1) * 8, :], bones[:], tm[:, h * 8:(h + 1) * 8, :],
                                                    start=True, stop=True),
                      reads=[('sc_bones',), tmk], writes=[psak])
            pg.op('dve', lambda e: e.tensor_tensor(out=ST[:], in0=ST[:], in1=bc(q['dec'], tt), op=ALU.mult),
                  reads=[stk, qk['dec']], writes=[stk])
            t2, t2k = t2r.next()
            pg.op('dve', lambda e: e.tensor_tensor(out=t2[:], in0=pv[:], in1=bc(q['kh'], tt), op=ALU.mult),
                  reads=[pvk, qk['kh']], writes=[t2k])
            pg.op('dve', lambda e: e.tensor_tensor(out=ST[:], in0=ST[:], in1=t2[:], op=ALU.add),
                  reads=[stk, t2k], writes=[stk])
            t1, t1k = t1r.next()
            pg.op('dve', lambda e: e.tensor_tensor(out=t1[:], in0=psa[:], in1=bc(q['bn'], tt), op=ALU.mult),
                  reads=[psak, qk['bn']], writes=[t1k])
            pg.op('dve', lambda e: e.tensor_tensor(out=ST[:], in0=ST[:], in1=t1[:], op=ALU.add),
                  reads=[stk, t1k], writes=[stk])
            tm2, tm2k = tmp2r.next()
            pg.op('dve', lambda e: e.tensor_tensor(out=tm2[:], in0=ST[:], in1=bc(q['r'], tt), op=ALU.mult),
                  reads=[stk, qk['r']], writes=[tm2k])
            for hp in range(16):
                pg.op('pe', lambda e, hp=hp: e.matmul(po[:, 2 * hp:2 * hp + 2, oi], tm2[:, hp, :], sel2[:],
                                                      start=True, stop=True),
                      reads=[('sc_sel2',), tm2k], writes=[pok])

        def run_tokens(c0, cn, stfn):
            q, qk = {}, {}
            for n in names:
                q[n], qk[n] = qr[n].next()
                pg.dma('sp', q[n][:, :, 0:cn], scr[n].rearrange("(hp p) t -> p hp t", p=128)[:, :, c0:c0 + cn],
                       writes=[qk[n]])
            ob, obk = osb.next()
            for s0 in range(0, cn, 16):
                sn = min(16, cn - s0)
                po, pok = ps_o.next()
                for i in range(sn):
                    ST, stk, after = stfn(c0 + s0 + i)
                    step(ST, stk, q, qk, s0 + i, po, pok, i)
                    if after:
                        after()
                pg.op('act', lambda e: e.activation(out=ob[:, :, s0:s0 + sn], in_=po[:, :, 0:sn], func=AF.Identity),
                      reads=[pok], writes=[obk])
            pg.dma('sp', oT.rearrange("(h v) t -> v h t", v=64)[:, :, c0:c0 + cn], ob[:, :, 0:cn], reads=[obk],
                   writes=[('oT', c0)])

        if L:
            pg.op('dve', lambda e: e.memset(STb[0][:], 0.0), writes=[('ST', 0)])
            for c0 in range(0, L, CH):
                run_tokens(c0, min(CH, L - c0), lambda t: (STb[0], ('ST', 0), None))
            pg.dma('sp', st_out_p.rearrange("p (a b) -> p a b", a=16), STb[0][:], reads=[('ST', 0)],
                   writes=[('st_out_p',)])
        if NS:
            def stfn(t):
                i = t - L
                b = i % 2
                pg.dma('sp', STb[b][:], st_in[i].rearrange("p (a b) -> p a b", a=16), writes=[('ST', b)])

                def after():
                    pg.dma('sp', st_out_s[i].rearrange("p (a b) -> p a b", a=16), STb[b][:], reads=[('ST', b)],
                           writes=[('st_out_s', i)])
                return STb[b], ('ST', b), after
            run_tokens(L, NS, stfn)
        pg.barrier()


def rwkv_post(pg, L, NS, oT, scr, cw, yT):
    T = L + NS
    with ExitStack() as es:
        lnw = load_cols(pg, es, cw['lnw'], 16, 'c_lnw')
        lnb = load_cols(pg, es, cw['lnb'], 16, 'c_lnb')
        bones = pg.sb(es, [128, 128], F32, 'po_bones')
        pg.op('dve', lambda e: e.memset(bones[:], 0.0), writes=[('po_bones',)])
        pg.op('dve', lambda e: e.memset(bones[0:64, 0:64], 1.0 / 64), reads=[('po_bones',)], writes=[('po_bones',)])
        pg.op('dve', lambda e: e.memset(bones[64:128, 64:128], 1.0 / 64), reads=[('po_bones',)], writes=[('po_bones',)])
        epsc = pg.sb(es, [128, 1], F32, 'po_eps')
        pg.op('dve', lambda e: e.memset(epsc[:], 64e-5), writes=[('po_eps',)])
        inr = Ring(pg, es, 6, [128, 512], F32, 'po_in')
        tmp = Ring(pg, es, 4, [128, 512], F32, 'po_tm')
        outr = Ring(pg, es, 3, [128, 512], BF16, 'po_out')
        pp = PRing(pg, es, 4, [128, 512], F32, 'po_pp')
        for (t0, tw) in token_tiles(T):
            for hp in range(16):
                rs = slice(hp * 128, (hp + 1) * 128)
                o, ok = inr.next()
                pg.dma('sp', o[:, 0:tw], oT[rs, t0:t0 + tw], writes=[ok])
                g, gk = inr.next()
                pg.dma('sp', g[:, 0:tw], scr['g'][rs, t0:t0 + tw], writes=[gk])
                bo, bok = inr.next()
                pg.dma('sp', bo[:, 0:tw], scr['bonus'][rs, t0:t0 + tw], writes=[bok])
                p1, p1k = pp.next()
                pg.op('pe', lambda e: e.matmul(p1[:, 0:tw], bones[:], o[:, 0:tw], start=True, stop=True),
                      reads=[('po_bones',), ok], writes=[p1k])
                c, ck = tmp.next()
                pg.op('dve', lambda e: e.tensor_tensor(out=c[:, 0:tw], in0=o[:, 0:tw], in1=p1[:, 0:tw], op=ALU.subtract),
                      reads=[ok, p1k], writes=[ck])
                sq, sqk = tmp.next()
                pg.op('act', lambda e: e.activation(out=sq[:, 0:tw], in_=c[:, 0:tw], func=AF.Square),
                      reads=[ck], writes=[sqk])
                p2, p2k = pp.next()
                pg.op('pe', lambda e: e.matmul(p2[:, 0:tw], bones[:], sq[:, 0:tw], start=True, stop=True),
                      reads=[('po_bones',), sqk], writes=[p2k])
                pg.op('act', lambda e: e.activation(out=sq[:, 0:tw], in_=p2[:, 0:tw], func=AF.Sqrt, bias=epsc[:, 0:1],
                                                    scale=1.0),
                      reads=[p2k, ('po_eps',)], writes=[sqk])
                pg.op('dve', lambda e: e.reciprocal(out=sq[:, 0:tw], in_=sq[:, 0:tw]), reads=[sqk], writes=[sqk])
                pg.op('dve', lambda e: e.scalar_tensor_tensor(out=c[:, 0:tw], in0=c[:, 0:tw], scalar=lnw[:, hp:hp + 1],
                                                              in1=sq[:, 0:tw], op0=ALU.mult, op1=ALU.mult),
                      reads=[ck, sqk, ('c_lnw',)], writes=[ck])
                pg.op('dve', lambda e: e.scalar_tensor_tensor(out=c[:, 0:tw], in0=c[:, 0:tw], scalar=lnb[:, hp:hp + 1],
                                                              in1=bo[:, 0:tw], op0=ALU.add, op1=ALU.add),
                      reads=[ck, bok, ('c_lnb',)], writes=[ck])
                y, yk = outr.next()
                pg.op('dve', lambda e: e.tensor_tensor(out=y[:, 0:tw], in0=c[:, 0:tw], in1=g[:, 0:tw], op=ALU.mult),
                      reads=[ck, gk], writes=[yk])
                pg.dma('sp', yT[rs, t0:t0 + tw], y[:, 0:tw], reads=[yk], writes=[('yT', hp, t0)])
        pg.barrier()


def ssd_conv(pg, L, NS, ussmT, convT, cw, xbcT, conv_out_p, conv_out_s):
    T = L + NS
    with ExitStack() as es:
        cwt = pg.sb(es, [128, 24, 4], F32, 'cv_w')
        cbt = pg.sb(es, [128, 24], F32, 'cv_b')
        pg.dma('sp', cwt[:], cw['convw'], writes=[('cv_w',)])
        pg.dma('sp', cbt[:], cw['convb'], writes=[('cv_b',)])
        inr = Ring(pg, es, 3, [128, 516], F32, 'cv_in')
        acc = Ring(pg, es, 3, [128, 512], F32, 'cv_acc')
        smp = Ring(pg, es, 2, [128, max(NS, 1), 4], F32, 'cv_smp')
        if L:
            pg.dma('sp', conv_out_p, ussmT[2048:5120, L - 3:L], writes=[('conv_out_p',)])
        if NS:
            with pg.nc.allow_non_contiguous_dma(reason="tiny conv state shuffles"):
                pg.dma('sp', conv_out_s[:, :, 0:2], convT[:, :, 1:3], writes=[('conv_out_s', 0)])
                pg.dma('sp', conv_out_s[:, :, 2:3], ussmT[2048:5120, L:L + NS].unsqueeze(2), writes=[('conv_out_s', 1)])
        for blk in range(24):
            r0 = 2048 + blk * 128
            for (t0, tw) in token_tiles(L):
                it, ik = inr.next()
                if t0 == 0:
                    pg.op('pool', lambda e: e.memset(it[:, 0:3], 0.0), writes=[ik])
                    pg.dma('sp', it[:, 3:tw + 3], ussmT[r0:r0 + 128, 0:tw], reads=[ik], writes=[ik])
                else:
                    pg.dma('sp', it[:, 0:tw + 3], ussmT[r0:r0 + 128, t0 - 3:t0 + tw], writes=[ik])
                a, ak = acc.next()
                pg.op('dve', lambda e: e.tensor_scalar(out=a[:, 0:tw], in0=it[:, 0:tw], scalar1=cwt[:, blk, 0:1],
                                                       scalar2=None, op0=ALU.mult),
                      reads=[ik, ('cv_w',)], writes=[ak])
                for j in range(1, 4):
                    pg.op('dve', lambda e, j=j: e.scalar_tensor_tensor(
                        out=a[:, 0:tw], in0=it[:, j:j + tw], scalar=cwt[:, blk, j:j + 1], in1=a[:, 0:tw],
                        op0=ALU.mult, op1=ALU.add), reads=[ik, ak, ('cv_w',)], writes=[ak])
                pg.op('act', lambda e: e.activation(out=a[:, 0:tw], in_=a[:, 0:tw], func=AF.Silu,
                                                    bias=cbt[:, blk:blk + 1], scale=1.0),
                      reads=[ak, ('cv_b',)], writes=[ak])
                pg.dma('sp', xbcT[blk * 128:(blk + 1) * 128, t0:t0 + tw], a[:, 0:tw], reads=[ak],
                       writes=[('xbcT', blk, t0)])
            if NS:
                st, sk = smp.next()
                with pg.nc.allow_non_contiguous_dma(reason="tiny conv state loads"):
                    pg.dma('sp', st[:, :, 0:3], convT[blk * 128:(blk + 1) * 128, :, :], writes=[sk])
                    pg.dma('sp', st[:, :, 3:4], ussmT[r0:r0 + 128, L:L + NS].unsqueeze(2), reads=[sk], writes=[sk])
                a, ak = acc.next()
                pg.op('dve', lambda e: e.tensor_scalar(out=a[:, 0:NS], in0=st[:, :, 0], scalar1=cwt[:, blk, 0:1],
                                                       scalar2=None, op0=ALU.mult),
                      reads=[sk, ('cv_w',)], writes=[ak])
                for j in range(1, 4):
                    pg.op('dve', lambda e, j=j: e.scalar_tensor_tensor(
                        out=a[:, 0:NS], in0=st[:, :, j], scalar=cwt[:, blk, j:j + 1], in1=a[:, 0:NS],
                        op0=ALU.mult, op1=ALU.add), reads=[sk, ak, ('cv_w',)], writes=[ak])
                pg.op('act', lambda e: e.activation(out=a[:, 0:NS], in_=a[:, 0:NS], func=AF.Silu,
                                                    bias=cbt[:, blk:blk + 1], scale=1.0),
                      reads=[ak, ('cv_b',)], writes=[ak])
                pg.dma('sp', xbcT[blk * 128:(blk + 1) * 128, L:L + NS], a[:, 0:NS], reads=[ak],
                       writes=[('xbcT', blk, L)])
        pg.barrier()


def ssd_main(pg, L, NS, ussmT, xbcT, cw, cst, st_in, st_out_p, st_out_s, yrawT):
    T = L + NS
    CK = 64
    with ExitStack() as es:
        ident = pg.sb(es, [128, 128], F32, 'sd_id')
        selh = pg.sb(es, [32, 32, 64], F32, 'sd_selh')
        negm = pg.sb(es, [64, 64], F32, 'sd_negm')
        sell = pg.sb(es, [64, 128], F32, 'sd_sell')
        pg.dma('sp', ident[:], cst['ident'], writes=[('sd_id',)])
        pg.dma('sp', selh[:], cst['selh'], writes=[('sd_selh',)])
        pg.dma('sp', negm[:], cst['negmask'], writes=[('sd_negm',)])
        pg.dma('sp', sell[:], cst['sellast'], writes=[('sd_sell',)])
        dtb = pg.sb(es, [32, 1], F32, 'sd_dtb')
        alog = pg.sb(es, [32, 1], F32, 'sd_alog')
        pg.dma('sp', dtb[:], cw['dtb'], writes=[('sd_dtb',)])
        pg.dma('sp', alog[:], cw['alog'], writes=[('sd_alog',)])
        one = pg.sb(es, [128, 1], F32, 'sd_one')
        pg.op('dve', lambda e: e.memset(one[:], 1.0), writes=[('sd_one',)])
        onerow = pg.sb(es, [1, 128], F32, 'sd_onerow')
        pg.op('dve', lambda e: e.memset(onerow[:], 1.0), writes=[('sd_sell',)])
        dtf = pg.sb(es, [32, T], F32, 'sd_dt')
        cumA = pg.sb(es, [32, T], F32, 'sd_cumA')
        cumB = pg.sb(es, [32, T], F32, 'sd_cumB')
        dtdte = pg.sb(es, [32, T], F32, 'sd_dtdte')
        ecum = pg.sb(es, [32, T], F32, 'sd_ecum')
        ncum = pg.sb(es, [32, T], F32, 'sd_ncum')
        K = ('sd_small',)
        pg.dma('sp', dtf[:], ussmT[5120:5152, :], writes=[K])
        pg.op('act', lambda e: e.activation(out=dtf[:], in_=dtf[:], func=AF.Exp, bias=dtb[:, 0:1], scale=1.0),
              reads=[K, ('sd_dtb',)], writes=[K])
        pg.op('act', lambda e: e.activation(out=dtf[:], in_=dtf[:], func=AF.Ln, bias=one[0:32, 0:1], scale=1.0),
              reads=[K, ('sd_one',)], writes=[K])
        pg.op('act', lambda e: e.activation(out=alog[:], in_=alog[:], func=AF.Exp), reads=[('sd_alog',)],
              writes=[('sd_alog',)])
        pg.op('dve', lambda e: e.tensor_scalar(out=cumA[:], in0=dtf[:], scalar1=alog[:, 0:1], scalar2=-1.0,
                                               op0=ALU.mult, op1=ALU.mult),
              reads=[K, ('sd_alog',)], writes=[K])
        src, dst = cumA, cumB
        nch = L // CK
        if L:
            sh = 1
            while sh < CK:
                sv = src[:, 0:L].rearrange("p (c s) -> p c s", s=CK)
                dv = dst[:, 0:L].rearrange("p (c s) -> p c s", s=CK)
                pg.op('dve', lambda e, sv=sv, dv=dv, sh=sh: e.tensor_copy(out=dv[:, :, 0:sh], in_=sv[:, :, 0:sh]),
                      reads=[K], writes=[K])
                pg.op('dve', lambda e, sv=sv, dv=dv, sh=sh: e.tensor_tensor(out=dv[:, :, sh:CK], in0=sv[:, :, sh:CK],
                                                                          in1=sv[:, :, 0:CK - sh], op=ALU.add),
                      reads=[K], writes=[K])
                if NS:
                    pg.op('dve', lambda e, src=src, dst=dst: e.tensor_copy(out=dst[:, L:T], in_=src[:, L:T]),
                          reads=[K], writes=[K])
                src, dst = dst, src
                sh *= 2
        cum = src
        pg.op('act', lambda e: e.activation(out=ecum[:], in_=cum[:], func=AF.Exp), reads=[K], writes=[K])
        pg.op('dve', lambda e: e.tensor_scalar(out=ncum[:], in0=cum[:], scalar1=-1.0, scalar2=None, op0=ALU.mult),
              reads=[K], writes=[K])
        for c in range(nch):
            cs = slice(c * CK, (c + 1) * CK)
            pg.op('act', lambda e, cs=cs, c=c: e.activation(out=dtdte[:, cs], in_=cum[:, cs], func=AF.Exp,
                                                          bias=cum[:, (c + 1) * CK - 1:(c + 1) * CK], scale=-1.0),
                  reads=[K], writes=[K])
        if NS:
            pg.op('dve', lambda e: e.memset(dtdte[:, L:T], 1.0), reads=[K], writes=[K])
        pg.op('dve', lambda e: e.tensor_tensor(out=dtdte[:], in0=dtdte[:], in1=dtf[:], op=ALU.mult),
              reads=[K], writes=[K])

        BIG = 256
        xf_r = Ring(pg, es, 2, [128, 16, BIG], F32, 'sd_xf')
        bc_r = Ring(pg, es, 2, [128, 8, BIG], F32, 'sd_bcf')
        bcb_r = Ring(pg, es, 2, [128, 8, BIG], BF16, 'sd_bcb')
        yfm_r = Ring(pg, es, 2, [128, 16, BIG], F32, 'sd_yfm')
        pb = PRing(pg, es, 8, [128, 512], F32, 'sd_pb')
        smT = Ring(pg, es, 2, [64, 4, 32], F32, 'sd_smT')
        BT_r = Ring(pg, es, 2, [64, 512], BF16, 'sd_BT')
        xdt_r = Ring(pg, es, 2, [64, 8, 64], BF16, 'sd_xdt')
        xdte_r = Ring(pg, es, 2, [64, 8, 64], BF16, 'sd_xdte')
        L_r = Ring(pg, es, 2, [64, 8, 64], F32, 'sd_L')
        M_r = Ring(pg, es, 2, [64, 8, 64], BF16, 'sd_M')
        yo_r = Ring(pg, es, 2, [64, 8, 64], F32, 'sd_yo')
        yt_r = Ring(pg, es, 2, [64, 512], F32, 'sd_yt')
        cd_r = Ring(pg, es, 2, [128, 32], F32, 'sd_cd')
        cb_r = Ring(pg, es, 2, [64, 256], F32, 'sd_cb')
        hT = [pg.sb(es, [128, 32, 64], F32, 'sd_h%d' % i) for i in range(2)]
        hbf = [pg.sb(es, [128, 32, 64], BF16, 'sd_hb%d' % i) for i in range(2)]

        def chunk(c0, C, off, xf, xfk, bcb, bcbk, bcf, bcfk, yfm, yfmk, hb):
            H, Hk, HB, HBk = hT[hb], ('sd_h', hb), hbf[hb], ('sd_hb', hb)
            sm, smk = smT.next()
            pt, ptk = pb.next()
            for j, arr in enumerate((dtf, dtdte, ncum, ecum)):
                pg.op('pe', lambda e, j=j, arr=arr: e.transpose(pt[0:C, j * 32:(j + 1) * 32], arr[:, c0:c0 + C],
                                                              ident[0:32, 0:32]),
                      reads=[K, ('sd_id',)], writes=[ptk])
            pg.op('act', lambda e: e.activation(out=sm[0:C, :, :], in_=pt[0:C, 0:128].rearrange("p (a b) -> p a b", a=4),
                                                func=AF.Identity), reads=[ptk], writes=[smk])
            pB, pBk = pb.next()
            for g in range(4):
                pg.op('pe', lambda e, g=g: e.transpose(pB[0:C, g * 128:(g + 1) * 128], bcf[:, g, off:off + C], ident[:]),
                      reads=[bcfk, ('sd_id',)], writes=[pBk])
            BT, BTk = BT_r.next()
            pg.op('act', lambda e: e.activation(out=BT[0:C, :], in_=pB[0:C, :], func=AF.Identity),
                  reads=[pBk], writes=[BTk])
            pcb, pcbk = pb.next()
            for g in range(4):
                pg.op('pe', lambda e, g=g: e.matmul(pcb[0:C, g * 64:g * 64 + C], bcb[:, g, off:off + C],
                                                    bcb[:, 4 + g, off:off + C], start=True, stop=True),
                      reads=[bcbk], writes=[pcbk])
            cbs, cbsk = cb_r.next()
            pg.op('act', lambda e: e.activation(out=cbs[0:C, :], in_=pcb[0:C, 0:256], func=AF.Identity),
                  reads=[pcbk], writes=[cbsk])
            pcd, pcdk = pb.next()
            sl = sell[0:64, :] if C == 64 else onerow[0:1, :]
            pg.op('pe', lambda e: e.matmul(pcd[:, 0:32], sl, sm[0:C, 3, :], start=True, stop=True),
                  reads=[('sd_sell',), smk], writes=[pcdk])
            cd, cdk = cd_r.next()
            pg.op('act', lambda e: e.activation(out=cd[:], in_=pcd[:, 0:32], func=AF.Identity), reads=[pcdk], writes=[cdk])
            for g in range(4):
                hs = slice(g * 8, (g + 1) * 8)
                px, pxk = pb.next()
                for b4 in range(4):
                    blk = g * 4 + b4
                    pg.op('pe', lambda e, b4=b4, blk=blk: e.transpose(px[0:C, b4 * 128:(b4 + 1) * 128],
                                                                    xf[:, blk, off:off + C], ident[:]),
                          reads=[xfk, ('sd_id',)], writes=[pxk])
                xdt, xdtk = xdt_r.next()
                xdte, xdtek = xdte_r.next()
                pxv = px[0:C, :].rearrange("p (h q) -> p h q", h=8)
                pg.op('dve', lambda e: e.tensor_tensor(out=xdt[0:C], in0=pxv,
                                                       in1=sm[0:C, 0, hs].unsqueeze(2).to_broadcast([C, 8, 64]), op=ALU.mult),
                      reads=[pxk, smk], writes=[xdtk])
                pg.op('dve', lambda e: e.tensor_tensor(out=xdte[0:C], in0=pxv,
                                                       in1=sm[0:C, 1, hs].unsqueeze(2).to_broadcast([C, 8, 64]), op=ALU.mult),
                      reads=[pxk, smk], writes=[xdtek])
                pL, pLk = pb.next()
                Lt, Ltk = L_r.next()
                for hl in range(8):
                    h = g * 8 + hl
                    pg.op('pe', lambda e, h=h, hl=hl: e.matmul(pL[0:C, hl * 64:hl * 64 + C], selh[:, h, 0:C],
                                                               cum[:, c0:c0 + C], start=True, stop=False),
                          reads=[('sd_selh',), K], writes=[pLk])
                    pg.op('pe', lambda e, hl=hl: e.matmul(pL[0:C, hl * 64:hl * 64 + C], ident[0:C, 0:C], negm[0:C, 0:C],
                                                          start=False, stop=True),
                          reads=[('sd_id',), ('sd_negm',)], writes=[pLk])
                    pg.op('act', lambda e, h=h, hl=hl: e.activation(out=Lt[0:C, hl, 0:C], in_=pL[0:C, hl * 64:hl * 64 + C],
                                                                    func=AF.Exp, bias=sm[0:C, 2, h:h + 1], scale=1.0),
                          reads=[pLk, smk], writes=[Ltk])
                M, Mk = M_r.next()
                pg.op('dve', lambda e: e.tensor_tensor(
                    out=M[0:C, :, 0:C], in0=Lt[0:C, :, 0:C],
                    in1=cbs[0:C, g * 64:g * 64 + C].unsqueeze(1).to_broadcast([C, 8, C]), op=ALU.mult),
                    reads=[Ltk, cbsk], writes=[Mk])
                py, pyk = pb.next()
                for hl in range(8):
                    pg.op('pe', lambda e, hl=hl: e.matmul(py[0:C, hl * 64:(hl + 1) * 64], M[0:C, hl, 0:C], xdt[0:C, hl, :],
                                                          start=True, stop=True),
                          reads=[Mk, xdtk], writes=[pyk])
                po, pok = pb.next()
                pg.op('pe', lambda e: e.matmul(po[0:C, :], bcb[:, 4 + g, off:off + C],
                                               HB[:, hs, :].rearrange("p h q -> p (h q)"), start=True, stop=True),
                      reads=[bcbk, HBk], writes=[pok])
                yo, yok = yo_r.next()
                pg.op('dve', lambda e: e.tensor_tensor(out=yo[0:C], in0=po[0:C, :].rearrange("p (h q) -> p h q", h=8),
                                                       in1=sm[0:C, 3, hs].unsqueeze(2).to_broadcast([C, 8, 64]), op=ALU.mult),
                      reads=[pok, smk], writes=[yok])
                yt, ytk = yt_r.next()
                pg.op('dve', lambda e: e.tensor_tensor(out=yt[0:C, :], in0=py[0:C, :],
                                                       in1=yo[0:C].rearrange("p h q -> p (h q)"), op=ALU.add),
                      reads=[pyk, yok], writes=[ytk])
                pyT, pyTk = pb.next()
                for b4 in range(4):
                    pg.op('pe', lambda e, b4=b4: e.transpose(pyT[:, b4 * 64:b4 * 64 + C], yt[0:C, b4 * 128:(b4 + 1) * 128],
                                                            ident[0:C, 0:C]),
                          reads=[ytk, ('sd_id',)], writes=[pyTk])
                pg.op('act', lambda e: e.activation(
                    out=yfm[:, g * 4:(g + 1) * 4, off:off + C],
                    in_=pyT[:, 0:256].rearrange("p (a b) -> p a b", a=4)[:, :, 0:C], func=AF.Identity),
                    reads=[pyTk], writes=[yfmk])
                pcs, pcsk = pb.next()
                pg.op('pe', lambda e: e.matmul(pcs[:, :], BT[0:C, g * 128:(g + 1) * 128],
                                               xdte[0:C].rearrange("p h q -> p (h q)"), start=True, stop=True),
                      reads=[BTk, xdtek], writes=[pcsk])
                pg.op('dve', lambda e: e.tensor_tensor(out=H[:, hs, :], in0=H[:, hs, :],
                                                       in1=cd[:, hs].unsqueeze(2).to_broadcast([128, 8, 64]), op=ALU.mult),
                      reads=[Hk, cdk], writes=[Hk])
                pg.op('dve', lambda e: e.tensor_tensor(out=H[:, hs, :], in0=H[:, hs, :],
                                                       in1=pcs[:, :].rearrange("p (h q) -> p h q", h=8), op=ALU.add),
                      reads=[Hk, pcsk], writes=[Hk])
            pg.op('act', lambda e: e.activation(out=HB[:], in_=H[:], func=AF.Identity), reads=[Hk], writes=[HBk])

        def load_big(b0, bn):
            xf, xfk = xf_r.next()
            pg.dma('sp', xf[:, :, 0:bn], xbcT[0:2048, b0:b0 + bn].rearrange("(c p) t -> p c t", p=128), writes=[xfk])
            bcf, bcfk = bc_r.next()
            pg.dma('sp', bcf[:, :, 0:bn], xbcT[2048:3072, b0:b0 + bn].rearrange("(c p) t -> p c t", p=128), writes=[bcfk])
            bcb, bcbk = bcb_r.next()
            pg.op('pool', lambda e: e.tensor_copy(out=bcb[:, :, 0:bn], in_=bcf[:, :, 0:bn]), reads=[bcfk], writes=[bcbk])
            yfm, yfmk = yfm_r.next()
            return xf, xfk, bcb, bcbk, bcf, bcfk, yfm, yfmk

        def store_big(b0, bn, yfm, yfmk):
            pg.dma('sp', yrawT[:, b0:b0 + bn].rearrange("(c p) t -> p c t", p=128), yfm[:, :, 0:bn], reads=[yfmk],
                   writes=[('yrawT', b0)])

        if L:
            pg.op('dve', lambda e: e.memset(hT[0][:], 0.0), writes=[('sd_h', 0)])
            pg.op('pool', lambda e: e.memset(hbf[0][:], 0.0), writes=[('sd_hb', 0)])
            for b0 in range(0, L, BIG):
                bn = min(BIG, L - b0)
                bufs = load_big(b0, bn)
                for off in range(0, bn, CK):
                    chunk(b0 + off, CK, off, *bufs, 0)
                store_big(b0, bn, bufs[6], bufs[7])
            pg.dma('sp', st_out_p.rearrange("p (h q) -> p h q", h=32), hT[0][:], reads=[('sd_h', 0)], writes=[('ssm_out_p',)])
        if NS:
            bufs = load_big(L, NS)
            for i in range(NS):
                b = i % 2
                pg.dma('sp', hT[b][:], st_in[i].rearrange("p (h q) -> p h q", h=32), writes=[('sd_h', b)])
                pg.op('act', lambda e, b=b: e.activation(out=hbf[b][:], in_=hT[b][:], func=AF.Identity),
                      reads=[('sd_h', b)], writes=[('sd_hb', b)])
                chunk(L + i, 1, i, *bufs, b)
                pg.dma('sp', st_out_s[i].rearrange("p (h q) -> p h q", h=32), hT[b][:], reads=[('sd_h', b)],
                       writes=[('ssm_out_s', i)])
            store_big(L, NS, bufs[6], bufs[7])
        pg.barrier()


def ssd_post(pg, L, NS, yrawT, xbcT, ussmT, cw, yT):
    T = L + NS
    with ExitStack() as es:
        dcol = load_cols(pg, es, cw['dcol'], 16, 'sp_d')
        nw = load_cols(pg, es, cw['ssm_norm'], 16, 'sp_nw')
        ones = pg.sb(es, [128, 128], F32, 'sp_ones')
        pg.op('dve', lambda e: e.memset(ones[:], 1.0), writes=[('sp_ones',)])
        epsc = pg.sb(es, [128, 1], F32, 'sp_eps')
        pg.op('dve', lambda e: e.memset(epsc[:], 1e-6), writes=[('sp_eps',)])
        inr = Ring(pg, es, 6, [128, 512], F32, 'sp_in')
        yz_r = Ring(pg, es, 8, [128, 512], F32, 'sp_yz')
        tmp = Ring(pg, es, 4, [128, 512], F32, 'sp_tm')
        outr = Ring(pg, es, 3, [128, 512], BF16, 'sp_out')
        pp = PRing(pg, es, 2, [128, 512], F32, 'sp_pp')
        for (t0, tw) in token_tiles(T):
            for g in range(4):
                p, pk = pp.next()
                yzs = []
                for b4 in range(4):
                    blk = g * 4 + b4
                    rs = slice(blk * 128, (blk + 1) * 128)
                    yr, yrk = inr.next()
                    pg.dma('sp', yr[:, 0:tw], yrawT[rs, t0:t0 + tw], writes=[yrk])
                    xs, xsk = inr.next()
                    pg.dma('sp', xs[:, 0:tw], xbcT[rs, t0:t0 + tw], writes=[xsk])
                    z, zk = inr.next()
                    pg.dma('sp', z[:, 0:tw], ussmT[rs, t0:t0 + tw], writes=[zk])
                    pg.op('dve', lambda e, blk=blk, xs=xs, yr=yr: e.scalar_tensor_tensor(
                        out=yr[:, 0:tw], in0=xs[:, 0:tw], scalar=dcol[:, blk:blk + 1], in1=yr[:, 0:tw],
                        op0=ALU.mult, op1=ALU.add), reads=[xsk, yrk, ('sp_d',)], writes=[yrk])
                    pg.op('act', lambda e, z=z: e.activation(out=z[:, 0:tw], in_=z[:, 0:tw], func=AF.Silu),
                          reads=[zk], writes=[zk])
                    yz, yzk = yz_r.next()
                    pg.op('dve', lambda e, yz=yz, yr=yr, z=z: e.tensor_tensor(out=yz[:, 0:tw], in0=yr[:, 0:tw],
                                                                              in1=z[:, 0:tw], op=ALU.mult),
                          reads=[yrk, zk], writes=[yzk])
                    sq, sqk = tmp.next()
                    pg.op('act', lambda e, sq=sq, yz=yz: e.activation(out=sq[:, 0:tw], in_=yz[:, 0:tw], func=AF.Square),
                          reads=[yzk], writes=[sqk])
                    pg.op('pe', lambda e, sq=sq, b4=b4: e.matmul(p[:, 0:tw], ones[:], sq[:, 0:tw], start=(b4 == 0),
                                                                 stop=(b4 == 3)),
                          reads=[('sp_ones',), sqk], writes=[pk])
                    yzs.append((yz, yzk, blk))
                rs_, rsk = tmp.next()
                pg.op('act', lambda e: e.activation(out=rs_[:, 0:tw], in_=p[:, 0:tw], func=AF.Sqrt, bias=epsc[:, 0:1],
                                                    scale=1.0 / 512), reads=[pk, ('sp_eps',)], writes=[rsk])
                pg.op('dve', lambda e: e.reciprocal(out=rs_[:, 0:tw], in_=rs_[:, 0:tw]), reads=[rsk], writes=[rsk])
                for (yz, yzk, blk) in yzs:
                    o, ok = outr.next()
                    pg.op('dve', lambda e, yz=yz, blk=blk, o=o: e.scalar_tensor_tensor(
                        out=o[:, 0:tw], in0=yz[:, 0:tw], scalar=nw[:, blk:blk + 1], in1=rs_[:, 0:tw],
                        op0=ALU.mult, op1=ALU.mult), reads=[yzk, rsk, ('sp_nw',)], writes=[ok])
                    pg.dma('sp', yT[blk * 128:(blk + 1) * 128, t0:t0 + tw], o[:, 0:tw], reads=[ok],
                           writes=[('yssmT', blk, t0)])
        pg.barrier()


def gla_prep(pg, L, NS, uglaT, cw, lw, scr):
    T = L + NS
    CK = 64
    nch = L // CK
    with ExitStack() as es:
        fb = load_cols(pg, es, cw['gla_fb'], 8, 'gp_fb')
        nfb = pg.sb(es, [128, 8], F32, 'gp_nfb')
        pg.op('dve', lambda e: e.tensor_scalar(out=nfb[:], in0=fb[:], scalar1=-1.0, scalar2=None, op0=ALU.mult),
              reads=[('gp_fb',)], writes=[('gp_nfb',)])
        one = pg.sb(es, [128, 1], F32, 'gp_one')
        pg.op('dve', lambda e: e.memset(one[:], 1.0), writes=[('gp_one',)])
        fup = pg.sb(es, [16, 1024], BF16, 'gp_fup')
        pg.dma('pool', fup[:], lw['gla_fup'], writes=[('gp_fup',)])
        flr = Ring(pg, es, 2, [16, 512], BF16, 'gp_fl')
        inr = Ring(pg, es, 4, [128, 512], F32, 'gp_in')
        tmp = Ring(pg, es, 8, [128, 512], F32, 'gp_tm')
        outr = Ring(pg, es, 4, [128, 512], BF16, 'gp_out')
        cdr = Ring(pg, es, 2, [128, 16], F32, 'gp_cd')
        pp = PRing(pg, es, 2, [128, 512], F32, 'gp_pp')
        tiles = token_tiles(L) + ([(L, NS)] if NS else [])
        for (t0, tw) in tiles:
            samp = (t0 >= L)
            ck = 1 if samp else CK
            nc_ = tw // ck
            fl, flk = flr.next()
            pg.dma('pool', fl[:, 0:tw], uglaT[6144:6160, t0:t0 + tw], writes=[flk])
            for j in range(8):
                p, pk = pp.next()
                pg.op('pe', lambda e: e.matmul(p[:, 0:tw], fup[:, j * 128:(j + 1) * 128], fl[:, 0:tw], start=True, stop=True),
                      reads=[('gp_fup',), flk], writes=[pk])
                a, ak = tmp.next()
                pg.op('act', lambda e: e.activation(out=a[:, 0:tw], in_=p[:, 0:tw], func=AF.Exp, bias=nfb[:, j:j + 1],
                                                    scale=-1.0), reads=[pk, ('gp_nfb',)], writes=[ak])
                pg.op('act', lambda e: e.activation(out=a[:, 0:tw], in_=a[:, 0:tw], func=AF.Ln, bias=one[:, 0:1], scale=1.0),
                      reads=[ak, ('gp_one',)], writes=[ak])
                pg.op('dve', lambda e: e.tensor_scalar(out=a[:, 0:tw], in0=a[:, 0:tw], scalar1=-1.0 / 16, scalar2=None,
                                                       op0=ALU.mult), reads=[ak], writes=[ak])
                b, bk = a, ak
                if not samp:
                    b2, b2k = tmp.next()
                    src, srck, dst, dstk = a, ak, b2, b2k
                    sh = 1
                    while sh < CK:
                        sv = src[:, 0:tw].rearrange("p (c s) -> p c s", s=CK)
                        dv = dst[:, 0:tw].rearrange("p (c s) -> p c s", s=CK)
                        pg.op('dve', lambda e, sv=sv, dv=dv, sh=sh: e.tensor_copy(out=dv[:, :, 0:sh], in_=sv[:, :, 0:sh]),
                              reads=[srck], writes=[dstk])
                        pg.op('dve', lambda e, sv=sv, dv=dv, sh=sh: e.tensor_tensor(
                            out=dv[:, :, sh:CK], in0=sv[:, :, sh:CK], in1=sv[:, :, 0:CK - sh], op=ALU.add),
                            reads=[srck, dstk], writes=[dstk])
                        src, srck, dst, dstk = dst, dstk, src, srck
                        sh *= 2
                    b, bk = src, srck
                eb, ebk = tmp.next()
                pg.op('act', lambda e: e.activation(out=eb[:, 0:tw], in_=b[:, 0:tw], func=AF.Exp), reads=[bk], writes=[ebk])
                enb, enbk = tmp.next()
                pg.op('act', lambda e: e.activation(out=enb[:, 0:tw], in_=b[:, 0:tw], func=AF.Exp, scale=-1.0),
                      reads=[bk], writes=[enbk])
                ee, eek = tmp.next()
                if samp:
                    pg.op('dve', lambda e: e.memset(ee[:, 0:tw], 1.0), writes=[eek])
                else:
                    for c in range(nc_):
                        pg.op('act', lambda e, c=c: e.activation(out=ee[:, c * CK:(c + 1) * CK], in_=b[:, c * CK:(c + 1) * CK],
                                                               func=AF.Exp, bias=b[:, (c + 1) * CK - 1:(c + 1) * CK], scale=-1.0),
                              reads=[bk], writes=[eek])
                cd, cdk = cdr.next()
                if samp:
                    pg.op('dve', lambda e: e.tensor_copy(out=cd[:, 0:nc_], in_=eb[:, 0:tw]), reads=[ebk], writes=[cdk])
                    cc0 = nch
                else:
                    pg.op('dve', lambda e: e.tensor_copy(
                        out=cd[:, 0:nc_], in_=eb[:, 0:tw].rearrange("p (c s) -> p c s", s=CK)[:, :, CK - 1]),
                        reads=[ebk], writes=[cdk])
                    cc0 = t0 // CK
                with pg.nc.allow_non_contiguous_dma(reason="small cdec store"):
                    pg.dma('sp', scr['cdec'][j * 128:(j + 1) * 128, cc0:cc0 + nc_], cd[:, 0:nc_], reads=[cdk],
                           writes=[('cdec', j, t0)])
                q, qk = inr.next()
                pg.dma('sp', q[:, 0:tw], uglaT[j * 128:(j + 1) * 128, t0:t0 + tw], writes=[qk])
                k, kk = inr.next()
                pg.dma('sp', k[:, 0:tw], uglaT[1024 + j * 128:1024 + (j + 1) * 128, t0:t0 + tw], writes=[kk])
                o1, o1k = outr.next()
                pg.op('dve', lambda e: e.scalar_tensor_tensor(out=o1[:, 0:tw], in0=q[:, 0:tw], scalar=256.0 ** -0.5,
                                                              in1=eb[:, 0:tw], op0=ALU.mult, op1=ALU.mult),
                      reads=[qk, ebk], writes=[o1k])
                pg.dma('sp', scr['qin'][j * 128:(j + 1) * 128, t0:t0 + tw], o1[:, 0:tw], reads=[o1k], writes=[('qin', j, t0)])
                o2, o2k = outr.next()
                pg.op('dve', lambda e: e.tensor_tensor(out=o2[:, 0:tw], in0=k[:, 0:tw], in1=enb[:, 0:tw], op=ALU.mult),
                      reads=[kk, enbk], writes=[o2k])
                pg.dma('sp', scr['kin'][j * 128:(j + 1) * 128, t0:t0 + tw], o2[:, 0:tw], reads=[o2k], writes=[('kin', j, t0)])
                o3, o3k = outr.next()
                pg.op('dve', lambda e: e.tensor_tensor(out=o3[:, 0:tw], in0=k[:, 0:tw], in1=ee[:, 0:tw], op=ALU.mult),
                      reads=[kk, eek], writes=[o3k])
                pg.dma('sp', scr['kend'][j * 128:(j + 1) * 128, t0:t0 + tw], o3[:, 0:tw], reads=[o3k], writes=[('kend', j, t0)])
        pg.barrier()


def gla_main(pg, L, NS, uglaT, scr, cw, cst, st_in, st_out_p, st_out_s, yrawT):
    T = L + NS
    CK = 64
    nch = L // CK
    BIG = 256
    with ExitStack() as es:
        identb = pg.sb(es, [128, 128], BF16, 'gm_idb')
        ident = pg.sb(es, [128, 128], F32, 'gm_id')
        maskT = pg.sb(es, [64, 64], F32, 'gm_mask')
        nwb = pg.sb(es, [64, 512], F32, 'gm_nwb')
        pg.dma('sp', identb[:], cst['identb'], writes=[('gm_idb',)])
        pg.dma('sp', ident[:], cst['ident'], writes=[('gm_id',)])
        pg.dma('sp', maskT[:], cst['maskT'], writes=[('gm_mask',)])
        pg.dma('sp', nwb[:], cw['gla_nwb'], writes=[('gm_nwb',)])
        epsc = pg.sb(es, [64, 1], F32, 'gm_eps')
        pg.op('dve', lambda e: e.memset(epsc[:], 1e-6), writes=[('gm_eps',)])
        cdec = pg.sb(es, [128, 8, nch + NS], F32, 'gm_cdec')
        with pg.nc.allow_non_contiguous_dma(reason="small cdec load"):
            pg.dma('sp', cdec[:], scr['cdec'].rearrange("(j p) c -> p j c", p=128), writes=[('gm_cdec',)])
        qk_r = {n: Ring(pg, es, 2, [128, 8, BIG], BF16, 'gm_' + n) for n in ('qin', 'kin', 'kend')}
        vf_r = Ring(pg, es, 2, [128, 16, BIG], F32, 'gm_vf')
        vb_r = Ring(pg, es, 2, [128, 16, BIG], BF16, 'gm_vb')
        yfm_r = Ring(pg, es, 2, [128, 16, BIG], F32, 'gm_yfm')
        S = [pg.sb(es, [128, 8, 512], F32, 'gm_S%d' % i) for i in range(2)]
        Sb = [pg.sb(es, [128, 8, 512], BF16, 'gm_Sb%d' % i) for i in range(2)]
        pb = PRing(pg, es, 6, [128, 512], F32, 'gm_pb')
        pbf = PRing(pg, es, 2, [128, 1024], BF16, 'gm_pbf')
        keT_r = Ring(pg, es, 2, [64, 1024], BF16, 'gm_keT')
        vt_r = Ring(pg, es, 3, [64, 512], BF16, 'gm_vt')
        at_r = Ring(pg, es, 2, [64, 4, 64], BF16, 'gm_at')
        sq_r = Ring(pg, es, 2, [64, 512], F32, 'gm_sq')
        ss_r = Ring(pg, es, 4, [64, 1], F32, 'gm_ss')
        on_r = Ring(pg, es, 2, [64, 512], F32, 'gm_on')

        def chunk(ci, C, off, bufs, sb_):
            qin, qink, kin, kink, kend, kendk, vb, vbk, yfm, yfmk = bufs
            St, Stk, Sbt, Sbtk = S[sb_], ('gm_S', sb_), Sb[sb_], ('gm_Sb', sb_)
            pk_, pkk = pbf.next()
            for j in range(8):
                pg.op('pe', lambda e, j=j: e.transpose(pk_[0:C, j * 128:(j + 1) * 128], kend[:, j, off:off + C], identb[:]),
                      reads=[kendk, ('gm_idb',)], writes=[pkk])
            keT, keTk = keT_r.next()
            pg.op('act', lambda e: e.activation(out=keT[0:C, :], in_=pk_[0:C, :], func=AF.Identity), reads=[pkk], writes=[keTk])
            pa, pak = pb.next()
            for h in range(4):
                for jj in range(2):
                    j = h * 2 + jj
                    pg.op('pe', lambda e, h=h, j=j, jj=jj: e.matmul(pa[0:C, h * 64:h * 64 + C], kin[:, j, off:off + C],
                                                                    qin[:, j, off:off + C], start=(jj == 0), stop=(jj == 1)),
                          reads=[kink, qink], writes=[pak])
            at, atk = at_r.next()
            pg.op('dve', lambda e: e.tensor_tensor(
                out=at[0:C, :, 0:C], in0=pa[0:C, 0:256].rearrange("p (h t) -> p h t", h=4)[:, :, 0:C],
                in1=maskT[0:C, 0:C].unsqueeze(1).to_broadcast([C, 4, C]), op=ALU.mult),
                reads=[pak, ('gm_mask',)], writes=[atk])
            for h in range(4):
                pv, pvk = pbf.next()
                for b4 in range(4):
                    pg.op('pe', lambda e, b4=b4: e.transpose(pv[0:C, b4 * 128:(b4 + 1) * 128],
                                                            vb[:, h * 4 + b4, off:off + C], identb[:]),
                          reads=[vbk, ('gm_idb',)], writes=[pvk])
                vt, vtk = vt_r.next()
                pg.op('act', lambda e: e.activation(out=vt[0:C, :], in_=pv[0:C, 0:512], func=AF.Identity),
                      reads=[pvk], writes=[vtk])
                po, pok = pb.next()
                pg.op('pe', lambda e: e.matmul(po[0:C, :], at[0:C, h, 0:C], vt[0:C, :], start=True, stop=False),
                      reads=[atk, vtk], writes=[pok])
                for jj in range(2):
                    j = h * 2 + jj
                    pg.op('pe', lambda e, j=j, jj=jj: e.matmul(po[0:C, :], qin[:, j, off:off + C], Sbt[:, j, :],
                                                               start=False, stop=(jj == 1)),
                          reads=[qink, Sbtk], writes=[pok])
                sq, sqk = sq_r.next()
                pg.op('act', lambda e: e.activation(out=sq[0:C, :], in_=po[0:C, :], func=AF.Square), reads=[pok], writes=[sqk])
                ss, ssk = ss_r.next()
                pg.op('dve', lambda e: e.tensor_reduce(out=ss[0:C, :], in_=sq[0:C, :], axis=AX.X, op=ALU.add),
                      reads=[sqk], writes=[ssk])
                pg.op('act', lambda e: e.activation(out=ss[0:C, :], in_=ss[0:C, :], func=AF.Sqrt, bias=epsc[0:C, 0:1],
                                                    scale=1.0 / 512), reads=[ssk, ('gm_eps',)], writes=[ssk])
                pg.op('dve', lambda e: e.reciprocal(out=ss[0:C, :], in_=ss[0:C, :]), reads=[ssk], writes=[ssk])
                on, onk = on_r.next()
                pg.op('dve', lambda e: e.scalar_tensor_tensor(out=on[0:C, :], in0=po[0:C, :], scalar=ss[0:C, 0:1],
                                                              in1=nwb[0:C, :], op0=ALU.mult, op1=ALU.mult),
                      reads=[pok, ssk, ('gm_nwb',)], writes=[onk])
                pt, ptk = pb.next()
                for b4 in range(4):
                    pg.op('pe', lambda e, b4=b4: e.transpose(pt[:, b4 * 64:b4 * 64 + C], on[0:C, b4 * 128:(b4 + 1) * 128],
                                                            ident[0:C, 0:C]),
                          reads=[onk, ('gm_id',)], writes=[ptk])
                pg.op('act', lambda e: e.activation(
                    out=yfm[:, h * 4:(h + 1) * 4, off:off + C],
                    in_=pt[:, 0:256].rearrange("p (a b) -> p a b", a=4)[:, :, 0:C], func=AF.Identity),
                    reads=[ptk], writes=[yfmk])
                for jj in range(2):
                    j = h * 2 + jj
                    pc, pck = pb.next()
                    pg.op('pe', lambda e, j=j: e.matmul(pc[:, :], keT[0:C, j * 128:(j + 1) * 128], vt[0:C, :],
                                                        start=True, stop=True),
                          reads=[keTk, vtk], writes=[pck])
                    pg.op('dve', lambda e, j=j, pc=pc: e.scalar_tensor_tensor(
                        out=St[:, j, :], in0=St[:, j, :], scalar=cdec[:, j, ci:ci + 1], in1=pc[:, :],
                        op0=ALU.mult, op1=ALU.add), reads=[Stk, pck, ('gm_cdec',)], writes=[Stk])
            pg.op('pool', lambda e: e.tensor_copy(out=Sbt[:], in_=St[:]), reads=[Stk], writes=[Sbtk])

        def load_big(b0, bn):
            out = []
            for n in ('qin', 'kin', 'kend'):
                t, k = qk_r[n].next()
                pg.dma('sp', t[:, :, 0:bn], scr[n][:, b0:b0 + bn].rearrange("(j p) t -> p j t", p=128), writes=[k])
                out += [t, k]
            vf, vfk = vf_r.next()
            pg.dma('sp', vf[:, :, 0:bn], uglaT[2048:4096, b0:b0 + bn].rearrange("(j p) t -> p j t", p=128), writes=[vfk])
            vb, vbk = vb_r.next()
            pg.op('pool', lambda e: e.tensor_copy(out=vb[:, :, 0:bn], in_=vf[:, :, 0:bn]), reads=[vfk], writes=[vbk])
            yfm, yfmk = yfm_r.next()
            return out + [vb, vbk, yfm, yfmk]

        def store_big(b0, bn, yfm, yfmk):
            pg.dma('sp', yrawT[:, b0:b0 + bn].rearrange("(c p) t -> p c t", p=128), yfm[:, :, 0:bn], reads=[yfmk],
                   writes=[('gyrawT', b0)])

        if L:
            pg.op('dve', lambda e: e.memset(S[0][:], 0.0), writes=[('gm_S', 0)])
            pg.op('pool', lambda e: e.memset(Sb[0][:], 0.0), writes=[('gm_Sb', 0)])
            for b0 in range(0, L, BIG):
                bn = min(BIG, L - b0)
                bufs = load_big(b0, bn)
                for off in range(0, bn, CK):
                    chunk((b0 + off) // CK, CK, off, bufs, 0)
                store_big(b0, bn, bufs[8], bufs[9])
            pg.dma('sp', st_out_p.rearrange("p (j v) -> p j v", j=8), S[0][:], reads=[('gm_S', 0)], writes=[('gla_out_p',)])
        if NS:
            bufs = load_big(L, NS)
            for i in range(NS):
                b = i % 2
                pg.dma('sp', S[b][:], st_in[i].rearrange("p (j v) -> p j v", j=8), writes=[('gm_S', b)])
                pg.op('pool', lambda e, b=b: e.tensor_copy(out=Sb[b][:], in_=S[b][:]), reads=[('gm_S', b)],
                      writes=[('gm_Sb', b)])
                chunk(nch + i, 1, i, bufs, b)
                pg.dma('sp', st_out_s[i].rearrange("p (j v) -> p j v", j=8), S[b][:], reads=[('gm_S', b)],
                       writes=[('gla_out_s', i)])
            store_big(L, NS, bufs[8], bufs[9])
        pg.barrier()


def gla_post(pg, L, NS, yrawT, uglaT, yT):
    T = L + NS
    with ExitStack() as es:
        inr = Ring(pg, es, 6, [128, 512], F32, 'gq_in')
        outr = Ring(pg, es, 3, [128, 512], BF16, 'gq_out')
        for (t0, tw) in token_tiles(T):
            for blk in range(16):
                rs = slice(blk * 128, (blk + 1) * 128)
                y, yk = inr.next()
                pg.dma('sp', y[:, 0:tw], yrawT[rs, t0:t0 + tw], writes=[yk])
                g, gk = inr.next()
                pg.dma('sp', g[:, 0:tw], uglaT[4096 + blk * 128:4096 + (blk + 1) * 128, t0:t0 + tw], writes=[gk])
                pg.op('act', lambda e: e.activation(out=g[:, 0:tw], in_=g[:, 0:tw], func=AF.Silu), reads=[gk], writes=[gk])
                o, ok = outr.next()
                pg.op('dve', lambda e: e.tensor_tensor(out=o[:, 0:tw], in0=y[:, 0:tw], in1=g[:, 0:tw], op=ALU.mult),
                      reads=[yk, gk], writes=[ok])
                pg.dma('sp', yT[rs, t0:t0 + tw], o[:, 0:tw], reads=[ok], writes=[('yglaT', blk, t0)])
        pg.barrier()


def linear_B(cx, W, K, N, XsrcT, epi, tiles=None):
    pg = cx.pg
    KC = K // 128
    KP = KC // 8
    NG = N // 512
    tiles = tiles or token_tiles(cx.T)
    XTb = cx.XTflat[:, 0:KC * 512].rearrange("p (k t) -> p k t", t=512)
    gcount = 0
    for (t0, tw) in tiles:
        pg.dma('sp', XTb[:, :, 0:tw], XsrcT[:, t0:t0 + tw].rearrange("(kc p) t -> p kc t", p=128),
               writes=[('XT',)])
        order = [(g, kp) for g in range(NG) for kp in range(KP)]
        loaded = {}
        state = {'next': 0}

        def issue_loads(upto):
            while state['next'] < len(order) and state['next'] < upto:
                g, kp = order[state['next']]
                slot = cx.wi % cx.NW
                cx.wi += 1
                src = W[kp * 1024:(kp + 1) * 1024, g * 512:(g + 1) * 512].rearrange("(kc p) c -> p kc c", p=128)
                dstv = cx.wbuf[slot][:].rearrange("p a b -> p (a b)").rearrange("p (k c) -> p k c", c=512)
                pg.dma('pool', dstv, src, writes=[('w', slot)])
                loaded[(g, kp)] = slot
                state['next'] += 1

        idx = 0
        for g in range(NG):
            banks = [(gcount % 2) * 4 + cb for cb in range(4)]
            gcount += 1
            for kp in range(KP):
                issue_loads(idx + cx.NW)
                slot = loaded[(g, kp)]
                wv = cx.wbuf[slot][:].rearrange("p a b -> p (a b)").rearrange("p (k c) -> p k c", c=512)
                for cb in range(4):
                    for k in range(8):
                        kc = kp * 8 + k
                        pg.op('pe', lambda e, cb=cb, k=k, kc=kc, wv=wv: e.matmul(
                            cx.psb[banks[cb]][:, 0:tw], wv[:, k, cb * 128:(cb + 1) * 128], XTb[:, kc, 0:tw],
                            start=(kc == 0), stop=(kc == KC - 1)),
                            reads=[('w', slot), ('XT',)], writes=[('ps', banks[cb])])
                idx += 1
            for cb in range(4):
                epi(g * 512 + cb * 128, 128, t0, tw, cx.psb[banks[cb]][:, 0:tw], ('ps', banks[cb]))


D = 4096
IN_COLS = 30192
O1, O2, O3 = 6592, 6592 + 5152, 6592 + 5152 + 6160
DFF = 16384

SMALL_COLS = ['norm_mix', 'norm_ffn', 'norm_ple']


def build_program(nc, L, NS, DEPTH=2):
    T = L + NS
    nch = L // 64

    def din(name, shape, dt=F32):
        return nc.dram_tensor(name, list(shape), dt, kind="ExternalInput").ap()

    def dout(name, shape, dt=F32):
        return nc.dram_tensor(name, list(shape), dt, kind="ExternalOutput").ap()

    def dscr(name, shape, dt=F32):
        return nc.dram_tensor(name, list(shape), dt, kind="Internal").ap()

    I = {}
    I['xT0'] = din('xT0', [D, T])
    I['pT'] = din('pT', [DEPTH, 256, T])
    I['rw_st'] = din('rw_st', [DEPTH, max(NS, 1), 128, 1024])
    I['shiftT'] = din('shiftT', [DEPTH, 6592, max(NS, 1)])
    I['ssm_st'] = din('ssm_st', [DEPTH, max(NS, 1), 128, 2048])
    I['convT'] = din('convT', [DEPTH, 3072, max(NS, 1), 3])
    I['gla_st'] = din('gla_st', [DEPTH, max(NS, 1), 128, 4096])
    I['w_in'] = din('w_in', [DEPTH, D, IN_COLS])
    I['w_branch'] = din('w_branch', [DEPTH, 3, 2048, D])
    I['w_out'] = din('w_out', [DEPTH, D, D])
    I['w_ff1'] = din('w_ff1', [DEPTH, D, DFF])
    I['w_ff2'] = din('w_ff2', [DEPTH, DFF, D])
    I['w_ple_gate'] = din('w_ple_gate', [DEPTH, D, D])
    I['w_ple_proj'] = din('w_ple_proj', [DEPTH, 256, D])
    I['rw_w2'] = din('rw_w2', [DEPTH, 96, 2048])
    I['rw_a2'] = din('rw_a2', [DEPTH, 96, 2048])
    I['rw_g2'] = din('rw_g2', [DEPTH, 256, 2048])
    I['gla_fup'] = din('gla_fup', [DEPTH, 16, 1024])
    for n in ('norm_mix', 'norm_ffn', 'norm_ple'):
        I[n] = din(n, [DEPTH, 128, 32])
    I['norm_final'] = din('norm_final', [128, 32])
    I['rw_mu'] = din('rw_mu', [DEPTH, 128, 52])
    for n in ('rw_w0', 'rw_a0', 'rw_kk', 'rw_ka', 'rw_rk', 'rw_lnw', 'rw_lnb', 'ssm_dcol', 'ssm_norm'):
        I[n] = din(n, [DEPTH, 128, 16])
    I['convw'] = din('convw', [DEPTH, 128, 24, 4])
    I['convb'] = din('convb', [DEPTH, 128, 24])
    I['dtb'] = din('dtb', [DEPTH, 32, 1])
    I['alog'] = din('alog', [DEPTH, 32, 1])
    I['gla_fb'] = din('gla_fb', [DEPTH, 128, 8])
    I['gla_nwb'] = din('gla_nwb', [DEPTH, 64, 512])
    I['k_bones_bf'] = din('k_bones_bf', [128, 128], BF16)
    I['k_istack'] = din('k_istack', [128, 64], BF16)
    I['k_sel2'] = din('k_sel2', [128, 2], BF16)
    I['k_ident'] = din('k_ident', [128, 128])
    I['k_identb'] = din('k_identb', [128, 128], BF16)
    I['k_selh'] = din('k_selh', [32, 32, 64])
    I['k_negmask'] = din('k_negmask', [64, 64])
    I['k_sellast'] = din('k_sellast', [64, 128])
    I['k_maskT'] = din('k_maskT', [64, 64])
    I['k_maskSU'] = din('k_maskSU', [64, 64])
    I['k_maskSL'] = din('k_maskSL', [64, 64])

    O = {}
    O['yT'] = dout('yT', [D, T])
    O['rw_out_p'] = dout('rw_out_p', [DEPTH, 128, 1024])
    O['rw_out_s'] = dout('rw_out_s', [DEPTH, max(NS, 1), 128, 1024])
    O['shift_out'] = dout('shift_out', [DEPTH, 6592, NS + 1])
    O['ssm_out_p'] = dout('ssm_out_p', [DEPTH, 128, 2048])
    O['ssm_out_s'] = dout('ssm_out_s', [DEPTH, max(NS, 1), 128, 2048])
    O['conv_out_p'] = dout('conv_out_p', [DEPTH, 3072, 3])
    O['conv_out_s'] = dout('conv_out_s', [DEPTH, 3072, max(NS, 1), 3])
    O['gla_out_p'] = dout('gla_out_p', [DEPTH, 128, 4096])
    O['gla_out_s'] = dout('gla_out_s', [DEPTH, max(NS, 1), 128, 4096])

    S = {}
    S['xT'] = dscr('s_xT', [D, T])
    S['urwT'] = dscr('s_urwT', [6592, T])
    S['ussmT'] = dscr('s_ussmT', [5152, T])
    S['uglaT'] = dscr('s_uglaT', [6160, T])
    S['gateT'] = dscr('s_gateT', [3 * D, T], BF16)
    for n in ('lw', 'an', 'bn', 'kh', 'r', 'v', 'g', 'bonus'):
        S['rw_' + n] = dscr('s_rw_' + n, [2048, T])
    S['oT'] = dscr('s_oT', [2048, T])
    S['xbcT'] = dscr('s_xbcT', [3072, T])
    S['yrawT'] = dscr('s_yrawT', [2048, T])
    S['qin'] = dscr('s_qin', [1024, T], BF16)
    S['kin'] = dscr('s_kin', [1024, T], BF16)
    S['kend'] = dscr('s_kend', [1024, T], BF16)
    S['cdec'] = dscr('s_cdec', [1024, nch + NS])
    S['gyrawT'] = dscr('s_gyrawT', [2048, T])
    for n in ('yrwT', 'yssmT', 'yglaT'):
        S[n] = dscr('s_' + n, [2048, T], BF16)
    S['mergedT'] = dscr('s_mergedT', [D, T])
    S['aT'] = dscr('s_aT', [DFF, T], BF16)
    S['pgT'] = dscr('s_pgT', [D, T], BF16)

    with ExitStack() as es:
        pg = PG(nc, es)
        xcur = I['xT0']
        for l in range(DEPTH):
            with ExitStack() as es2:
                cx = Ctx(pg, es2, T)
                gm = load_cols(pg, es2, I['norm_mix'][l], 32, 'g_mix')
                phase_norm(cx, xcur, gm, D, 1e-6)
                pg.barrier()
                W = I['w_in'][l]
                linear_A(cx, W, D, [(0, O1)], make_epi(cx, S['urwT'], 'urwT', rowoff=0))
                linear_A(cx, W, D, [(O1, O2 - O1)], make_epi(cx, S['ussmT'], 'ussmT', rowoff=O1))
                linear_A(cx, W, D, [(O2, O3 - O2)], make_epi(cx, S['uglaT'], 'uglaT', rowoff=O2))
                linear_A(cx, W, D, [(O3, 3 * D)], make_epi(cx, S['gateT'], 'gateT', func=AF.Sigmoid, rowoff=O3))
                pg.barrier()
            lo = L - 1 if L else 0
            pg.dma('sp', O['shift_out'][l][:, (0 if L else 1):NS + 1], S['urwT'][:, lo:L + NS], writes=[('shift_out', l)])
            cw = dict(mu=I['rw_mu'][l], w0=I['rw_w0'][l], a0=I['rw_a0'][l], kk=I['rw_kk'][l], ka=I['rw_ka'][l],
                      rk=I['rw_rk'][l], lnw=I['rw_lnw'][l], lnb=I['rw_lnb'][l])
            lw = dict(w2=I['rw_w2'][l], a2=I['rw_a2'][l], g2=I['rw_g2'][l])
            scr = {n: S['rw_' + n] for n in ('lw', 'an', 'bn', 'kh', 'r', 'v', 'g', 'bonus')}
            cst = dict(ident=I['k_ident'], maskSU=I['k_maskSU'], maskSL=I['k_maskSL'], maskU=I['k_maskT'])
            rwkv_prep(pg, L, NS, S['urwT'], I['shiftT'][l], cw, lw, scr)
            rwkv_scan2(pg, L, NS, scr, cst, I['rw_st'][l], O['rw_out_p'][l], O['rw_out_s'][l], S['oT'])
            rwkv_post(pg, L, NS, S['oT'], scr, cw, S['yrwT'])
            cw = dict(convw=I['convw'][l], convb=I['convb'][l], dtb=I['dtb'][l], alog=I['alog'][l],
                      dcol=I['ssm_dcol'][l], ssm_norm=I['ssm_norm'][l])
            cst = dict(ident=I['k_ident'], selh=I['k_selh'], negmask=I['k_negmask'], sellast=I['k_sellast'])
            ssd_conv(pg, L, NS, S['ussmT'], I['convT'][l], cw, S['xbcT'], O['conv_out_p'][l], O['conv_out_s'][l])
            ssd_main(pg, L, NS, S['ussmT'], S['xbcT'], cw, cst, I['ssm_st'][l], O['ssm_out_p'][l], O['ssm_out_s'][l],
                     S['yrawT'])
            ssd_post(pg, L, NS, S['yrawT'], S['xbcT'], S['ussmT'], cw, S['yssmT'])
            cw = dict(gla_fb=I['gla_fb'][l], gla_nwb=I['gla_nwb'][l])
            lw = dict(gla_fup=I['gla_fup'][l])
            scr = dict(qin=S['qin'], kin=S['kin'], kend=S['kend'], cdec=S['cdec'])
            cst = dict(identb=I['k_identb'], ident=I['k_ident'], maskT=I['k_maskT'])
            gla_prep(pg, L, NS, S['uglaT'], cw, lw, scr)
            gla_main(pg, L, NS, S['uglaT'], scr, cw, cst, I['gla_st'][l], O['gla_out_p'][l], O['gla_out_s'][l],
                     S['gyrawT'])
            gla_post(pg, L, NS, S['gyrawT'], S['uglaT'], S['yglaT'])
            with ExitStack() as es2:
                cx = Ctx(pg, es2, T)
                for j, yn in enumerate(('yrwT', 'yssmT', 'yglaT')):
                    pg.dma('sp', cx.XT[:, 0:16, :], S[yn].rearrange("(kc p) t -> p kc t", p=128), writes=[('XT',)])
                    gate = S['gateT'][j * D:(j + 1) * D, :]
                    linear_A(cx, I['w_branch'][l, j], 2048, [(0, D)],
                             make_epi(cx, S['mergedT'], 'mergedT', mul=gate, mulkey='gateT',
                                      add=(S['mergedT'] if j else None), addkey='mergedT'))
                    pg.barrier()
                for kc in range(32):
                    pg.dma('pool', cx.XT[:, kc, :], S['mergedT'][kc * 128:(kc + 1) * 128, :], writes=[('XT',)])
                linear_A(cx, I['w_out'][l], D, [(0, D)], make_epi(cx, S['xT'], 'xT', add=xcur, addkey='xT'))
                pg.barrier()
                xcur = S['xT']
                gf = load_cols(pg, es2, I['norm_ffn'][l], 32, 'g_ffn')
                phase_norm(cx, xcur, gf, D, 1e-6)
                pg.barrier()
                linear_A(cx, I['w_ff1'][l], D, [(0, DFF)], make_epi(cx, S['aT'], 'aT', func=AF.Relu, square=True))
                pg.barrier()
                for qd in range(4):
                    pg.dma('sp', cx.XT[:, :, :], S['aT'][qd * D:(qd + 1) * D, :].rearrange("(kc p) t -> p kc t", p=128),
                           writes=[('XT',)])
                    linear_A(cx, I['w_ff2'][l][qd * D:(qd + 1) * D, :], D, [(0, D)],
                             make_epi(cx, S['xT'], 'xT', add=xcur, addkey='xT'))
                    pg.barrier()
                gp = load_cols(pg, es2, I['norm_ple'][l], 32, 'g_ple')
                phase_norm(cx, xcur, gp, D, 1e-6)
                pg.barrier()
                linear_A(cx, I['w_ple_gate'][l], D, [(0, D)], make_epi(cx, S['pgT'], 'pgT', func=AF.Sigmoid))
                pg.barrier()
                pg.dma('pool', cx.XT[:, 0:2, :], I['pT'][l].rearrange("(kc p) t -> p kc t", p=128), writes=[('XT',)])
                linear_A(cx, I['w_ple_proj'][l], 256, [(0, D)],
                         make_epi(cx, S['xT'], 'xT', mul=S['pgT'], mulkey='pgT', add=xcur, addkey='xT'))
                pg.barrier()
                if l == DEPTH - 1:
                    gfin = load_cols(pg, es2, I['norm_final'], 32, 'g_fin')
                    phase_norm(cx, xcur, gfin, D, 1e-6, dst=O['yT'])
                    pg.barrier()
        pg.barrier()
    return pg


def _colvec(v):
    v = np.asarray(v, np.float32).reshape(-1)
    n = (v.size + 127) // 128
    p = np.zeros(n * 128, np.float32)
    p[:v.size] = v
    return np.ascontiguousarray(p.reshape(n, 128).T)


def _stack(fn, arr):
    return np.ascontiguousarray(np.stack([fn(arr[l]) for l in range(arr.shape[0])]))


def _mu_layout(mu):
    out = np.zeros((128, 52), np.float32)
    out[:, :48] = mu[:6144].reshape(48, 128).T
    out[:96, 48] = mu[6144:6240]
    out[:96, 49] = mu[6240:6336]
    out[:, 50] = mu[6336:6464]
    out[:, 51] = mu[6464:6592]
    return out


def _consts():
    import ml_dtypes
    bf = ml_dtypes.bfloat16
    bones = np.zeros((128, 128), np.float32)
    bones[:64, :64] = 1
    bones[64:, 64:] = 1
    ist = np.zeros((128, 64), np.float32)
    ist[np.arange(128), np.arange(128) % 64] = 1
    sel2 = np.zeros((128, 2), np.float32)
    sel2[:64, 0] = 1
    sel2[64:, 1] = 1
    selh = np.zeros((32, 32, 64), np.float32)
    for h in range(32):
        selh[h, h, :] = 1
    s = np.arange(64)[:, None]
    t = np.arange(64)[None, :]
    sellast = np.zeros((64, 128), np.float32)
    sellast[63, :] = 1
    return {
        'k_bones_bf': bones.astype(bf), 'k_istack': ist.astype(bf), 'k_sel2': sel2.astype(bf),
        'k_ident': np.eye(128, dtype=np.float32), 'k_identb': np.eye(128, dtype=np.float32).astype(bf),
        'k_selh': selh, 'k_negmask': np.where(s <= t, 0.0, -30000.0).astype(np.float32),
        'k_sellast': sellast, 'k_maskT': (s <= t).astype(np.float32),
        'k_maskSU': (s < t).astype(np.float32), 'k_maskSL': (s > t).astype(np.float32),
    }


def shared_inputs(inp):
    f = lambda a: np.ascontiguousarray(np.asarray(a, np.float32))
    DEPTH = inp['w_in'].shape[0]
    m = {}
    for n in ('w_in', 'w_branch', 'w_out', 'w_ff1', 'w_ff2', 'w_ple_gate', 'w_ple_proj', 'rw_w2', 'rw_a2', 'rw_g2'):
        m[n] = f(inp[n])
    m['gla_fup'] = f(inp['gla_f_up'])
    for n in ('norm_mix', 'norm_ffn', 'norm_ple'):
        m[n] = _stack(_colvec, f(inp[n]))
    m['norm_final'] = _colvec(inp['norm_final'])
    m['rw_mu'] = _stack(_mu_layout, f(inp['rw_mu']))
    for n, src in (('rw_w0', 'rw_w0'), ('rw_a0', 'rw_a0'), ('rw_kk', 'rw_kk'), ('rw_ka', 'rw_ka'), ('rw_rk', 'rw_rk'),
                   ('rw_lnw', 'rw_ln_w'), ('rw_lnb', 'rw_ln_b'), ('ssm_norm', 'ssm_norm')):
        m[n] = _stack(_colvec, f(inp[src]).reshape(DEPTH, -1))
    m['ssm_dcol'] = _stack(lambda d: _colvec(np.repeat(d, 64)), f(inp['ssm_d']))
    m['convw'] = _stack(lambda w: np.ascontiguousarray(w.reshape(4, 24, 128).transpose(2, 1, 0)), f(inp['ssm_conv_w']))
    m['convb'] = _stack(_colvec, f(inp['ssm_conv_b']))
    m['dtb'] = f(inp['ssm_dt_bias']).reshape(DEPTH, 32, 1)
    m['alog'] = f(inp['ssm_a_log']).reshape(DEPTH, 32, 1)
    m['gla_fb'] = _stack(_colvec, f(inp['gla_f_bias']))
    m['gla_nwb'] = _stack(lambda g: np.ascontiguousarray(np.tile(g, (64, 1))), f(inp['gla_norm']))
    m.update(_consts())
    return m


def core_inputs(inp, seq, s0, NS):
    f = lambda a: np.asarray(a, np.float32)
    DEPTH = inp['w_in'].shape[0]
    xp = f(inp['x_prompt'])[seq]
    xs = f(inp['x_sample'])[s0:s0 + NS, 0]
    m = {}
    m['xT0'] = np.ascontiguousarray(np.concatenate([xp, xs], 0).T)
    pp = f(inp['p_prompt'])[:, seq]
    ps = f(inp['p_sample'])[:, s0:s0 + NS, 0]
    m['pT'] = np.ascontiguousarray(np.concatenate([pp, ps], 1).transpose(0, 2, 1))
    S = f(inp['state_rwkv'])[:, s0:s0 + NS]
    m['rw_st'] = np.ascontiguousarray(
        S.reshape(DEPTH, NS, 16, 2, 64, 64).transpose(0, 1, 3, 5, 2, 4).reshape(DEPTH, NS, 128, 1024))
    m['shiftT'] = np.ascontiguousarray(f(inp['state_shift'])[:, s0:s0 + NS].transpose(0, 2, 1))
    H = f(inp['state_ssm'])[:, s0:s0 + NS]
    m['ssm_st'] = np.ascontiguousarray(H.transpose(0, 1, 4, 2, 3).reshape(DEPTH, NS, 128, 2048))
    C = f(inp['state_conv'])[:, s0:s0 + NS]
    m['convT'] = np.ascontiguousarray(C.transpose(0, 3, 1, 2))
    G = f(inp['state_gla'])[:, s0:s0 + NS]
    m['gla_st'] = np.ascontiguousarray(
        G.reshape(DEPTH, NS, 8, 128, 512).transpose(0, 1, 3, 2, 4).reshape(DEPTH, NS, 128, 4096))
    return m


def unpack_core(R, L, NS):
    DEPTH = R['rw_out_p'].shape[0]
    g = lambda k: np.asarray(R[k], np.float32)
    o = {}
    yT = g('yT')
    o['y_p'] = np.ascontiguousarray(yT[:, :L].T)
    o['y_s'] = np.ascontiguousarray(yT[:, L:].T)
    conv = lambda Dv: Dv.reshape(Dv.shape[0], -1, 2, 64, 16, 64).transpose(0, 1, 4, 2, 5, 3).reshape(Dv.shape[0], -1, 32, 64, 64)
    o['rw_p'] = conv(g('rw_out_p')[:, None])[:, 0]
    o['rw_s'] = conv(g('rw_out_s'))
    sh = g('shift_out')
    o['sh_p'] = sh[:, :, 0]
    o['sh_s'] = sh[:, :, 1:].transpose(0, 2, 1)
    o['ssm_p'] = g('ssm_out_p').reshape(DEPTH, 128, 32, 64).transpose(0, 2, 3, 1)
    o['ssm_s'] = g('ssm_out_s').reshape(DEPTH, -1, 128, 32, 64).transpose(0, 1, 3, 4, 2)
    o['conv_p'] = g('conv_out_p').transpose(0, 2, 1)
    o['conv_s'] = g('conv_out_s').transpose(0, 2, 3, 1)
    o['gla_p'] = g('gla_out_p').reshape(DEPTH, 128, 8, 512).transpose(0, 2, 1, 3).reshape(DEPTH, 4, 256, 512)
    o['gla_s'] = g('gla_out_s').reshape(DEPTH, -1, 128, 8, 512).transpose(0, 1, 3, 2, 4).reshape(DEPTH, -1, 4, 256, 512)
    return o


def kernel(**inputs):
    B, L = inputs['x_prompt'].shape[0], inputs['x_prompt'].shape[1]
    NSAMP = inputs['x_sample'].shape[0]
    n = 8
    NS = NSAMP // n
    DEPTH = inputs['w_in'].shape[0]
    nc = bass.Bass("TRN2", target_bir_lowering=False)
    build_program(nc, L, NS, DEPTH)
    shared = shared_inputs(inputs)
    in_maps = []
    for c in range(n):
        m = dict(shared)
        m.update(core_inputs(inputs, c % B, c * NS, NS))
        in_maps.append(m)
    res = run_bass_kernel_spmd(nc, in_maps, core_ids=list(range(n)))
    outs = [unpack_core(r, L, NS) for r in res.results]
    cat_p = lambda k, ax: np.ascontiguousarray(np.stack([outs[c][k] for c in range(B)], axis=ax)).astype(np.float32)
    cat_s = lambda k, ax: np.ascontiguousarray(np.concatenate([outs[c][k] for c in range(n)], axis=ax)).astype(np.float32)
    y_prompt = cat_p('y_p', 0)
    y_sample = cat_s('y_s', 0)[:, None, :]
    return (y_prompt, y_sample,
            cat_p('rw_p', 1), cat_s('rw_s', 1),
            cat_p('sh_p', 1), cat_s('sh_s', 1),
            cat_p('ssm_p', 1), cat_s('ssm_s', 1),
            cat_p('conv_p', 1), cat_s('conv_s', 1),
            cat_p('gla_p', 1), cat_s('gla_s', 1))


def rwkv_scan2(pg, L, NS, scr, cst, st_in, st_out_p, st_out_s, oT):
    names = ['lw', 'an', 'bn', 'kh', 'r', 'v']
    CK = 64
    OB = 128
    with ExitStack() as es:
        ident = pg.sb(es, [128, 128], F32, 'c2_id')
        mSU = pg.sb(es, [64, 64], F32, 'c2_mSU')
        mSL = pg.sb(es, [64, 64], F32, 'c2_mSL')
        mU = pg.sb(es, [64, 64], F32, 'c2_mU')
        pg.dma('sp', ident[:], cst['ident'], writes=[('c2_id',)])
        pg.dma('sp', mSU[:], cst['maskSU'], writes=[('c2_mSU',)])
        pg.dma('sp', mSL[:], cst['maskSL'], writes=[('c2_mSL',)])
        pg.dma('sp', mU[:], cst['maskU'], writes=[('c2_mU',)])
        KM = {'SU': (mSU, ('c2_mSU',)), 'SL': (mSL, ('c2_mSL',)), 'U': (mU, ('c2_mU',))}
        STb = [pg.sb(es, [64, 32, 64], F32, 'c2_ST%d' % i) for i in range(2)]
        sh = [64, 8, 64]
        inr = {n: Ring(pg, es, 2, sh, F32, 'c2i_' + n) for n in names}
        cumr = Ring(pg, es, 4, sh, F32, 'c2_cum')
        er = Ring(pg, es, 4, sh, F32, 'c2_e')
        tr_ = {n: Ring(pg, es, 2, sh, F32, 'c2t_' + n) for n in ('At', 'Bt', 'Kt', 'Be', 'Ke', 'N', 'NT', 'AakT', 'T', 'ATt')}
        Pr = Ring(pg, es, 4, sh, F32, 'c2_P')
        PTr = Ring(pg, es, 4, sh, F32, 'c2_PT')
        per_ = {n: Ring(pg, es, 4, sh, F32, 'c2p_' + n) for n in ('Rt', 'Abr', 'Akr', 'Ah', 'Wh', 'BeT', 'KeT', 'VT')}
        gCr = Ring(pg, es, 4, [64, 8], F32, 'c2_gC')
        UTr = Ring(pg, es, 2, sh, F32, 'c2_UT')
        OTr = Ring(pg, es, 2, sh, F32, 'c2_OT')
        osb = Ring(pg, es, 2, [128, 16, OB], F32, 'c2_osb')
        pb = PRing(pg, es, 8, [128, 512], F32, 'c2_pb')

        def v3(p, C):
            return p[0:C, 0:512].rearrange("p (h t) -> p h t", h=8)

        def pre(c0, C, hb, out):
            h0 = hb * 8
            q, qk = {}, {}
            for n in names:
                q[n], qk[n] = inr[n].next()
                with pg.nc.allow_non_contiguous_dma(reason="single-token sample columns"):
                    pg.dma('sp', q[n][:, :, 0:C], scr[n].rearrange("(h k) t -> k h t", k=64)[:, h0:h0 + 8, c0:c0 + C],
                           writes=[qk[n]])
            if C > 1:
                src, srck = q['lw'], qk['lw']
                s_ = 1
                while s_ < C:
                    dst, dstk = cumr.next()
                    pg.op('pool', lambda e, src=src, dst=dst, s_=s_: e.tensor_copy(out=dst[:, :, 0:s_], in_=src[:, :, 0:s_]),
                          reads=[srck], writes=[dstk])
                    pg.op('pool', lambda e, src=src, dst=dst, s_=s_: e.tensor_tensor(
                        out=dst[:, :, s_:C], in0=src[:, :, s_:C], in1=src[:, :, 0:C - s_], op=ALU.add),
                        reads=[srck, dstk], writes=[dstk])
                    src, srck = dst, dstk
                    s_ *= 2
                cum, cumk = src, srck
            else:
                cum, cumk = q['lw'], qk['lw']
            lw = q['lw']

            def ew(fn_pre, scale, dsts):
                e, ek = er.next()
                if fn_pre is not None:
                    fn_pre(e, ek)
                    src_, srck_ = e, ek
                else:
                    src_, srck_ = cum, cumk
                pg.op('act', lambda en: en.activation(out=e[:, :, 0:C], in_=src_[:, :, 0:C], func=AF.Exp, scale=scale),
                      reads=[srck_], writes=[ek])
                for (srcn, (d, dk), eng) in dsts:
                    pg.op(eng, lambda en, srcn=srcn, d=d: en.tensor_tensor(out=d[:, :, 0:C], in0=q[srcn][:, :, 0:C],
                                                                         in1=e[:, :, 0:C], op=ALU.mult),
                          reads=[qk[srcn], ek], writes=[dk])
                return e, ek

            At = tr_['At'].next()
            Bt = tr_['Bt'].next()
            Kt = tr_['Kt'].next()
            Be = tr_['Be'].next()
            Ke = tr_['Ke'].next()
            Rt = per_['Rt'].next()
            ew(lambda e, ek: pg.op('dve', lambda en: en.tensor_tensor(out=e[:, :, 0:C], in0=cum[:, :, 0:C],
                                                                      in1=lw[:, :, 0:C], op=ALU.subtract),
                                   reads=[cumk, qk['lw']], writes=[ek]), 1.0, [('an', At, 'dve')])
            ew(None, -1.0, [('bn', Bt, 'dve'), ('kh', Kt, 'pool')])
            e3, e3k = ew(None, 1.0, [('r', Rt, 'dve')])
            gC, gCk = gCr.next()
            pg.op('dve', lambda en: en.tensor_copy(out=gC[:, :], in_=e3[:, :, C - 1]), reads=[e3k], writes=[gCk])
            ew(lambda e, ek: pg.op('dve', lambda en: en.tensor_tensor(
                out=e[:, :, 0:C], in0=cum[:, :, C - 1:C].to_broadcast([64, 8, C]), in1=cum[:, :, 0:C], op=ALU.subtract),
                reads=[cumk], writes=[ek]), 1.0, [('bn', Be, 'dve'), ('kh', Ke, 'pool')])
            res = {}
            for (nm, X, Y, mk, ring) in (('N', Bt, At, 'SU', tr_['N']), ('NT', At, Bt, 'SL', tr_['NT']),
                                         ('AakT', At, Kt, 'SL', tr_['AakT']), ('Abr', Bt, Rt, 'U', per_['Abr']),
                                         ('Akr', Kt, Rt, 'U', per_['Akr'])):
                p, pk = pb.next()
                for hl in range(8):
                    pg.op('pe', lambda e, hl=hl, X=X, Y=Y, p=p: e.matmul(p[0:C, hl * 64:hl * 64 + C], X[0][:, hl, 0:C],
                                                                        Y[0][:, hl, 0:C], start=True, stop=True),
                          reads=[X[1], Y[1]], writes=[pk])
                d, dk = ring.next()
                m, mkey = KM[mk]
                pg.op('dve', lambda e, p=p, d=d, m=m: e.tensor_tensor(
                    out=d[0:C, :, 0:C], in0=v3(p, C)[:, :, 0:C], in1=m[0:C, 0:C].unsqueeze(1).to_broadcast([C, 8, C]),
                    op=ALU.mult), reads=[pk, mkey], writes=[dk])
                res[nm] = (d, dk)
            for (nm, X, ring) in (('ATt', At, tr_['ATt']), ('BeT', Be, per_['BeT']), ('KeT', Ke, per_['KeT']),
                                  ('VT', (q['v'], qk['v']), per_['VT'])):
                p, pk = pb.next()
                for hl in range(8):
                    pg.op('pe', lambda e, hl=hl, X=X, p=p: e.transpose(p[0:C, hl * 64:(hl + 1) * 64], X[0][:, hl, 0:C],
                                                                      ident[0:64, 0:64]),
                          reads=[X[1], ('c2_id',)], writes=[pk])
                d, dk = ring.next()
                pg.op('act', lambda e, p=p, d=d: e.activation(out=d[0:C, :, :], in_=v3(p, C), func=AF.Identity),
                      reads=[pk], writes=[dk])
                res[nm] = (d, dk)
            T, Tk = tr_['T'].next()
            N, Nk = res['N']
            pg.op('dve', lambda e: e.tensor_tensor(out=T[0:C, :, 0:C], in0=N[0:C, :, 0:C],
                                                   in1=ident[0:C, 0:C].unsqueeze(1).to_broadcast([C, 8, C]), op=ALU.add),
                  reads=[Nk, ('c2_id',)], writes=[Tk])
            yield
            P, Pk = res['N']
            PT, PTk = res['NT']
            rounds = 0
            s_ = 2
            while s_ < C:
                rounds += 1
                s_ *= 2
            for i in range(rounds):
                last = (i == rounds - 1)
                if not last:
                    p, pk = pb.next()
                    for hl in range(8):
                        pg.op('pe', lambda e, hl=hl, p=p, P=P, PT=PT: e.matmul(
                            p[0:C, hl * 64:hl * 64 + C], PT[0:C, hl, 0:C], P[0:C, hl, 0:C], start=True, stop=True),
                            reads=[Pk, PTk], writes=[pk])
                    Pn, Pnk = Pr.next()
                    pg.op('act', lambda e, p=p, Pn=Pn: e.activation(out=Pn[0:C, :, 0:C], in_=v3(p, C)[:, :, 0:C],
                                                                    func=AF.Identity), reads=[pk], writes=[Pnk])
                p2, p2k = pb.next()
                for hl in range(8):
                    pg.op('pe', lambda e, hl=hl, p2=p2, P=P, PT=PT: e.matmul(
                        p2[0:C, hl * 64:hl * 64 + C], P[0:C, hl, 0:C], PT[0:C, hl, 0:C], start=True, stop=True),
                        reads=[Pk, PTk], writes=[p2k])
                PTn, PTnk = PTr.next()
                pg.op('act', lambda e, p2=p2, PTn=PTn: e.activation(out=PTn[0:C, :, 0:C], in_=v3(p2, C)[:, :, 0:C],
                                                                    func=AF.Identity), reads=[p2k], writes=[PTnk])
                yield
                p3, p3k = pb.next()
                for hl in range(8):
                    pg.op('pe', lambda e, hl=hl, p3=p3, PTn=PTn: e.matmul(
                        p3[0:C, hl * 64:hl * 64 + C], PTn[0:C, hl, 0:C], T[0:C, hl, 0:C], start=True, stop=True),
                        reads=[PTnk, Tk], writes=[p3k])
                pg.op('dve', lambda e, p3=p3: e.tensor_tensor(out=T[0:C, :, 0:C], in0=T[0:C, :, 0:C],
                                                              in1=v3(p3, C)[:, :, 0:C], op=ALU.add),
                      reads=[Tk, p3k], writes=[Tk])
                yield
                if not last:
                    P, Pk = Pn, Pnk
                PT, PTk = PTn, PTnk
            ATt, ATtk = res['ATt']
            AakT, AakTk = res['AakT']
            p, pk = pb.next()
            for hl in range(8):
                pg.op('pe', lambda e, hl=hl, p=p: e.matmul(p[0:64, hl * 64:hl * 64 + C], ATt[0:C, hl, :], T[0:C, hl, 0:C],
                                                           start=True, stop=True), reads=[ATtk, Tk], writes=[pk])
            Ah, Ahk = per_['Ah'].next()
            pg.op('act', lambda e, p=p: e.activation(out=Ah[:, :, 0:C], in_=v3(p, 64)[:, :, 0:C], func=AF.Identity),
                  reads=[pk], writes=[Ahk])
            p, pk = pb.next()
            for hl in range(8):
                pg.op('pe', lambda e, hl=hl, p=p: e.matmul(p[0:C, hl * 64:hl * 64 + C], AakT[0:C, hl, 0:C], T[0:C, hl, 0:C],
                                                           start=True, stop=True), reads=[AakTk, Tk], writes=[pk])
            Wh, Whk = per_['Wh'].next()
            pg.op('act', lambda e, p=p: e.activation(out=Wh[0:C, :, 0:C], in_=v3(p, C)[:, :, 0:C], func=AF.Identity),
                  reads=[pk], writes=[Whk])
            out.update(dict(Rt=Rt, gC=(gC, gCk), Abr=res['Abr'], Akr=res['Akr'], Ah=(Ah, Ahk), Wh=(Wh, Whk),
                            BeT=res['BeT'], KeT=res['KeT'], VT=res['VT']))

        def seq(C, hb, pr, ST, stk, ob, obk, ooff):
            h0 = hb * 8
            Rt, Rtk = pr['Rt']
            gC, gCk = pr['gC']
            Abr, Abrk = pr['Abr']
            Akr, Akrk = pr['Akr']
            Ah, Ahk = pr['Ah']
            Wh, Whk = pr['Wh']
            BeT, BeTk = pr['BeT']
            KeT, KeTk = pr['KeT']
            VT, VTk = pr['VT']
            p, pk = pb.next()
            for hl in range(8):
                o_ = p[0:C, hl * 64:(hl + 1) * 64]
                pg.op('pe', lambda e, hl=hl, o_=o_: e.matmul(o_, Ah[:, hl, 0:C], ST[:, h0 + hl, :], start=True, stop=False),
                      reads=[Ahk, stk], writes=[pk])
                pg.op('pe', lambda e, hl=hl, o_=o_: e.matmul(o_, Wh[0:C, hl, 0:C], VT[0:C, hl, :], start=False, stop=True),
                      reads=[Whk, VTk], writes=[pk])
            UT, UTk = UTr.next()
            pg.op('act', lambda e: e.activation(out=UT[0:C, :, :], in_=v3(p, C), func=AF.Identity), reads=[pk], writes=[UTk])
            p2, p2k = pb.next()
            for hl in range(8):
                o_ = p2[0:C, hl * 64:(hl + 1) * 64]
                pg.op('pe', lambda e, hl=hl, o_=o_: e.matmul(o_, Rt[:, hl, 0:C], ST[:, h0 + hl, :], start=True, stop=False),
                      reads=[Rtk, stk], writes=[p2k])
                pg.op('pe', lambda e, hl=hl, o_=o_: e.matmul(o_, Abr[0:C, hl, 0:C], UT[0:C, hl, :], start=False, stop=False),
                      reads=[Abrk, UTk], writes=[p2k])
                pg.op('pe', lambda e, hl=hl, o_=o_: e.matmul(o_, Akr[0:C, hl, 0:C], VT[0:C, hl, :], start=False, stop=True),
                      reads=[Akrk, VTk], writes=[p2k])
            OT, OTk = OTr.next()
            pg.op('dve', lambda e: e.tensor_copy(out=OT[0:C, :, :], in_=v3(p2, C)), reads=[p2k], writes=[OTk])
            p3, p3k = pb.next()
            for j in range(4):
                pg.op('pe', lambda e, j=j: e.transpose(p3[:, j * 64:j * 64 + C],
                                                      OT[0:C, 2 * j:2 * j + 2, :].rearrange("p a b -> p (a b)"),
                                                      ident[0:C, 0:C]), reads=[OTk, ('c2_id',)], writes=[p3k])
            pg.op('act', lambda e: e.activation(out=ob[:, hb * 4:(hb + 1) * 4, ooff:ooff + C],
                                                in_=p3[:, 0:256].rearrange("p (a b) -> p a b", a=4)[:, :, 0:C],
                                                func=AF.Identity), reads=[p3k], writes=[obk])
            p4, p4k = pb.next()
            for hl in range(8):
                o_ = p4[0:64, hl * 64:(hl + 1) * 64]
                pg.op('pe', lambda e, hl=hl, o_=o_: e.matmul(o_, BeT[0:C, hl, :], UT[0:C, hl, :], start=True, stop=False),
                      reads=[BeTk, UTk], writes=[p4k])
                pg.op('pe', lambda e, hl=hl, o_=o_: e.matmul(o_, KeT[0:C, hl, :], VT[0:C, hl, :], start=False, stop=True),
                      reads=[KeTk, VTk], writes=[p4k])
            pg.op('dve', lambda e: e.tensor_tensor(out=ST[:, h0:h0 + 8, :], in0=ST[:, h0:h0 + 8, :],
                                                   in1=gC[:, :].unsqueeze(2).to_broadcast([64, 8, 64]), op=ALU.mult),
                  reads=[stk, gCk], writes=[stk])
            pg.op('dve', lambda e: e.tensor_tensor(out=ST[:, h0:h0 + 8, :], in0=ST[:, h0:h0 + 8, :], in1=v3(p4, 64),
                                                   op=ALU.add), reads=[stk, p4k], writes=[stk])

        def run_chunk(c0, C, ST, stk, ob, obk, ooff):
            for pair in range(2):
                outs = [{}, {}]
                gens = [pre(c0, C, pair * 2 + i, outs[i]) for i in range(2)]
                live = list(gens)
                while live:
                    for g in list(live):
                        try:
                            next(g)
                        except StopIteration:
                            live.remove(g)
                for i in range(2):
                    seq(C, pair * 2 + i, outs[i], ST, stk, ob, obk, ooff)

        def st_view(ap2d):
            return ap2d.rearrange("(hh k) (hp v) -> k hp hh v", hh=2, v=64)

        def sb_view(t):
            return t[:].rearrange("k (hp hh) v -> k hp hh v", hh=2)

        oTv = oT.rearrange("(c p) t -> p c t", p=128)
        if L:
            pg.op('dve', lambda e: e.memset(STb[0][:], 0.0), writes=[('c2_ST', 0)])
            for b0 in range(0, L, OB):
                bn_ = min(OB, L - b0)
                ob, obk = osb.next()
                for off in range(0, bn_, CK):
                    run_chunk(b0 + off, CK, STb[0], ('c2_ST', 0), ob, obk, off)
                pg.dma('sp', oTv[:, :, b0:b0 + bn_], ob[:, :, 0:bn_], reads=[obk], writes=[('oT', b0)])
            for hh in range(2):
                pg.dma('sp', st_view(st_out_p)[:, :, hh, :], sb_view(STb[0])[:, :, hh, :], reads=[('c2_ST', 0)],
                       writes=[('st_out_p', hh)])
        if NS:
            ob, obk = osb.next()
            for i in range(NS):
                b = i % 2
                for hh in range(2):
                    pg.dma('sp', sb_view(STb[b])[:, :, hh, :], st_view(st_in[i])[:, :, hh, :], writes=[('c2_ST', b)])
                run_chunk(L + i, 1, STb[b], ('c2_ST', b), ob, obk, i)
                for hh in range(2):
                    pg.dma('sp', st_view(st_out_s[i])[:, :, hh, :], sb_view(STb[b])[:, :, hh, :], reads=[('c2_ST', b)],
                           writes=[('st_out_s', i, hh)])
            pg.dma('sp', oTv[:, :, L:L + NS], ob[:, :, 0:NS], reads=[obk], writes=[('oT', L)])
        pg.barrier()
```

```python
import numpy as np
import concourse.bass as bass
import concourse.mybir as mybir
from concourse.bass_utils import run_bass_kernel_spmd
from contextlib import ExitStack

F32 = mybir.dt.float32
BF16 = mybir.dt.bfloat16
AF = mybir.ActivationFunctionType
ALU = mybir.AluOpType
AX = mybir.AxisListType


class PG:
    def __init__(self, nc, es):
        self.nc = nc
        self.es = es
        self.E = {'pe': nc.tensor, 'dve': nc.vector, 'act': nc.scalar,
                  'pool': nc.gpsimd, 'sp': nc.sync}
        self.sem = {}
        self.cnt = {}
        for e in ('pe', 'dve', 'act', 'pool'):
            self.sem[e] = es.enter_context(nc.semaphore('s_' + e))
            self.cnt[e] = 0
        self.dq = {}
        for q, n in (('sp', 16), ('pool', 8), ('act', 6)):
            names = []
            for i in range(n):
                nm = 'd_%s%d' % (q, i)
                self.sem[nm] = es.enter_context(nc.semaphore(nm))
                self.cnt[nm] = 0
                names.append(nm)
            self.dq[q] = names
        self.rr = {'sp': 0, 'pool': 0, 'act': 0}
        self.waited = {e: {} for e in self.E}
        self.lastw = {}
        self.rd = {}
        self.nops = 0
        self._uid = 0

    def uid(self, p='t'):
        self._uid += 1
        return '%s%d' % (p, self._uid)

    def sb(self, es, shape, dt, name=None):
        return es.enter_context(self.nc.sbuf_tensor(self.uid('sb_' + (name or '')), list(shape), dt))

    def ps(self, es, shape, dt=F32, name=None):
        return es.enter_context(self.nc.psum_tensor(self.uid('pz_' + (name or '')), list(shape), dt))

    def _wait(self, e, toks):
        need = {}
        for (s, v) in toks:
            if s == e and e == 'pe':
                continue
            if v > need.get(s, 0):
                need[s] = v
        w = self.waited[e]
        for s, v in need.items():
            if w.get(s, 0) < v:
                self.E[e].wait_ge(self.sem[s], v)
                w[s] = v

    def _deps(self, reads, writes):
        toks = []
        for k in reads:
            t = self.lastw.get(k)
            if t:
                toks.append(t)
        for k in writes:
            t = self.lastw.get(k)
            if t:
                toks.append(t)
            r = self.rd.get(k)
            if r:
                toks.extend(r.items())
        return toks

    def _commit(self, tok, reads, writes):
        for k in reads:
            d = self.rd.setdefault(k, {})
            if d.get(tok[0], 0) < tok[1]:
                d[tok[0]] = tok[1]
        for k in writes:
            self.lastw[k] = tok
            self.rd[k] = {}

    def op(self, e, fn, reads=(), writes=()):
        self._wait(e, self._deps(reads, writes))
        inst = fn(self.E[e])
        self.cnt[e] += 1
        inst.then_inc(self.sem[e], 1)
        self._commit((e, self.cnt[e]), reads, writes)
        self.nops += 1

    def dma(self, q, out, in_, reads=(), writes=(), **kw):
        names = self.dq[q]
        s = names[self.rr[q] % len(names)]
        self.rr[q] += 1
        toks = self._deps(reads, writes)
        if self.cnt[s] > 0:
            toks.append((s, self.cnt[s]))
        self._wait(q, toks)
        inst = self.E[q].dma_start(out=out, in_=in_, **kw)
        self.cnt[s] += 16
        inst.then_inc(self.sem[s], 16)
        self._commit((s, self.cnt[s]), reads, writes)
        self.nops += 1

    def barrier(self):
        toks = [(s, c) for s, c in self.cnt.items() if c > 0]
        for e in self.E:
            self._wait(e, toks)


def token_tiles(T, w=512):
    out = []
    t = 0
    while t < T:
        out.append((t, min(w, T - t)))
        t += w
    return out


class Ctx:
    def __init__(self, pg, es, T, KCX=32):
        self.pg = pg
        self.T = T
        nc = pg.nc
        self.XTflat = pg.sb(es, [128, max(KCX * T, 128 * 512)], BF16, 'XT')
        self.XT = self.XTflat[:, 0:KCX * T].rearrange("p (k t) -> p k t", t=T)
        self.NW = 4
        self.wbuf = [pg.sb(es, [128, 16, 256], BF16, 'wb%d' % i) for i in range(self.NW)]
        self.wi = 0
        self.NB = 8
        self.psb = [pg.ps(es, [128, 512], F32, 'psb%d' % i) for i in range(self.NB)]
        self.bi = 0
        self.NS = 4
        self.stg = [pg.sb(es, [128, 512], F32, 'stg%d' % i) for i in range(self.NS)]
        self.si = 0
        self.aux = [pg.sb(es, [128, 512], F32, 'aux%d' % i) for i in range(self.NS)]
        self.auxb = [pg.sb(es, [128, 512], BF16, 'auxb%d' % i) for i in range(self.NS)]
        self.ai = 0
        self.ost = [pg.sb(es, [128, 512], BF16, 'ost%d' % i) for i in range(self.NS)]
        self.oi = 0
        self.ones = pg.sb(es, [128, 128], F32, 'ones')
        pg.op('dve', lambda e: e.memset(self.ones[:], 1.0), writes=[('ones',)])
        self.epsc = pg.sb(es, [128, 1], F32, 'epsc')
        pg.op('dve', lambda e: e.memset(self.epsc[:], 1e-6), writes=[('epsc',)])

    def bank(self):
        b = self.bi % self.NB
        self.bi += 1
        return b


def phase_norm(cx, xT, gcol, D, eps, tiles=None, dst=None):
    pg = cx.pg
    KC = D // 128
    for (t0, tw) in (tiles or token_tiles(cx.T)):
        b = cx.bank()
        for kc in range(KC):
            s = cx.si % cx.NS
            cx.si += 1
            pg.dma('sp', cx.stg[s][:, 0:tw], xT[kc * 128:(kc + 1) * 128, t0:t0 + tw],
                   reads=[('xT', kc, t0)], writes=[('stg', s)])
            a = cx.ai % cx.NS
            cx.ai += 1
            pg.op('act', lambda e, s=s, a=a: e.activation(out=cx.aux[a][:, 0:tw], in_=cx.stg[s][:, 0:tw],
                                                         func=AF.Square),
                  reads=[('stg', s)], writes=[('aux', a)])
            pg.op('pe', lambda e, a=a, kc=kc: e.matmul(cx.psb[b][:, 0:tw], cx.ones[:], cx.aux[a][:, 0:tw],
                                                      start=(kc == 0), stop=(kc == KC - 1)),
                  reads=[('aux', a), ('ones',)], writes=[('ps', b)])
        a = cx.ai % cx.NS
        cx.ai += 1
        pg.op('act', lambda e: e.activation(out=cx.aux[a][:, 0:tw], in_=cx.psb[b][:, 0:tw],
                                            func=AF.Sqrt, bias=cx.epsc[:, 0:1], scale=1.0 / D),
              reads=[('ps', b), ('epsc',)], writes=[('aux', a)])
        pg.op('dve', lambda e: e.reciprocal(out=cx.aux[a][:, 0:tw], in_=cx.aux[a][:, 0:tw]),
              reads=[('aux', a)], writes=[('aux', a)])
        for kc in range(KC):
            s = cx.si % cx.NS
            cx.si += 1
            pg.dma('sp', cx.stg[s][:, 0:tw], xT[kc * 128:(kc + 1) * 128, t0:t0 + tw],
                   reads=[('xT', kc, t0)], writes=[('stg', s)])
            if dst is None:
                pg.op('dve', lambda e, s=s, kc=kc: e.scalar_tensor_tensor(
                    out=cx.XT[:, kc, t0:t0 + tw], in0=cx.stg[s][:, 0:tw], scalar=gcol[:, kc:kc + 1],
                    in1=cx.aux[a][:, 0:tw], op0=ALU.mult, op1=ALU.mult),
                    reads=[('stg', s), ('aux', a), ('gam',)], writes=[('XT',)])
            else:
                pg.op('dve', lambda e, s=s, kc=kc: e.scalar_tensor_tensor(
                    out=cx.stg[s][:, 0:tw], in0=cx.stg[s][:, 0:tw], scalar=gcol[:, kc:kc + 1],
                    in1=cx.aux[a][:, 0:tw], op0=ALU.mult, op1=ALU.mult),
                    reads=[('stg', s), ('aux', a), ('gam',)], writes=[('stg', s)])
                pg.dma('sp', dst[kc * 128:(kc + 1) * 128, t0:t0 + tw], cx.stg[s][:, 0:tw],
                       reads=[('stg', s)], writes=[('yT', kc, t0)])


def col_groups(segs, gw=256):
    out = []
    for (s0, w) in segs:
        c = 0
        while c < w:
            g = min(gw, w - c)
            blocks = []
            o = 0
            while o < g:
                blocks.append((o, min(128, g - o)))
                o += 128
            out.append((s0 + c, g, blocks))
            c += g
    return out


def linear_A(cx, W, K, segs, epi, xkey=('XT',), tiles=None):
    pg = cx.pg
    KC = K // 128
    KP = (KC + 15) // 16
    groups = col_groups(segs)
    tiles = tiles or token_tiles(cx.T)
    loaded = {}
    order = [(gi, kp) for gi in range(len(groups)) for kp in range(KP)]
    state = {'next': 0}

    def issue_loads(upto):
        while state['next'] < len(order) and state['next'] < upto:
            gi, kp = order[state['next']]
            c0, gw, _ = groups[gi]
            slot = cx.wi % cx.NW
            cx.wi += 1
            k0 = kp * 16
            nk = min(16, KC - k0)
            src = W[k0 * 128:(k0 + nk) * 128, c0:c0 + gw].rearrange("(kc p) c -> p kc c", p=128)
            pg.dma('pool', cx.wbuf[slot][:, 0:nk, 0:gw], src, reads=[], writes=[('w', slot)])
            loaded[(gi, kp)] = slot
            state['next'] += 1

    for gi, (c0, gw, blocks) in enumerate(groups):
        issue_loads((gi + 1) * KP + min(cx.NW - KP, KP))
        for (t0, tw) in tiles:
            for (off, cw) in blocks:
                b = cx.bank()
                for kc in range(KC):
                    slot = loaded[(gi, kc // 16)]
                    pg.op('pe', lambda e, slot=slot, kc=kc: e.matmul(
                        cx.psb[b][0:cw, 0:tw], cx.wbuf[slot][:, kc % 16, off:off + cw],
                        cx.XT[:, kc, t0:t0 + tw], start=(kc == 0), stop=(kc == KC - 1)),
                        reads=[('w', slot), xkey], writes=[('ps', b)])
                epi(c0 + off, cw, t0, tw, cx.psb[b][0:cw, 0:tw], ('ps', b))


def make_epi(cx, dst, dkey, func=AF.Identity, mul=None, mulkey=None, add=None, addkey=None,
             square=False, rowoff=0):
    pg = cx.pg
    out_bf = (dst.dtype == BF16)

    def epi(c0, cw, t0, tw, ps, pskey):
        r0 = c0 - rowoff
        simple = (mul is None and add is None and not square)
        if simple:
            if out_bf:
                o = cx.oi % cx.NS
                cx.oi += 1
                ot, okey = cx.ost[o], ('ost', o)
            else:
                o = cx.si % cx.NS
                cx.si += 1
                ot, okey = cx.stg[o], ('stg', o)
            pg.op('act', lambda e: e.activation(out=ot[0:cw, 0:tw], in_=ps, func=func),
                  reads=[pskey], writes=[okey])
            pg.dma('sp', dst[r0:r0 + cw, t0:t0 + tw], ot[0:cw, 0:tw], reads=[okey],
                   writes=[(dkey, r0, t0)])
            return
        s = cx.si % cx.NS
        cx.si += 1
        st, skey = cx.stg[s], ('stg', s)
        pg.op('act', lambda e: e.activation(out=st[0:cw, 0:tw], in_=ps, func=func),
              reads=[pskey], writes=[skey])
        steps = []
        if mul is not None:
            steps.append(('mul', mul, mulkey))
        if add is not None:
            steps.append(('add', add, addkey))
        if square:
            steps.append(('sq', None, None))
        for i, (kind, src, skey2) in enumerate(steps):
            last = (i == len(steps) - 1)
            if last and out_bf:
                o = cx.oi % cx.NS
                cx.oi += 1
                ot, okey = cx.ost[o], ('ost', o)
            else:
                ot, okey = st, skey
            if kind == 'sq':
                pg.op('dve', lambda e, ot=ot: e.tensor_tensor(out=ot[0:cw, 0:tw], in0=st[0:cw, 0:tw],
                                                              in1=st[0:cw, 0:tw], op=ALU.mult),
                      reads=[skey], writes=[okey])
            else:
                a = cx.ai % cx.NS
                cx.ai += 1
                if src.dtype == BF16:
                    at, akey = cx.auxb[a], ('auxb', a)
                else:
                    at, akey = cx.aux[a], ('aux', a)
                pg.dma('sp', at[0:cw, 0:tw], src[r0:r0 + cw, t0:t0 + tw],
                       reads=[(skey2, r0, t0)], writes=[akey])
                opx = ALU.mult if kind == 'mul' else ALU.add
                pg.op('dve', lambda e, ot=ot, at=at, opx=opx: e.tensor_tensor(
                    out=ot[0:cw, 0:tw], in0=st[0:cw, 0:tw], in1=at[0:cw, 0:tw], op=opx),
                    reads=[skey, akey], writes=[okey])
        pg.dma('sp', dst[r0:r0 + cw, t0:t0 + tw], ot[0:cw, 0:tw], reads=[okey],
               writes=[(dkey, r0, t0)])
    return epi


class Ring:
    def __init__(self, pg, es, n, shape, dt, name):
        self.t = [pg.sb(es, shape, dt, '%s%d' % (name, i)) for i in range(n)]
        self.name = name
        self.n = n
        self.i = 0

    def next(self):
        j = self.i % self.n
        self.i += 1
        return self.t[j], (self.name, j)


class PRing:
    def __init__(self, pg, es, n, shape, dt, name):
        self.t = [pg.ps(es, shape, dt, '%s%d' % (name, i)) for i in range(n)]
        self.name = name
        self.n = n
        self.i = 0

    def next(self):
        j = self.i % self.n
        self.i += 1
        return self.t[j], (self.name, j)


def load_cols(pg, es, dram, n, name, q='sp', dt=F32):
    t = pg.sb(es, [128, n], dt, name)
    pg.dma(q, t[:], dram, writes=[(name,)])
    return t


BD = 2048
RW_COLS = 6592


def rwkv_prep(pg, L, NS, urwT, shiftT, cw, lw, scr):
    T = L + NS
    with ExitStack() as es:
        mu = load_cols(pg, es, cw['mu'], 52, 'c_mu')
        w0 = load_cols(pg, es, cw['w0'], 16, 'c_w0')
        a0 = load_cols(pg, es, cw['a0'], 16, 'c_a0')
        kkc = load_cols(pg, es, cw['kk'], 16, 'c_kk')
        kac = load_cols(pg, es, cw['ka'], 16, 'c_ka')
        rkc = load_cols(pg, es, cw['rk'], 16, 'c_rk')
        nw0 = pg.sb(es, [128, 16], F32, 'c_nw0')
        omka = pg.sb(es, [128, 16], F32, 'c_omka')
        pg.op('dve', lambda e: e.tensor_scalar(out=nw0[:], in0=w0[:], scalar1=-1.0, scalar2=None, op0=ALU.mult),
              reads=[('c_w0',)], writes=[('c_nw0',)])
        pg.op('dve', lambda e: e.tensor_scalar(out=omka[:], in0=kac[:], scalar1=-1.0, scalar2=1.0,
                                               op0=ALU.mult, op1=ALU.add),
              reads=[('c_ka',)], writes=[('c_omka',)])
        cst = pg.sb(es, [128, 4], F32, 'c_cst')
        pg.op('dve', lambda e: e.memset(cst[:, 0:1], 1.0), writes=[('c_cst',)])
        pg.op('dve', lambda e: e.memset(cst[:, 1:2], -0.5), reads=[('c_cst',)], writes=[('c_cst',)])
        bones = pg.sb(es, [128, 128], F32, 'bones')
        pg.op('dve', lambda e: e.memset(bones[:], 0.0), writes=[('bones',)])
        pg.op('dve', lambda e: e.memset(bones[0:64, 0:64], 1.0), reads=[('bones',)], writes=[('bones',)])
        pg.op('dve', lambda e: e.memset(bones[64:128, 64:128], 1.0), reads=[('bones',)], writes=[('bones',)])
        w2 = pg.sb(es, [96, 2048], BF16, 'l_w2')
        a2 = pg.sb(es, [96, 2048], BF16, 'l_a2')
        g2 = pg.sb(es, [128, 2, 2048], BF16, 'l_g2')
        pg.dma('pool', w2[:], lw['w2'], writes=[('l_w2',)])
        pg.dma('pool', a2[:], lw['a2'], writes=[('l_a2',)])
        pg.dma('pool', g2[:], lw['g2'].rearrange("(c p) n -> p c n", p=128), writes=[('l_g2',)])

        inr = Ring(pg, es, 6, [128, 513], F32, 'rin')
        tmp = Ring(pg, es, 10, [128, 512], F32, 'rtm')
        lor = Ring(pg, es, 2, [128, 4, 512], BF16, 'rlo')
        pp = PRing(pg, es, 6, [128, 512], F32, 'rpp')

        def load_shift(row0, nr, t0, tw, mucol):
            it, ik = inr.next()
            if t0 >= L:
                pg.dma('sp', it[0:nr, 1:tw + 1], urwT[row0:row0 + nr, t0:t0 + tw], writes=[ik])
                pt, pk = inr.next()
                pg.dma('sp', pt[0:nr, 0:tw], shiftT[row0:row0 + nr, 0:tw], writes=[pk])
                prev, pkeys = pt[0:nr, 0:tw], [pk]
            else:
                if t0 == 0:
                    pg.op('pool', lambda e: e.memset(it[0:nr, 0:1], 0.0), writes=[ik])
                    pg.dma('sp', it[0:nr, 1:tw + 1], urwT[row0:row0 + nr, 0:tw], reads=[ik], writes=[ik])
                else:
                    pg.dma('sp', it[0:nr, 0:tw + 1], urwT[row0:row0 + nr, t0 - 1:t0 + tw], writes=[ik])
                prev, pkeys = it[0:nr, 0:tw], []
            u = it[0:nr, 1:tw + 1]
            dt_, dk = tmp.next()
            pg.op('dve', lambda e: e.tensor_tensor(out=dt_[0:nr, 0:tw], in0=prev, in1=u, op=ALU.subtract),
                  reads=[ik] + pkeys, writes=[dk])
            pg.op('dve', lambda e: e.scalar_tensor_tensor(out=dt_[0:nr, 0:tw], in0=dt_[0:nr, 0:tw], scalar=mucol,
                                                          in1=u, op0=ALU.mult, op1=ALU.add),
                  reads=[ik, dk, ('c_mu',)], writes=[dk])
            return dt_, dk

        def store(name, hp, t0, tw, tl, tk):
            pg.dma('sp', scr[name][hp * 128:(hp + 1) * 128, t0:t0 + tw], tl[:, 0:tw], reads=[tk],
                   writes=[(name, hp, t0)])

        tiles = token_tiles(L) + ([(L, NS)] if NS else [])
        for (t0, tw) in tiles:
            lt, lk = lor.next()
            x, xk_ = load_shift(6144, 96, t0, tw, mu[0:96, 48:49])
            pg.op('act', lambda e: e.activation(out=lt[0:96, 0, 0:tw], in_=x[0:96, 0:tw], func=AF.Tanh),
                  reads=[xk_], writes=[lk])
            x, xk_ = load_shift(6240, 96, t0, tw, mu[0:96, 49:50])
            pg.op('act', lambda e: e.activation(out=lt[0:96, 1, 0:tw], in_=x[0:96, 0:tw], func=AF.Identity),
                  reads=[xk_], writes=[lk])
            for j in range(2):
                x, xk_ = load_shift(6336 + j * 128, 128, t0, tw, mu[:, 50 + j:51 + j])
                pg.op('act', lambda e, j=j: e.activation(out=lt[:, 2 + j, 0:tw], in_=x[:, 0:tw], func=AF.Sigmoid),
                      reads=[xk_], writes=[lk])
            for hp in range(16):
                xr, xrk = load_shift(hp * 128, 128, t0, tw, mu[:, hp:hp + 1])
                xk, xkk = load_shift(2048 + hp * 128, 128, t0, tw, mu[:, 16 + hp:17 + hp])
                xv, xvk = load_shift(4096 + hp * 128, 128, t0, tw, mu[:, 32 + hp:33 + hp])
                store('r', hp, t0, tw, xr, xrk)
                store('v', hp, t0, tw, xv, xvk)
                cs = slice(hp * 128, (hp + 1) * 128)
                p1, p1k = pp.next()
                pg.op('pe', lambda e: e.matmul(p1[:, 0:tw], w2[:, cs], lt[0:96, 0, 0:tw], start=True, stop=True),
                      reads=[('l_w2',), lk], writes=[p1k])
                e1, e1k = tmp.next()
                pg.op('act', lambda e: e.activation(out=e1[:, 0:tw], in_=p1[:, 0:tw], func=AF.Exp,
                                                    bias=nw0[:, hp:hp + 1], scale=-1.0),
                      reads=[p1k, ('c_nw0',)], writes=[e1k])
                pg.op('act', lambda e: e.activation(out=e1[:, 0:tw], in_=e1[:, 0:tw], func=AF.Ln,
                                                    bias=cst[:, 0:1], scale=1.0),
                      reads=[e1k, ('c_cst',)], writes=[e1k])
                pg.op('act', lambda e: e.activation(out=e1[:, 0:tw], in_=e1[:, 0:tw], func=AF.Exp,
                                                    bias=cst[:, 1:2], scale=-1.0),
                      reads=[e1k, ('c_cst',)], writes=[e1k])
                if 'lw' in scr:
                    pg.op('dve', lambda e: e.tensor_scalar(out=e1[:, 0:tw], in0=e1[:, 0:tw], scalar1=-1.0, scalar2=None,
                                                           op0=ALU.mult), reads=[e1k], writes=[e1k])
                    store('lw', hp, t0, tw, e1, e1k)
                else:
                    pg.op('act', lambda e: e.activation(out=e1[:, 0:tw], in_=e1[:, 0:tw], func=AF.Exp, scale=-1.0),
                          reads=[e1k], writes=[e1k])
                    store('dec', hp, t0, tw, e1, e1k)
                p2, p2k = pp.next()
                pg.op('pe', lambda e: e.matmul(p2[:, 0:tw], a2[:, cs], lt[0:96, 1, 0:tw], start=True, stop=True),
                      reads=[('l_a2',), lk], writes=[p2k])
                lr, lrk = tmp.next()
                pg.op('act', lambda e: e.activation(out=lr[:, 0:tw], in_=p2[:, 0:tw], func=AF.Sigmoid,
                                                    bias=a0[:, hp:hp + 1], scale=1.0),
                      reads=[p2k, ('c_a0',)], writes=[lrk])
                p3, p3k = pp.next()
                for j in range(2):
                    pg.op('pe', lambda e, j=j: e.matmul(p3[:, 0:tw], g2[:, j, cs], lt[:, 2 + j, 0:tw],
                                                        start=(j == 0), stop=(j == 1)),
                          reads=[('l_g2',), lk], writes=[p3k])
                gt, gk = tmp.next()
                pg.op('act', lambda e: e.activation(out=gt[:, 0:tw], in_=p3[:, 0:tw], func=AF.Identity),
                      reads=[p3k], writes=[gk])
                store('g', hp, t0, tw, gt, gk)
                kk, kkk = tmp.next()
                pg.op('dve', lambda e: e.tensor_scalar(out=kk[:, 0:tw], in0=xk[:, 0:tw], scalar1=kkc[:, hp:hp + 1],
                                                       scalar2=None, op0=ALU.mult),
                      reads=[xkk, ('c_kk',)], writes=[kkk])
                sq, sqk = tmp.next()
                pg.op('pool', lambda e: e.tensor_tensor(out=sq[:, 0:tw], in0=kk[:, 0:tw], in1=kk[:, 0:tw], op=ALU.mult),
                      reads=[kkk], writes=[sqk])
                p4, p4k = pp.next()
                pg.op('pe', lambda e: e.matmul(p4[:, 0:tw], bones[:], sq[:, 0:tw], start=True, stop=True),
                      reads=[('bones',), sqk], writes=[p4k])
                pg.op('act', lambda e: e.activation(out=sq[:, 0:tw], in_=p4[:, 0:tw], func=AF.Sqrt),
                      reads=[p4k], writes=[sqk])
                pg.op('dve', lambda e: e.tensor_scalar(out=sq[:, 0:tw], in0=sq[:, 0:tw], scalar1=1e-12, scalar2=None,
                                                       op0=ALU.max),
                      reads=[sqk], writes=[sqk])
                pg.op('dve', lambda e: e.reciprocal(out=sq[:, 0:tw], in_=sq[:, 0:tw]), reads=[sqk], writes=[sqk])
                pg.op('dve', lambda e: e.tensor_tensor(out=kk[:, 0:tw], in0=kk[:, 0:tw], in1=sq[:, 0:tw], op=ALU.mult),
                      reads=[kkk, sqk], writes=[kkk])
                store('an', hp, t0, tw, kk, kkk)
                pg.op('dve', lambda e: e.scalar_tensor_tensor(out=sq[:, 0:tw], in0=kk[:, 0:tw], scalar=-1.0,
                                                              in1=lr[:, 0:tw], op0=ALU.mult, op1=ALU.mult),
                      reads=[kkk, lrk], writes=[sqk])
                store('bn', hp, t0, tw, sq, sqk)
                pg.op('dve', lambda e: e.tensor_scalar(out=lr[:, 0:tw], in0=lr[:, 0:tw], scalar1=kac[:, hp:hp + 1],
                                                       scalar2=omka[:, hp:hp + 1], op0=ALU.mult, op1=ALU.add),
                      reads=[lrk, ('c_ka',), ('c_omka',)], writes=[lrk])
                pg.op('dve', lambda e: e.tensor_tensor(out=lr[:, 0:tw], in0=lr[:, 0:tw], in1=xk[:, 0:tw], op=ALU.mult),
                      reads=[lrk, xkk], writes=[lrk])
                store('kh', hp, t0, tw, lr, lrk)
                rr_, rrk = tmp.next()
                pg.op('dve', lambda e: e.scalar_tensor_tensor(out=rr_[:, 0:tw], in0=xr[:, 0:tw], scalar=rkc[:, hp:hp + 1],
                                                              in1=lr[:, 0:tw], op0=ALU.mult, op1=ALU.mult),
                      reads=[xrk, lrk, ('c_rk',)], writes=[rrk])
                p5, p5k = pp.next()
                pg.op('pe', lambda e: e.matmul(p5[:, 0:tw], bones[:], rr_[:, 0:tw], start=True, stop=True),
                      reads=[('bones',), rrk], writes=[p5k])
                pg.op('dve', lambda e: e.tensor_tensor(out=rr_[:, 0:tw], in0=p5[:, 0:tw], in1=xv[:, 0:tw], op=ALU.mult),
                      reads=[p5k, xvk], writes=[rrk])
                store('bonus', hp, t0, tw, rr_, rrk)
        pg.barrier()


def rwkv_scan(pg, L, NS, scr, cst, st_in, st_out_p, st_out_s, oT):
    T = L + NS
    names = ['dec', 'an', 'bn', 'kh', 'r', 'v']
    CH = 128
    with ExitStack() as es:
        bones = pg.sb(es, [128, 128], BF16, 'sc_bones')
        istack = pg.sb(es, [128, 64], BF16, 'sc_ist')
        sel2 = pg.sb(es, [128, 2], BF16, 'sc_sel2')
        pg.dma('sp', bones[:], cst['bones_bf'], writes=[('sc_bones',)])
        pg.dma('sp', istack[:], cst['istack'], writes=[('sc_ist',)])
        pg.dma('sp', sel2[:], cst['sel2'], writes=[('sc_sel2',)])
        qr = {n: Ring(pg, es, 2, [128, 16, CH], F32, 'sq_' + n) for n in names}
        STb = [pg.sb(es, [128, 16, 64], F32, 'ST%d' % i) for i in range(2)]
        tmpr = Ring(pg, es, 2, [128, 16, 64], BF16, 'sc_tmp')
        tmp2r = Ring(pg, es, 2, [128, 16, 64], BF16, 'sc_tmp2')
        t1r = Ring(pg, es, 2, [128, 16, 64], F32, 'sc_t1')
        t2r = Ring(pg, es, 2, [128, 16, 64], F32, 'sc_t2')
        dvr = Ring(pg, es, 4, [128, 16, 64], BF16, 'sc_dv')
        osb = Ring(pg, es, 2, [64, 32, CH], F32, 'sc_osb')
        ps_sa = PRing(pg, es, 1, [128, 16, 64], F32, 'ps_sa')
        ps_v = PRing(pg, es, 2, [128, 16, 64], F32, 'ps_v')
        ps_o = PRing(pg, es, 2, [64, 32, 16], F32, 'ps_o')

        def bc(t, tt):
            return t[:, :, tt:tt + 1].to_broadcast([128, 16, 64])

        def step(ST, stk, q, qk, tt, po, pok, oi):
            dv, dvk = dvr.next()
            pg.op('pool', lambda e: e.tensor_tensor(out=dv[:], in0=istack[:].unsqueeze(1).to_broadcast([128, 16, 64]),
                                                    in1=bc(q['v'], tt), op=ALU.mult),
                  reads=[('sc_ist',), qk['v']], writes=[dvk])
            pv, pvk = ps_v.next()
            for h in range(2):
                pg.op('pe', lambda e, h=h: e.matmul(pv[:, h * 8:(h + 1) * 8, :], bones[:], dv[:, h * 8:(h + 1) * 8, :],
                                                    start=True, stop=True),
                      reads=[('sc_bones',), dvk], writes=[pvk])
            tm, tmk = tmpr.next()
            pg.op('dve', lambda e: e.tensor_tensor(out=tm[:], in0=ST[:], in1=bc(q['an'], tt), op=ALU.mult),
                  reads=[stk, qk['an']], writes=[tmk])
            psa, psak = ps_sa.next()
            for h in range(2):
                pg.op('pe', lambda e, h=h: e.matmul(psa[:, h * 8:(h + 1) * 8, :], bones[:], tm[:, h * 8:(h + 1) * 8, :],
                                                    start=True, stop=True),
                      reads=[('sc_bones',), tmk], writes=[psak])
            pg.op('dve', lambda e: e.tensor_tensor(out=ST[:], in0=ST[:], in1=bc(q['dec'], tt), op=ALU.mult),
                  reads=[stk, qk['dec']], writes=[stk])
            t2, t2k = t2r.next()
            pg.op('dve', lambda e: e.tensor_tensor(out=t2[:], in0=pv[:], in1=bc(q['kh'], tt), op=ALU.mult),
                  reads=[pvk, qk['kh']], writes=[t2k])
            pg.op('dve', lambda e: e.tensor_tensor(out=ST[:], in0=ST[:], in1=t2[:], op=ALU.add),
                  reads=[stk, t2k], writes=[stk])
            t1, t1k = t1r.next()
            pg.op('dve', lambda e: e.tensor_tensor(out=t1[:], in0=psa[:], in1=bc(q['bn'], tt), op=ALU.mult),
                  reads=[psak, qk['bn']], writes=[t1k])
            pg.op('dve', lambda e: e.tensor_tensor(out=ST[:], in0=ST[:], in1=t1[:], op=ALU.add),
                  reads=[stk, t1k], writes=[stk])
            tm2, tm2k = tmp2r.next()
            pg.op('dve', lambda e: e.tensor_tensor(out=tm2[:], in0=ST[:], in1=bc(q['r'], tt), op=ALU.mult),
                  reads=[stk, qk['r']], writes=[tm2k])
            for hp in range(16):
                pg.op('pe', lambda e, hp=hp: e.matmul(po[:, 2 * hp:2 * hp + 2, oi], tm2[:, hp, :], sel2[:],
                                                      start=True, stop=True),
                      reads=[('sc_sel2',), tm2k], writes=[pok])

        def run_tokens(c0, cn, stfn):
            q, qk = {}, {}
            for n in names:
                q[n], qk[n] = qr[n].next()
                pg.dma('sp', q[n][:, :, 0:cn], scr[n].rearrange("(hp p) t -> p hp t", p=128)[:, :, c0:c0 + cn],
                       writes=[qk[n]])
            ob, obk = osb.next()
            for s0 in range(0, cn, 16):
                sn = min(16, cn - s0)
                po, pok = ps_o.next()
                for i in range(sn):
                    ST, stk, after = stfn(c0 + s0 + i)
                    step(ST, stk, q, qk, s0 + i, po, pok, i)
                    if after:
                        after()
                pg.op('act', lambda e: e.activation(out=ob[:, :, s0:s0 + sn], in_=po[:, :, 0:sn], func=AF.Identity),
                      reads=[pok], writes=[obk])
            pg.dma('sp', oT.rearrange("(h v) t -> v h t", v=64)[:, :, c0:c0 + cn], ob[:, :, 0:cn], reads=[obk],
                   writes=[('oT', c0)])

        if L:
            pg.op('dve', lambda e: e.memset(STb[0][:], 0.0), writes=[('ST', 0)])
            for c0 in range(0, L, CH):
                run_tokens(c0, min(CH, L - c0), lambda t: (STb[0], ('ST', 0), None))
            pg.dma('sp', st_out_p.rearrange("p (a b) -> p a b", a=16), STb[0][:], reads=[('ST', 0)],
                   writes=[('st_out_p',)])
        if NS:
            def stfn(t):
                i = t - L
                b = i % 2
                pg.dma('sp', STb[b][:], st_in[i].rearrange("p (a b) -> p a b", a=16), writes=[('ST', b)])

                def after():
                    pg.dma('sp', st_out_s[i].rearrange("p (a b) -> p a b", a=16), STb[b][:], reads=[('ST', b)],
                           writes=[('st_out_s', i)])
                return STb[b], ('ST', b), after
            run_tokens(L, NS, stfn)
        pg.barrier()


def rwkv_post(pg, L, NS, oT, scr, cw, yT):
    T = L + NS
    with ExitStack() as es:
        lnw = load_cols(pg, es, cw['lnw'], 16, 'c_lnw')
        lnb = load_cols(pg, es, cw['lnb'], 16, 'c_lnb')
        bones = pg.sb(es, [128, 128], F32, 'po_bones')
        pg.op('dve', lambda e: e.memset(bones[:], 0.0), writes=[('po_bones',)])
        pg.op('dve', lambda e: e.memset(bones[0:64, 0:64], 1.0 / 64), reads=[('po_bones',)], writes=[('po_bones',)])
        pg.op('dve', lambda e: e.memset(bones[64:128, 64:128], 1.0 / 64), reads=[('po_bones',)], writes=[('po_bones',)])
        epsc = pg.sb(es, [128, 1], F32, 'po_eps')
        pg.op('dve', lambda e: e.memset(epsc[:], 64e-5), writes=[('po_eps',)])
        inr = Ring(pg, es, 6, [128, 512], F32, 'po_in')
        tmp = Ring(pg, es, 4, [128, 512], F32, 'po_tm')
        outr = Ring(pg, es, 3, [128, 512], BF16, 'po_out')
        pp = PRing(pg, es, 4, [128, 512], F32, 'po_pp')
        for (t0, tw) in token_tiles(T):
            for hp in range(16):
                rs = slice(hp * 128, (hp + 1) * 128)
                o, ok = inr.next()
                pg.dma('sp', o[:, 0:tw], oT[rs, t0:t0 + tw], writes=[ok])
                g, gk = inr.next()
                pg.dma('sp', g[:, 0:tw], scr['g'][rs, t0:t0 + tw], writes=[gk])
                bo, bok = inr.next()
                pg.dma('sp', bo[:, 0:tw], scr['bonus'][rs, t0:t0 + tw], writes=[bok])
                p1, p1k = pp.next()
                pg.op('pe', lambda e: e.matmul(p1[:, 0:tw], bones[:], o[:, 0:tw], start=True, stop=True),
                      reads=[('po_bones',), ok], writes=[p1k])
                c, ck = tmp.next()
                pg.op('dve', lambda e: e.tensor_tensor(out=c[:, 0:tw], in0=o[:, 0:tw], in1=p1[:, 0:tw], op=ALU.subtract),
                      reads=[ok, p1k], writes=[ck])
                sq, sqk = tmp.next()
                pg.op('act', lambda e: e.activation(out=sq[:, 0:tw], in_=c[:, 0:tw], func=AF.Square),
                      reads=[ck], writes=[sqk])
                p2, p2k = pp.next()
                pg.op('pe', lambda e: e.matmul(p2[:, 0:tw], bones[:], sq[:, 0:tw], start=True, stop=True),
                      reads=[('po_bones',), sqk], writes=[p2k])
                pg.op('act', lambda e: e.activation(out=sq[:, 0:tw], in_=p2[:, 0:tw], func=AF.Sqrt, bias=epsc[:, 0:1],
                                                    scale=1.0),
                      reads=[p2k, ('po_eps',)], writes=[sqk])
                pg.op('dve', lambda e: e.reciprocal(out=sq[:, 0:tw], in_=sq[:, 0:tw]), reads=[sqk], writes=[sqk])
                pg.op('dve', lambda e: e.scalar_tensor_tensor(out=c[:, 0:tw], in0=c[:, 0:tw], scalar=lnw[:, hp:hp + 1],
                                                              in1=sq[:, 0:tw], op0=ALU.mult, op1=ALU.mult),
                      reads=[ck, sqk, ('c_lnw',)], writes=[ck])
                pg.op('dve', lambda e: e.scalar_tensor_tensor(out=c[:, 0:tw], in0=c[:, 0:tw], scalar=lnb[:, hp:hp + 1],
                                                              in1=bo[:, 0:tw], op0=ALU.add, op1=ALU.add),
                      reads=[ck, bok, ('c_lnb',)], writes=[ck])
                y, yk = outr.next()
                pg.op('dve', lambda e: e.tensor_tensor(out=y[:, 0:tw], in0=c[:, 0:tw], in1=g[:, 0:tw], op=ALU.mult),
                      reads=[ck, gk], writes=[yk])
                pg.dma('sp', yT[rs, t0:t0 + tw], y[:, 0:tw], reads=[yk], writes=[('yT', hp, t0)])
        pg.barrier()


def ssd_conv(pg, L, NS, ussmT, convT, cw, xbcT, conv_out_p, conv_out_s):
    T = L + NS
    with ExitStack() as es:
        cwt = pg.sb(es, [128, 24, 4], F32, 'cv_w')
        cbt = pg.sb(es, [128, 24], F32, 'cv_b')
        pg.dma('sp', cwt[:], cw['convw'], writes=[('cv_w',)])
        pg.dma('sp', cbt[:], cw['convb'], writes=[('cv_b',)])
        inr = Ring(pg, es, 3, [128, 516], F32, 'cv_in')
        acc = Ring(pg, es, 3, [128, 512], F32, 'cv_acc')
        smp = Ring(pg, es, 2, [128, max(NS, 1), 4], F32, 'cv_smp')
        if L:
            pg.dma('sp', conv_out_p, ussmT[2048:5120, L - 3:L], writes=[('conv_out_p',)])
        if NS:
            with pg.nc.allow_non_contiguous_dma(reason="tiny conv state shuffles"):
                pg.dma('sp', conv_out_s[:, :, 0:2], convT[:, :, 1:3], writes=[('conv_out_s', 0)])
                pg.dma('sp', conv_out_s[:, :, 2:3], ussmT[2048:5120, L:L + NS].unsqueeze(2), writes=[('conv_out_s', 1)])
        for blk in range(24):
            r0 = 2048 + blk * 128
            for (t0, tw) in token_tiles(L):
                it, ik = inr.next()
                if t0 == 0:
                    pg.op('pool', lambda e: e.memset(it[:, 0:3], 0.0), writes=[ik])
                    pg.dma('sp', it[:, 3:tw + 3], ussmT[r0:r0 + 128, 0:tw], reads=[ik], writes=[ik])
                else:
                    pg.dma('sp', it[:, 0:tw + 3], ussmT[r0:r0 + 128, t0 - 3:t0 + tw], writes=[ik])
                a, ak = acc.next()
                pg.op('dve', lambda e: e.tensor_scalar(out=a[:, 0:tw], in0=it[:, 0:tw], scalar1=cwt[:, blk, 0:1],
                                                       scalar2=None, op0=ALU.mult),
                      reads=[ik, ('cv_w',)], writes=[ak])
                for j in range(1, 4):
                    pg.op('dve', lambda e, j=j: e.scalar_tensor_tensor(
                        out=a[:, 0:tw], in0=it[:, j:j + tw], scalar=cwt[:, blk, j:j + 1], in1=a[:, 0:tw],
                        op0=ALU.mult, op1=ALU.add), reads=[ik, ak, ('cv_w',)], writes=[ak])
                pg.op('act', lambda e: e.activation(out=a[:, 0:tw], in_=a[:, 0:tw], func=AF.Silu,
                                                    bias=cbt[:, blk:blk + 1], scale=1.0),
                      reads=[ak, ('cv_b',)], writes=[ak])
                pg.dma('sp', xbcT[blk * 128:(blk + 1) * 128, t0:t0 + tw], a[:, 0:tw], reads=[ak],
                       writes=[('xbcT', blk, t0)])
            if NS:
                st, sk = smp.next()
                with pg.nc.allow_non_contiguous_dma(reason="tiny conv state loads"):
                    pg.dma('sp', st[:, :, 0:3], convT[blk * 128:(blk + 1) * 128, :, :], writes=[sk])
                    pg.dma('sp', st[:, :, 3:4], ussmT[r0:r0 + 128, L:L + NS].unsqueeze(2), reads=[sk], writes=[sk])
                a, ak = acc.next()
                pg.op('dve', lambda e: e.tensor_scalar(out=a[:, 0:NS], in0=st[:, :, 0], scalar1=cwt[:, blk, 0:1],
                                                       scalar2=None, op0=ALU.mult),
                      reads=[sk, ('cv_w',)], writes=[ak])
                for j in range(1, 4):
                    pg.op('dve', lambda e, j=j: e.scalar_tensor_tensor(
                        out=a[:, 0:NS], in0=st[:, :, j], scalar=cwt[:, blk, j:j + 1], in1=a[:, 0:NS],
                        op0=ALU.mult, op1=ALU.add), reads=[sk, ak, ('cv_w',)], writes=[ak])
                pg.op('act', lambda e: e.activation(out=a[:, 0:NS], in_=a[:, 0:NS], func=AF.Silu,
                                                    bias=cbt[:, blk:blk + 1], scale=1.0),
                      reads=[ak, ('cv_b',)], writes=[ak])
                pg.dma('sp', xbcT[blk * 128:(blk + 1) * 128, L:L + NS], a[:, 0:NS], reads=[ak],
                       writes=[('xbcT', blk, L)])
        pg.barrier()


def ssd_main(pg, L, NS, ussmT, xbcT, cw, cst, st_in, st_out_p, st_out_s, yrawT):
    T = L + NS
    CK = 64
    with ExitStack() as es:
        ident = pg.sb(es, [128, 128], F32, 'sd_id')
        selh = pg.sb(es, [32, 32, 64], F32, 'sd_selh')
        negm = pg.sb(es, [64, 64], F32, 'sd_negm')
        sell = pg.sb(es, [64, 128], F32, 'sd_sell')
        pg.dma('sp', ident[:], cst['ident'], writes=[('sd_id',)])
        pg.dma('sp', selh[:], cst['selh'], writes=[('sd_selh',)])
        pg.dma('sp', negm[:], cst['negmask'], writes=[('sd_negm',)])
        pg.dma('sp', sell[:], cst['sellast'], writes=[('sd_sell',)])
        dtb = pg.sb(es, [32, 1], F32, 'sd_dtb')
        alog = pg.sb(es, [32, 1], F32, 'sd_alog')
        pg.dma('sp', dtb[:], cw['dtb'], writes=[('sd_dtb',)])
        pg.dma('sp', alog[:], cw['alog'], writes=[('sd_alog',)])
        one = pg.sb(es, [128, 1], F32, 'sd_one')
        pg.op('dve', lambda e: e.memset(one[:], 1.0), writes=[('sd_one',)])
        onerow = pg.sb(es, [1, 128], F32, 'sd_onerow')
        pg.op('dve', lambda e: e.memset(onerow[:], 1.0), writes=[('sd_sell',)])
        dtf = pg.sb(es, [32, T], F32, 'sd_dt')
        cumA = pg.sb(es, [32, T], F32, 'sd_cumA')
        cumB = pg.sb(es, [32, T], F32, 'sd_cumB')
        dtdte = pg.sb(es, [32, T], F32, 'sd_dtdte')
        ecum = pg.sb(es, [32, T], F32, 'sd_ecum')
        ncum = pg.sb(es, [32, T], F32, 'sd_ncum')
        K = ('sd_small',)
        pg.dma('sp', dtf[:], ussmT[5120:5152, :], writes=[K])
        pg.op('act', lambda e: e.activation(out=dtf[:], in_=dtf[:], func=AF.Exp, bias=dtb[:, 0:1], scale=1.0),
              reads=[K, ('sd_dtb',)], writes=[K])
        pg.op('act', lambda e: e.activation(out=dtf[:], in_=dtf[:], func=AF.Ln, bias=one[0:32, 0:1], scale=1.0),
              reads=[K, ('sd_one',)], writes=[K])
        pg.op('act', lambda e: e.activation(out=alog[:], in_=alog[:], func=AF.Exp), reads=[('sd_alog',)],
              writes=[('sd_alog',)])
        pg.op('dve', lambda e: e.tensor_scalar(out=cumA[:], in0=dtf[:], scalar1=alog[:, 0:1], scalar2=-1.0,
                                               op0=ALU.mult, op1=ALU.mult),
              reads=[K, ('sd_alog',)], writes=[K])
        src, dst = cumA, cumB
        nch = L // CK
        if L:
            sh = 1
            while sh < CK:
                sv = src[:, 0:L].rearrange("p (c s) -> p c s", s=CK)
                dv = dst[:, 0:L].rearrange("p (c s) -> p c s", s=CK)
                pg.op('dve', lambda e, sv=sv, dv=dv, sh=sh: e.tensor_copy(out=dv[:, :, 0:sh], in_=sv[:, :, 0:sh]),
                      reads=[K], writes=[K])
                pg.op('dve', lambda e, sv=sv, dv=dv, sh=sh: e.tensor_tensor(out=dv[:, :, sh:CK], in0=sv[:, :, sh:CK],
                                                                          in1=sv[:, :, 0:CK - sh], op=ALU.add),
                      reads=[K], writes=[K])
                if NS:
                    pg.op('dve', lambda e, src=src, dst=dst: e.tensor_copy(out=dst[:, L:T], in_=src[:, L:T]),
                          reads=[K], writes=[K])
                src, dst = dst, src
                sh *= 2
        cum = src
        pg.op('act', lambda e: e.activation(out=ecum[:], in_=cum[:], func=AF.Exp), reads=[K], writes=[K])
        pg.op('dve', lambda e: e.tensor_scalar(out=ncum[:], in0=cum[:], scalar1=-1.0, scalar2=None, op0=ALU.mult),
              reads=[K], writes=[K])
        for c in range(nch):
            cs = slice(c * CK, (c + 1) * CK)
            pg.op('act', lambda e, cs=cs, c=c: e.activation(out=dtdte[:, cs], in_=cum[:, cs], func=AF.Exp,
                                                          bias=cum[:, (c + 1) * CK - 1:(c + 1) * CK], scale=-1.0),
                  reads=[K], writes=[K])
        if NS:
            pg.op('dve', lambda e: e.memset(dtdte[:, L:T], 1.0), reads=[K], writes=[K])
        pg.op('dve', lambda e: e.tensor_tensor(out=dtdte[:], in0=dtdte[:], in1=dtf[:], op=ALU.mult),
              reads=[K], writes=[K])

        BIG = 256
        xf_r = Ring(pg, es, 2, [128, 16, BIG], F32, 'sd_xf')
        bc_r = Ring(pg, es, 2, [128, 8, BIG], F32, 'sd_bcf')
        bcb_r = Ring(pg, es, 2, [128, 8, BIG], BF16, 'sd_bcb')
        yfm_r = Ring(pg, es, 2, [128, 16, BIG], F32, 'sd_yfm')
        pb = PRing(pg, es, 8, [128, 512], F32, 'sd_pb')
        smT = Ring(pg, es, 2, [64, 4, 32], F32, 'sd_smT')
        BT_r = Ring(pg, es, 2, [64, 512], BF16, 'sd_BT')
        xdt_r = Ring(pg, es, 4, [64, 8, 64], BF16, 'sd_xdt')
        xdte_r = Ring(pg, es, 4, [64, 8, 64], BF16, 'sd_xdte')
        L_r = Ring(pg, es, 2, [64, 8, 64], F32, 'sd_L')
        M_r = Ring(pg, es, 4, [64, 8, 64], BF16, 'sd_M')
        yo_r = Ring(pg, es, 2, [64, 8, 64], F32, 'sd_yo')
        yt_r = Ring(pg, es, 4, [64, 512], F32, 'sd_yt')
        cd_r = Ring(pg, es, 2, [128, 32], F32, 'sd_cd')
        cb_r = Ring(pg, es, 2, [64, 256], F32, 'sd_cb')
        hT = [pg.sb(es, [128, 32, 64], F32, 'sd_h%d' % i) for i in range(2)]
        hbf = [pg.sb(es, [128, 32, 64], BF16, 'sd_hb%d' % i) for i in range(2)]

        def chunk(c0, C, off, xf, xfk, bcb, bcbk, bcf, bcfk, yfm, yfmk, hb):
            H, Hk, HB, HBk = hT[hb], ('sd_h', hb), hbf[hb], ('sd_hb', hb)
            sm, smk = smT.next()
            pt, ptk = pb.next()
            for j, arr in enumerate((dtf, dtdte, ncum, ecum)):
                pg.op('pe', lambda e, j=j, arr=arr: e.transpose(pt[0:C, j * 32:(j + 1) * 32], arr[:, c0:c0 + C],
                                                              ident[0:32, 0:32]),
                      reads=[K, ('sd_id',)], writes=[ptk])
            pg.op('act', lambda e: e.activation(out=sm[0:C, :, :], in_=pt[0:C, 0:128].rearrange("p (a b) -> p a b", a=4),
                                                func=AF.Identity), reads=[ptk], writes=[smk])
            pB, pBk = pb.next()
            for g in range(4):
                pg.op('pe', lambda e, g=g: e.transpose(pB[0:C, g * 128:(g + 1) * 128], bcf[:, g, off:off + C], ident[:]),
                      reads=[bcfk, ('sd_id',)], writes=[pBk])
            BT, BTk = BT_r.next()
            pg.op('act', lambda e: e.activation(out=BT[0:C, :], in_=pB[0:C, :], func=AF.Identity),
                  reads=[pBk], writes=[BTk])
            pcb, pcbk = pb.next()
            for g in range(4):
                pg.op('pe', lambda e, g=g: e.matmul(pcb[0:C, g * 64:g * 64 + C], bcb[:, g, off:off + C],
                                                    bcb[:, 4 + g, off:off + C], start=True, stop=True),
                      reads=[bcbk], writes=[pcbk])
            cbs, cbsk = cb_r.next()
            pg.op('act', lambda e: e.activation(out=cbs[0:C, :], in_=pcb[0:C, 0:256], func=AF.Identity),
                  reads=[pcbk], writes=[cbsk])
            pcd, pcdk = pb.next()
            sl = sell[0:64, :] if C == 64 else onerow[0:1, :]
            pg.op('pe', lambda e: e.matmul(pcd[:, 0:32], sl, sm[0:C, 3, :], start=True, stop=True),
                  reads=[('sd_sell',), smk], writes=[pcdk])
            cd, cdk = cd_r.next()
            pg.op('act', lambda e: e.activation(out=cd[:], in_=pcd[:, 0:32], func=AF.Identity), reads=[pcdk], writes=[cdk])
            def grp(g):
                hs = slice(g * 8, (g + 1) * 8)
                px, pxk = pb.next()
                for b4 in range(4):
                    blk = g * 4 + b4
                    pg.op('pe', lambda e, b4=b4, blk=blk: e.transpose(px[0:C, b4 * 128:(b4 + 1) * 128],
                                                                    xf[:, blk, off:off + C], ident[:]),
                          reads=[xfk, ('sd_id',)], writes=[pxk])
                xdt, xdtk = xdt_r.next()
                xdte, xdtek = xdte_r.next()
                pxv = px[0:C, :].rearrange("p (h q) -> p h q", h=8)
                pg.op('dve', lambda e: e.tensor_tensor(out=xdt[0:C], in0=pxv,
                                                       in1=sm[0:C, 0, hs].unsqueeze(2).to_broadcast([C, 8, 64]), op=ALU.mult),
                      reads=[pxk, smk], writes=[xdtk])
                pg.op('dve', lambda e: e.tensor_tensor(out=xdte[0:C], in0=pxv,
                                                       in1=sm[0:C, 1, hs].unsqueeze(2).to_broadcast([C, 8, 64]), op=ALU.mult),
                      reads=[pxk, smk], writes=[xdtek])
                pL, pLk = pb.next()
                Lt, Ltk = L_r.next()
                for hl in range(8):
                    h = g * 8 + hl
                    pg.op('pe', lambda e, h=h, hl=hl: e.matmul(pL[0:C, hl * 64:hl * 64 + C], selh[:, h, 0:C],
                                                               cum[:, c0:c0 + C], start=True, stop=False),
                          reads=[('sd_selh',), K], writes=[pLk])
                    pg.op('pe', lambda e, hl=hl: e.matmul(pL[0:C, hl * 64:hl * 64 + C], ident[0:C, 0:C], negm[0:C, 0:C],
                                                          start=False, stop=True),
                          reads=[('sd_id',), ('sd_negm',)], writes=[pLk])
                    pg.op('act', lambda e, h=h, hl=hl: e.activation(out=Lt[0:C, hl, 0:C], in_=pL[0:C, hl * 64:hl * 64 + C],
                                                                    func=AF.Exp, bias=sm[0:C, 2, h:h + 1], scale=1.0),
                          reads=[pLk, smk], writes=[Ltk])
                M, Mk = M_r.next()
                pg.op('dve', lambda e: e.tensor_tensor(
                    out=M[0:C, :, 0:C], in0=Lt[0:C, :, 0:C],
                    in1=cbs[0:C, g * 64:g * 64 + C].unsqueeze(1).to_broadcast([C, 8, C]), op=ALU.mult),
                    reads=[Ltk, cbsk], writes=[Mk])
                yield
                py, pyk = pb.next()
                for hl in range(8):
                    pg.op('pe', lambda e, hl=hl: e.matmul(py[0:C, hl * 64:(hl + 1) * 64], M[0:C, hl, 0:C], xdt[0:C, hl, :],
                                                          start=True, stop=True),
                          reads=[Mk, xdtk], writes=[pyk])
                po, pok = pb.next()
                pg.op('pe', lambda e: e.matmul(po[0:C, :], bcb[:, 4 + g, off:off + C],
                                               HB[:, hs, :].rearrange("p h q -> p (h q)"), start=True, stop=True),
                      reads=[bcbk, HBk], writes=[pok])
                yo, yok = yo_r.next()
                pg.op('dve', lambda e: e.tensor_tensor(out=yo[0:C], in0=po[0:C, :].rearrange("p (h q) -> p h q", h=8),
                                                       in1=sm[0:C, 3, hs].unsqueeze(2).to_broadcast([C, 8, 64]), op=ALU.mult),
                      reads=[pok, smk], writes=[yok])
                yt, ytk = yt_r.next()
                pg.op('dve', lambda e: e.tensor_tensor(out=yt[0:C, :], in0=py[0:C, :],
                                                       in1=yo[0:C].rearrange("p h q -> p (h q)"), op=ALU.add),
                      reads=[pyk, yok], writes=[ytk])
                yield
                pyT, pyTk = pb.next()
                for b4 in range(4):
                    pg.op('pe', lambda e, b4=b4: e.transpose(pyT[:, b4 * 64:b4 * 64 + C], yt[0:C, b4 * 128:(b4 + 1) * 128],
                                                            ident[0:C, 0:C]),
                          reads=[ytk, ('sd_id',)], writes=[pyTk])
                pg.op('act', lambda e: e.activation(
                    out=yfm[:, g * 4:(g + 1) * 4, off:off + C],
                    in_=pyT[:, 0:256].rearrange("p (a b) -> p a b", a=4)[:, :, 0:C], func=AF.Identity),
                    reads=[pyTk], writes=[yfmk])
                pcs, pcsk = pb.next()
                pg.op('pe', lambda e: e.matmul(pcs[:, :], BT[0:C, g * 128:(g + 1) * 128],
                                               xdte[0:C].rearrange("p h q -> p (h q)"), start=True, stop=True),
                      reads=[BTk, xdtek], writes=[pcsk])
                pg.op('dve', lambda e: e.tensor_tensor(out=H[:, hs, :], in0=H[:, hs, :],
                                                       in1=cd[:, hs].unsqueeze(2).to_broadcast([128, 8, 64]), op=ALU.mult),
                      reads=[Hk, cdk], writes=[Hk])
                pg.op('dve', lambda e: e.tensor_tensor(out=H[:, hs, :], in0=H[:, hs, :],
                                                       in1=pcs[:, :].rearrange("p (h q) -> p h q", h=8), op=ALU.add),
                      reads=[Hk, pcsk], writes=[Hk])
            live = [grp(g) for g in range(4)]
            while live:
                for gg in list(live):
                    try:
                        next(gg)
                    except StopIteration:
                        live.remove(gg)
            pg.op('act', lambda e: e.activation(out=HB[:], in_=H[:], func=AF.Identity), reads=[Hk], writes=[HBk])

        def load_big(b0, bn):
            xf, xfk = xf_r.next()
            pg.dma('sp', xf[:, :, 0:bn], xbcT[0:2048, b0:b0 + bn].rearrange("(c p) t -> p c t", p=128), writes=[xfk])
            bcf, bcfk = bc_r.next()
            pg.dma('sp', bcf[:, :, 0:bn], xbcT[2048:3072, b0:b0 + bn].rearrange("(c p) t -> p c t", p=128), writes=[bcfk])
            bcb, bcbk = bcb_r.next()
            pg.op('pool', lambda e: e.tensor_copy(out=bcb[:, :, 0:bn], in_=bcf[:, :, 0:bn]), reads=[bcfk], writes=[bcbk])
            yfm, yfmk = yfm_r.next()
            return xf, xfk, bcb, bcbk, bcf, bcfk, yfm, yfmk

        def store_big(b0, bn, yfm, yfmk):
            pg.dma('sp', yrawT[:, b0:b0 + bn].rearrange("(c p) t -> p c t", p=128), yfm[:, :, 0:bn], reads=[yfmk],
                   writes=[('yrawT', b0)])

        if L:
            pg.op('dve', lambda e: e.memset(hT[0][:], 0.0), writes=[('sd_h', 0)])
            pg.op('pool', lambda e: e.memset(hbf[0][:], 0.0), writes=[('sd_hb', 0)])
            for b0 in range(0, L, BIG):
                bn = min(BIG, L - b0)
                bufs = load_big(b0, bn)
                for off in range(0, bn, CK):
                    chunk(b0 + off, CK, off, *bufs, 0)
                store_big(b0, bn, bufs[6], bufs[7])
            pg.dma('sp', st_out_p.rearrange("p (h q) -> p h q", h=32), hT[0][:], reads=[('sd_h', 0)], writes=[('ssm_out_p',)])
        if NS:
            bufs = load_big(L, NS)
            for i in range(NS):
                b = i % 2
                pg.dma('sp', hT[b][:], st_in[i].rearrange("p (h q) -> p h q", h=32), writes=[('sd_h', b)])
                pg.op('act', lambda e, b=b: e.activation(out=hbf[b][:], in_=hT[b][:], func=AF.Identity),
                      reads=[('sd_h', b)], writes=[('sd_hb', b)])
                chunk(L + i, 1, i, *bufs, b)
                pg.dma('sp', st_out_s[i].rearrange("p (h q) -> p h q", h=32), hT[b][:], reads=[('sd_h', b)],
                       writes=[('ssm_out_s', i)])
            store_big(L, NS, bufs[6], bufs[7])
        pg.barrier()


def ssd_post(pg, L, NS, yrawT, xbcT, ussmT, cw, yT):
    T = L + NS
    with ExitStack() as es:
        dcol = load_cols(pg, es, cw['dcol'], 16, 'sp_d')
        nw = load_cols(pg, es, cw['ssm_norm'], 16, 'sp_nw')
        ones = pg.sb(es, [128, 128], F32, 'sp_ones')
        pg.op('dve', lambda e: e.memset(ones[:], 1.0), writes=[('sp_ones',)])
        epsc = pg.sb(es, [128, 1], F32, 'sp_eps')
        pg.op('dve', lambda e: e.memset(epsc[:], 1e-6), writes=[('sp_eps',)])
        inr = Ring(pg, es, 6, [128, 512], F32, 'sp_in')
        yz_r = Ring(pg, es, 8, [128, 512], F32, 'sp_yz')
        tmp = Ring(pg, es, 4, [128, 512], F32, 'sp_tm')
        outr = Ring(pg, es, 3, [128, 512], BF16, 'sp_out')
        pp = PRing(pg, es, 2, [128, 512], F32, 'sp_pp')
        for (t0, tw) in token_tiles(T):
            for g in range(4):
                p, pk = pp.next()
                yzs = []
                for b4 in range(4):
                    blk = g * 4 + b4
                    rs = slice(blk * 128, (blk + 1) * 128)
                    yr, yrk = inr.next()
                    pg.dma('sp', yr[:, 0:tw], yrawT[rs, t0:t0 + tw], writes=[yrk])
                    xs, xsk = inr.next()
                    pg.dma('sp', xs[:, 0:tw], xbcT[rs, t0:t0 + tw], writes=[xsk])
                    z, zk = inr.next()
                    pg.dma('sp', z[:, 0:tw], ussmT[rs, t0:t0 + tw], writes=[zk])
                    pg.op('dve', lambda e, blk=blk, xs=xs, yr=yr: e.scalar_tensor_tensor(
                        out=yr[:, 0:tw], in0=xs[:, 0:tw], scalar=dcol[:, blk:blk + 1], in1=yr[:, 0:tw],
                        op0=ALU.mult, op1=ALU.add), reads=[xsk, yrk, ('sp_d',)], writes=[yrk])
                    pg.op('act', lambda e, z=z: e.activation(out=z[:, 0:tw], in_=z[:, 0:tw], func=AF.Silu),
                          reads=[zk], writes=[zk])
                    yz, yzk = yz_r.next()
                    pg.op('dve', lambda e, yz=yz, yr=yr, z=z: e.tensor_tensor(out=yz[:, 0:tw], in0=yr[:, 0:tw],
                                                                              in1=z[:, 0:tw], op=ALU.mult),
                          reads=[yrk, zk], writes=[yzk])
                    sq, sqk = tmp.next()
                    pg.op('act', lambda e, sq=sq, yz=yz: e.activation(out=sq[:, 0:tw], in_=yz[:, 0:tw], func=AF.Square),
                          reads=[yzk], writes=[sqk])
                    pg.op('pe', lambda e, sq=sq, b4=b4: e.matmul(p[:, 0:tw], ones[:], sq[:, 0:tw], start=(b4 == 0),
                                                                 stop=(b4 == 3)),
                          reads=[('sp_ones',), sqk], writes=[pk])
                    yzs.append((yz, yzk, blk))
                rs_, rsk = tmp.next()
                pg.op('act', lambda e: e.activation(out=rs_[:, 0:tw], in_=p[:, 0:tw], func=AF.Sqrt, bias=epsc[:, 0:1],
                                                    scale=1.0 / 512), reads=[pk, ('sp_eps',)], writes=[rsk])
                pg.op('dve', lambda e: e.reciprocal(out=rs_[:, 0:tw], in_=rs_[:, 0:tw]), reads=[rsk], writes=[rsk])
                for (yz, yzk, blk) in yzs:
                    o, ok = outr.next()
                    pg.op('dve', lambda e, yz=yz, blk=blk, o=o: e.scalar_tensor_tensor(
                        out=o[:, 0:tw], in0=yz[:, 0:tw], scalar=nw[:, blk:blk + 1], in1=rs_[:, 0:tw],
                        op0=ALU.mult, op1=ALU.mult), reads=[yzk, rsk, ('sp_nw',)], writes=[ok])
                    pg.dma('sp', yT[blk * 128:(blk + 1) * 128, t0:t0 + tw], o[:, 0:tw], reads=[ok],
                           writes=[('yssmT', blk, t0)])
        pg.barrier()


def gla_prep(pg, L, NS, uglaT, cw, lw, scr):
    T = L + NS
    CK = 64
    nch = L // CK
    with ExitStack() as es:
        fb = load_cols(pg, es, cw['gla_fb'], 8, 'gp_fb')
        nfb = pg.sb(es, [128, 8], F32, 'gp_nfb')
        pg.op('dve', lambda e: e.tensor_scalar(out=nfb[:], in0=fb[:], scalar1=-1.0, scalar2=None, op0=ALU.mult),
              reads=[('gp_fb',)], writes=[('gp_nfb',)])
        one = pg.sb(es, [128, 1], F32, 'gp_one')
        pg.op('dve', lambda e: e.memset(one[:], 1.0), writes=[('gp_one',)])
        fup = pg.sb(es, [16, 1024], BF16, 'gp_fup')
        pg.dma('pool', fup[:], lw['gla_fup'], writes=[('gp_fup',)])
        flr = Ring(pg, es, 2, [16, 512], BF16, 'gp_fl')
        inr = Ring(pg, es, 4, [128, 512], F32, 'gp_in')
        tmp = Ring(pg, es, 8, [128, 512], F32, 'gp_tm')
        outr = Ring(pg, es, 4, [128, 512], BF16, 'gp_out')
        cdr = Ring(pg, es, 2, [128, 16], F32, 'gp_cd')
        pp = PRing(pg, es, 2, [128, 512], F32, 'gp_pp')
        tiles = token_tiles(L) + ([(L, NS)] if NS else [])
        for (t0, tw) in tiles:
            samp = (t0 >= L)
            ck = 1 if samp else CK
            nc_ = tw // ck
            fl, flk = flr.next()
            pg.dma('pool', fl[:, 0:tw], uglaT[6144:6160, t0:t0 + tw], writes=[flk])
            for j in range(8):
                p, pk = pp.next()
                pg.op('pe', lambda e: e.matmul(p[:, 0:tw], fup[:, j * 128:(j + 1) * 128], fl[:, 0:tw], start=True, stop=True),
                      reads=[('gp_fup',), flk], writes=[pk])
                a, ak = tmp.next()
                pg.op('act', lambda e: e.activation(out=a[:, 0:tw], in_=p[:, 0:tw], func=AF.Exp, bias=nfb[:, j:j + 1],
                                                    scale=-1.0), reads=[pk, ('gp_nfb',)], writes=[ak])
                pg.op('act', lambda e: e.activation(out=a[:, 0:tw], in_=a[:, 0:tw], func=AF.Ln, bias=one[:, 0:1], scale=1.0),
                      reads=[ak, ('gp_one',)], writes=[ak])
                pg.op('dve', lambda e: e.tensor_scalar(out=a[:, 0:tw], in0=a[:, 0:tw], scalar1=-1.0 / 16, scalar2=None,
                                                       op0=ALU.mult), reads=[ak], writes=[ak])
                b, bk = a, ak
                if not samp:
                    b2, b2k = tmp.next()
                    src, srck, dst, dstk = a, ak, b2, b2k
                    sh = 1
                    while sh < CK:
                        sv = src[:, 0:tw].rearrange("p (c s) -> p c s", s=CK)
                        dv = dst[:, 0:tw].rearrange("p (c s) -> p c s", s=CK)
                        pg.op('dve', lambda e, sv=sv, dv=dv, sh=sh: e.tensor_copy(out=dv[:, :, 0:sh], in_=sv[:, :, 0:sh]),
                              reads=[srck], writes=[dstk])
                        pg.op('dve', lambda e, sv=sv, dv=dv, sh=sh: e.tensor_tensor(
                            out=dv[:, :, sh:CK], in0=sv[:, :, sh:CK], in1=sv[:, :, 0:CK - sh], op=ALU.add),
                            reads=[srck, dstk], writes=[dstk])
                        src, srck, dst, dstk = dst, dstk, src, srck
                        sh *= 2
                    b, bk = src, srck
                eb, ebk = tmp.next()
                pg.op('act', lambda e: e.activation(out=eb[:, 0:tw], in_=b[:, 0:tw], func=AF.Exp), reads=[bk], writes=[ebk])
                enb, enbk = tmp.next()
                pg.op('act', lambda e: e.activation(out=enb[:, 0:tw], in_=b[:, 0:tw], func=AF.Exp, scale=-1.0),
                      reads=[bk], writes=[enbk])
                ee, eek = tmp.next()
                if samp:
                    pg.op('dve', lambda e: e.memset(ee[:, 0:tw], 1.0), writes=[eek])
                else:
                    for c in range(nc_):
                        pg.op('act', lambda e, c=c: e.activation(out=ee[:, c * CK:(c + 1) * CK], in_=b[:, c * CK:(c + 1) * CK],
                                                               func=AF.Exp, bias=b[:, (c + 1) * CK - 1:(c + 1) * CK], scale=-1.0),
                              reads=[bk], writes=[eek])
                cd, cdk = cdr.next()
                if samp:
                    pg.op('dve', lambda e: e.tensor_copy(out=cd[:, 0:nc_], in_=eb[:, 0:tw]), reads=[ebk], writes=[cdk])
                    cc0 = nch
                else:
                    pg.op('dve', lambda e: e.tensor_copy(
                        out=cd[:, 0:nc_], in_=eb[:, 0:tw].rearrange("p (c s) -> p c s", s=CK)[:, :, CK - 1]),
                        reads=[ebk], writes=[cdk])
                    cc0 = t0 // CK
                with pg.nc.allow_non_contiguous_dma(reason="small cdec store"):
                    pg.dma('sp', scr['cdec'][j * 128:(j + 1) * 128, cc0:cc0 + nc_], cd[:, 0:nc_], reads=[cdk],
                           writes=[('cdec', j, t0)])
                q, qk = inr.next()
                pg.dma('sp', q[:, 0:tw], uglaT[j * 128:(j + 1) * 128, t0:t0 + tw], writes=[qk])
                k, kk = inr.next()
                pg.dma('sp', k[:, 0:tw], uglaT[1024 + j * 128:1024 + (j + 1) * 128, t0:t0 + tw], writes=[kk])
                o1, o1k = outr.next()
                pg.op('dve', lambda e: e.scalar_tensor_tensor(out=o1[:, 0:tw], in0=q[:, 0:tw], scalar=256.0 ** -0.5,
                                                              in1=eb[:, 0:tw], op0=ALU.mult, op1=ALU.mult),
                      reads=[qk, ebk], writes=[o1k])
                pg.dma('sp', scr['qin'][j * 128:(j + 1) * 128, t0:t0 + tw], o1[:, 0:tw], reads=[o1k], writes=[('qin', j, t0)])
                o2, o2k = outr.next()
                pg.op('dve', lambda e: e.tensor_tensor(out=o2[:, 0:tw], in0=k[:, 0:tw], in1=enb[:, 0:tw], op=ALU.mult),
                      reads=[kk, enbk], writes=[o2k])
                pg.dma('sp', scr['kin'][j * 128:(j + 1) * 128, t0:t0 + tw], o2[:, 0:tw], reads=[o2k], writes=[('kin', j, t0)])
                o3, o3k = outr.next()
                pg.op('dve', lambda e: e.tensor_tensor(out=o3[:, 0:tw], in0=k[:, 0:tw], in1=ee[:, 0:tw], op=ALU.mult),
                      reads=[kk, eek], writes=[o3k])
                pg.dma('sp', scr['kend'][j * 128:(j + 1) * 128, t0:t0 + tw], o3[:, 0:tw], reads=[o3k], writes=[('kend', j, t0)])
        pg.barrier()


def gla_main(pg, L, NS, uglaT, scr, cw, cst, st_in, st_out_p, st_out_s, yrawT):
    T = L + NS
    CK = 64
    nch = L // CK
    BIG = 256
    with ExitStack() as es:
        identb = pg.sb(es, [128, 128], BF16, 'gm_idb')
        ident = pg.sb(es, [128, 128], F32, 'gm_id')
        maskT = pg.sb(es, [64, 64], F32, 'gm_mask')
        nwb = pg.sb(es, [64, 512], F32, 'gm_nwb')
        pg.dma('sp', identb[:], cst['identb'], writes=[('gm_idb',)])
        pg.dma('sp', ident[:], cst['ident'], writes=[('gm_id',)])
        pg.dma('sp', maskT[:], cst['maskT'], writes=[('gm_mask',)])
        pg.dma('sp', nwb[:], cw['gla_nwb'], writes=[('gm_nwb',)])
        epsc = pg.sb(es, [64, 1], F32, 'gm_eps')
        pg.op('dve', lambda e: e.memset(epsc[:], 1e-6), writes=[('gm_eps',)])
        cdec = pg.sb(es, [128, 8, nch + NS], F32, 'gm_cdec')
        with pg.nc.allow_non_contiguous_dma(reason="small cdec load"):
            pg.dma('sp', cdec[:], scr['cdec'].rearrange("(j p) c -> p j c", p=128), writes=[('gm_cdec',)])
        qk_r = {n: Ring(pg, es, 2, [128, 8, BIG], BF16, 'gm_' + n) for n in ('qin', 'kin', 'kend')}
        vf_r = Ring(pg, es, 2, [128, 16, BIG], F32, 'gm_vf')
        vb_r = Ring(pg, es, 2, [128, 16, BIG], BF16, 'gm_vb')
        yfm_r = Ring(pg, es, 2, [128, 16, BIG], F32, 'gm_yfm')
        S = [pg.sb(es, [128, 8, 512], F32, 'gm_S%d' % i) for i in range(2)]
        Sb = [pg.sb(es, [128, 8, 512], BF16, 'gm_Sb%d' % i) for i in range(2)]
        pb = PRing(pg, es, 6, [128, 512], F32, 'gm_pb')
        pbf = PRing(pg, es, 2, [128, 1024], BF16, 'gm_pbf')
        keT_r = Ring(pg, es, 2, [64, 1024], BF16, 'gm_keT')
        vt_r = Ring(pg, es, 3, [64, 512], BF16, 'gm_vt')
        at_r = Ring(pg, es, 2, [64, 4, 64], BF16, 'gm_at')
        sq_r = Ring(pg, es, 2, [64, 512], F32, 'gm_sq')
        ss_r = Ring(pg, es, 4, [64, 1], F32, 'gm_ss')
        on_r = Ring(pg, es, 2, [64, 512], F32, 'gm_on')

        def chunk(ci, C, off, bufs, sb_):
            qin, qink, kin, kink, kend, kendk, vb, vbk, yfm, yfmk = bufs
            St, Stk, Sbt, Sbtk = S[sb_], ('gm_S', sb_), Sb[sb_], ('gm_Sb', sb_)
            pk_, pkk = pbf.next()
            for j in range(8):
                pg.op('pe', lambda e, j=j: e.transpose(pk_[0:C, j * 128:(j + 1) * 128], kend[:, j, off:off + C], identb[:]),
                      reads=[kendk, ('gm_idb',)], writes=[pkk])
            keT, keTk = keT_r.next()
            pg.op('act', lambda e: e.activation(out=keT[0:C, :], in_=pk_[0:C, :], func=AF.Identity), reads=[pkk], writes=[keTk])
            pa, pak = pb.next()
            for h in range(4):
                for jj in range(2):
                    j = h * 2 + jj
                    pg.op('pe', lambda e, h=h, j=j, jj=jj: e.matmul(pa[0:C, h * 64:h * 64 + C], kin[:, j, off:off + C],
                                                                    qin[:, j, off:off + C], start=(jj == 0), stop=(jj == 1)),
                          reads=[kink, qink], writes=[pak])
            at, atk = at_r.next()
            pg.op('dve', lambda e: e.tensor_tensor(
                out=at[0:C, :, 0:C], in0=pa[0:C, 0:256].rearrange("p (h t) -> p h t", h=4)[:, :, 0:C],
                in1=maskT[0:C, 0:C].unsqueeze(1).to_broadcast([C, 4, C]), op=ALU.mult),
                reads=[pak, ('gm_mask',)], writes=[atk])
            for h in range(4):
                pv, pvk = pbf.next()
                for b4 in range(4):
                    pg.op('pe', lambda e, b4=b4: e.transpose(pv[0:C, b4 * 128:(b4 + 1) * 128],
                                                            vb[:, h * 4 + b4, off:off + C], identb[:]),
                          reads=[vbk, ('gm_idb',)], writes=[pvk])
                vt, vtk = vt_r.next()
                pg.op('act', lambda e: e.activation(out=vt[0:C, :], in_=pv[0:C, 0:512], func=AF.Identity),
                      reads=[pvk], writes=[vtk])
                po, pok = pb.next()
                pg.op('pe', lambda e: e.matmul(po[0:C, :], at[0:C, h, 0:C], vt[0:C, :], start=True, stop=False),
                      reads=[atk, vtk], writes=[pok])
                for jj in range(2):
                    j = h * 2 + jj
                    pg.op('pe', lambda e, j=j, jj=jj: e.matmul(po[0:C, :], qin[:, j, off:off + C], Sbt[:, j, :],
                                                               start=False, stop=(jj == 1)),
                          reads=[qink, Sbtk], writes=[pok])
                sq, sqk = sq_r.next()
                pg.op('act', lambda e: e.activation(out=sq[0:C, :], in_=po[0:C, :], func=AF.Square), reads=[pok], writes=[sqk])
                ss, ssk = ss_r.next()
                pg.op('dve', lambda e: e.tensor_reduce(out=ss[0:C, :], in_=sq[0:C, :], axis=AX.X, op=ALU.add),
                      reads=[sqk], writes=[ssk])
                pg.op('act', lambda e: e.activation(out=ss[0:C, :], in_=ss[0:C, :], func=AF.Sqrt, bias=epsc[0:C, 0:1],
                                                    scale=1.0 / 512), reads=[ssk, ('gm_eps',)], writes=[ssk])
                pg.op('dve', lambda e: e.reciprocal(out=ss[0:C, :], in_=ss[0:C, :]), reads=[ssk], writes=[ssk])
                on, onk = on_r.next()
                pg.op('dve', lambda e: e.scalar_tensor_tensor(out=on[0:C, :], in0=po[0:C, :], scalar=ss[0:C, 0:1],
                                                              in1=nwb[0:C, :], op0=ALU.mult, op1=ALU.mult),
                      reads=[pok, ssk, ('gm_nwb',)], writes=[onk])
                pt, ptk = pb.next()
                for b4 in range(4):
                    pg.op('pe', lambda e, b4=b4: e.transpose(pt[:, b4 * 64:b4 * 64 + C], on[0:C, b4 * 128:(b4 + 1) * 128],
                                                            ident[0:C, 0:C]),
                          reads=[onk, ('gm_id',)], writes=[ptk])
                pg.op('act', lambda e: e.activation(
                    out=yfm[:, h * 4:(h + 1) * 4, off:off + C],
                    in_=pt[:, 0:256].rearrange("p (a b) -> p a b", a=4)[:, :, 0:C], func=AF.Identity),
                    reads=[ptk], writes=[yfmk])
                for jj in range(2):
                    j = h * 2 + jj
                    pc, pck = pb.next()
                    pg.op('pe', lambda e, j=j: e.matmul(pc[:, :], keT[0:C, j * 128:(j + 1) * 128], vt[0:C, :],
                                                        start=True, stop=True),
                          reads=[keTk, vtk], writes=[pck])
                    pg.op('dve', lambda e, j=j, pc=pc: e.scalar_tensor_tensor(
                        out=St[:, j, :], in0=St[:, j, :], scalar=cdec[:, j, ci:ci + 1], in1=pc[:, :],
                        op0=ALU.mult, op1=ALU.add), reads=[Stk, pck, ('gm_cdec',)], writes=[Stk])
            pg.op('pool', lambda e: e.tensor_copy(out=Sbt[:], in_=St[:]), reads=[Stk], writes=[Sbtk])

        def load_big(b0, bn):
            out = []
            for n in ('qin', 'kin', 'kend'):
                t, k = qk_r[n].next()
                pg.dma('sp', t[:, :, 0:bn], scr[n][:, b0:b0 + bn].rearrange("(j p) t -> p j t", p=128), writes=[k])
                out += [t, k]
            vf, vfk = vf_r.next()
            pg.dma('sp', vf[:, :, 0:bn], uglaT[2048:4096, b0:b0 + bn].rearrange("(j p) t -> p j t", p=128), writes=[vfk])
            vb, vbk = vb_r.next()
            pg.op('pool', lambda e: e.tensor_copy(out=vb[:, :, 0:bn], in_=vf[:, :, 0:bn]), reads=[vfk], writes=[vbk])
            yfm, yfmk = yfm_r.next()
            return out + [vb, vbk, yfm, yfmk]

        def store_big(b0, bn, yfm, yfmk):
            pg.dma('sp', yrawT[:, b0:b0 + bn].rearrange("(c p) t -> p c t", p=128), yfm[:, :, 0:bn], reads=[yfmk],
                   writes=[('gyrawT', b0)])

        if L:
            pg.op('dve', lambda e: e.memset(S[0][:], 0.0), writes=[('gm_S', 0)])
            pg.op('pool', lambda e: e.memset(Sb[0][:], 0.0), writes=[('gm_Sb', 0)])
            for b0 in range(0, L, BIG):
                bn = min(BIG, L - b0)
                bufs = load_big(b0, bn)
                for off in range(0, bn, CK):
                    chunk((b0 + off) // CK, CK, off, bufs, 0)
                store_big(b0, bn, bufs[8], bufs[9])
            pg.dma('sp', st_out_p.rearrange("p (j v) -> p j v", j=8), S[0][:], reads=[('gm_S', 0)], writes=[('gla_out_p',)])
        if NS:
            bufs = load_big(L, NS)
            for i in range(NS):
                b = i % 2
                pg.dma('sp', S[b][:], st_in[i].rearrange("p (j v) -> p j v", j=8), writes=[('gm_S', b)])
                pg.op('pool', lambda e, b=b: e.tensor_copy(out=Sb[b][:], in_=S[b][:]), reads=[('gm_S', b)],
                      writes=[('gm_Sb', b)])
                chunk(nch + i, 1, i, bufs, b)
                pg.dma('sp', st_out_s[i].rearrange("p (j v) -> p j v", j=8), S[b][:], reads=[('gm_S', b)],
                       writes=[('gla_out_s', i)])
            store_big(L, NS, bufs[8], bufs[9])
        pg.barrier()


def gla_post(pg, L, NS, yrawT, uglaT, yT):
    T = L + NS
    with ExitStack() as es:
        inr = Ring(pg, es, 6, [128, 512], F32, 'gq_in')
        outr = Ring(pg, es, 3, [128, 512], BF16, 'gq_out')
        for (t0, tw) in token_tiles(T):
            for blk in range(16):
                rs = slice(blk * 128, (blk + 1) * 128)
                y, yk = inr.next()
                pg.dma('sp', y[:, 0:tw], yrawT[rs, t0:t0 + tw], writes=[yk])
                g, gk = inr.next()
                pg.dma('sp', g[:, 0:tw], uglaT[4096 + blk * 128:4096 + (blk + 1) * 128, t0:t0 + tw], writes=[gk])
                pg.op('act', lambda e: e.activation(out=g[:, 0:tw], in_=g[:, 0:tw], func=AF.Silu), reads=[gk], writes=[gk])
                o, ok = outr.next()
                pg.op('dve', lambda e: e.tensor_tensor(out=o[:, 0:tw], in0=y[:, 0:tw], in1=g[:, 0:tw], op=ALU.mult),
                      reads=[yk, gk], writes=[ok])
                pg.dma('sp', yT[rs, t0:t0 + tw], o[:, 0:tw], reads=[ok], writes=[('yglaT', blk, t0)])
        pg.barrier()


def linear_B(cx, W, K, N, XsrcT, epi, tiles=None):
    pg = cx.pg
    KC = K // 128
    KP = KC // 8
    NG = N // 512
    tiles = tiles or token_tiles(cx.T)
    XTb = cx.XTflat[:, 0:KC * 512].rearrange("p (k t) -> p k t", t=512)
    gcount = 0
    for (t0, tw) in tiles:
        pg.dma('sp', XTb[:, :, 0:tw], XsrcT[:, t0:t0 + tw].rearrange("(kc p) t -> p kc t", p=128),
               writes=[('XT',)])
        order = [(g, kp) for g in range(NG) for kp in range(KP)]
        loaded = {}
        state = {'next': 0}

        def issue_loads(upto):
            while state['next'] < len(order) and state['next'] < upto:
                g, kp = order[state['next']]
                slot = cx.wi % cx.NW
                cx.wi += 1
                src = W[kp * 1024:(kp + 1) * 1024, g * 512:(g + 1) * 512].rearrange("(kc p) c -> p kc c", p=128)
                dstv = cx.wbuf[slot][:].rearrange("p a b -> p (a b)").rearrange("p (k c) -> p k c", c=512)
                pg.dma('pool', dstv, src, writes=[('w', slot)])
                loaded[(g, kp)] = slot
                state['next'] += 1

        idx = 0
        for g in range(NG):
            banks = [(gcount % 2) * 4 + cb for cb in range(4)]
            gcount += 1
            for kp in range(KP):
                issue_loads(idx + cx.NW)
                slot = loaded[(g, kp)]
                wv = cx.wbuf[slot][:].rearrange("p a b -> p (a b)").rearrange("p (k c) -> p k c", c=512)
                for cb in range(4):
                    for k in range(8):
                        kc = kp * 8 + k
                        pg.op('pe', lambda e, cb=cb, k=k, kc=kc, wv=wv: e.matmul(
                            cx.psb[banks[cb]][:, 0:tw], wv[:, k, cb * 128:(cb + 1) * 128], XTb[:, kc, 0:tw],
                            start=(kc == 0), stop=(kc == KC - 1)),
                            reads=[('w', slot), ('XT',)], writes=[('ps', banks[cb])])
                idx += 1
            for cb in range(4):
                epi(g * 512 + cb * 128, 128, t0, tw, cx.psb[banks[cb]][:, 0:tw], ('ps', banks[cb]))


D = 4096
IN_COLS = 30192
O1, O2, O3 = 6592, 6592 + 5152, 6592 + 5152 + 6160
DFF = 16384

SMALL_COLS = ['norm_mix', 'norm_ffn', 'norm_ple']


def build_program(nc, L, NS, DEPTH=2):
    T = L + NS
    nch = L // 64

    def din(name, shape, dt=F32):
        return nc.dram_tensor(name, list(shape), dt, kind="ExternalInput").ap()

    def dout(name, shape, dt=F32):
        return nc.dram_tensor(name, list(shape), dt, kind="ExternalOutput").ap()

    def dscr(name, shape, dt=F32):
        return nc.dram_tensor(name, list(shape), dt, kind="Internal").ap()

    I = {}
    I['xT0'] = din('xT0', [D, T])
    I['pT'] = din('pT', [DEPTH, 256, T])
    I['rw_st'] = din('rw_st', [DEPTH, max(NS, 1), 128, 1024])
    I['shiftT'] = din('shiftT', [DEPTH, 6592, max(NS, 1)])
    I['ssm_st'] = din('ssm_st', [DEPTH, max(NS, 1), 128, 2048])
    I['convT'] = din('convT', [DEPTH, 3072, max(NS, 1), 3])
    I['gla_st'] = din('gla_st', [DEPTH, max(NS, 1), 128, 4096])
    I['w_in'] = din('w_in', [DEPTH, D, IN_COLS])
    I['w_branch'] = din('w_branch', [DEPTH, 3, 2048, D])
    I['w_out'] = din('w_out', [DEPTH, D, D])
    I['w_ff1'] = din('w_ff1', [DEPTH, D, DFF])
    I['w_ff2'] = din('w_ff2', [DEPTH, DFF, D])
    I['w_ple_gate'] = din('w_ple_gate', [DEPTH, D, D])
    I['w_ple_proj'] = din('w_ple_proj', [DEPTH, 256, D])
    I['rw_w2'] = din('rw_w2', [DEPTH, 96, 2048])
    I['rw_a2'] = din('rw_a2', [DEPTH, 96, 2048])
    I['rw_g2'] = din('rw_g2', [DEPTH, 256, 2048])
    I['gla_fup'] = din('gla_fup', [DEPTH, 16, 1024])
    for n in ('norm_mix', 'norm_ffn', 'norm_ple'):
        I[n] = din(n, [DEPTH, 128, 32])
    I['norm_final'] = din('norm_final', [128, 32])
    I['rw_mu'] = din('rw_mu', [DEPTH, 128, 52])
    for n in ('rw_w0', 'rw_a0', 'rw_kk', 'rw_ka', 'rw_rk', 'rw_lnw', 'rw_lnb', 'ssm_dcol', 'ssm_norm'):
        I[n] = din(n, [DEPTH, 128, 16])
    I['convw'] = din('convw', [DEPTH, 128, 24, 4])
    I['convb'] = din('convb', [DEPTH, 128, 24])
    I['dtb'] = din('dtb', [DEPTH, 32, 1])
    I['alog'] = din('alog', [DEPTH, 32, 1])
    I['gla_fb'] = din('gla_fb', [DEPTH, 128, 8])
    I['gla_nwb'] = din('gla_nwb', [DEPTH, 64, 512])
    I['k_bones_bf'] = din('k_bones_bf', [128, 128], BF16)
    I['k_istack'] = din('k_istack', [128, 64], BF16)
    I['k_sel2'] = din('k_sel2', [128, 2], BF16)
    I['k_ident'] = din('k_ident', [128, 128])
    I['k_identb'] = din('k_identb', [128, 128], BF16)
    I['k_selh'] = din('k_selh', [32, 32, 64])
    I['k_negmask'] = din('k_negmask', [64, 64])
    I['k_sellast'] = din('k_sellast', [64, 128])
    I['k_maskT'] = din('k_maskT', [64, 64])
    I['k_maskSU'] = din('k_maskSU', [64, 64])
    I['k_maskSL'] = din('k_maskSL', [64, 64])

    O = {}
    O['yT'] = dout('yT', [D, T])
    O['rw_out_p'] = dout('rw_out_p', [DEPTH, 128, 1024])
    O['rw_out_s'] = dout('rw_out_s', [DEPTH, max(NS, 1), 128, 1024])
    O['shift_out'] = dout('shift_out', [DEPTH, 6592, NS + 1])
    O['ssm_out_p'] = dout('ssm_out_p', [DEPTH, 128, 2048])
    O['ssm_out_s'] = dout('ssm_out_s', [DEPTH, max(NS, 1), 128, 2048])
    O['conv_out_p'] = dout('conv_out_p', [DEPTH, 3072, 3])
    O['conv_out_s'] = dout('conv_out_s', [DEPTH, 3072, max(NS, 1), 3])
    O['gla_out_p'] = dout('gla_out_p', [DEPTH, 128, 4096])
    O['gla_out_s'] = dout('gla_out_s', [DEPTH, max(NS, 1), 128, 4096])

    S = {}
    S['xT'] = dscr('s_xT', [D, T])
    S['urwT'] = dscr('s_urwT', [6592, T])
    S['ussmT'] = dscr('s_ussmT', [5152, T])
    S['uglaT'] = dscr('s_uglaT', [6160, T])
    S['gateT'] = dscr('s_gateT', [3 * D, T], BF16)
    for n in ('lw', 'an', 'bn', 'kh', 'r', 'v', 'g', 'bonus'):
        S['rw_' + n] = dscr('s_rw_' + n, [2048, T])
    S['oT'] = dscr('s_oT', [2048, T])
    S['xbcT'] = dscr('s_xbcT', [3072, T])
    S['yrawT'] = dscr('s_yrawT', [2048, T])
    S['qin'] = dscr('s_qin', [1024, T], BF16)
    S['kin'] = dscr('s_kin', [1024, T], BF16)
    S['kend'] = dscr('s_kend', [1024, T], BF16)
    S['cdec'] = dscr('s_cdec', [1024, nch + NS])
    S['gyrawT'] = dscr('s_gyrawT', [2048, T])
    for n in ('yrwT', 'yssmT', 'yglaT'):
        S[n] = dscr('s_' + n, [2048, T], BF16)
    S['mergedT'] = dscr('s_mergedT', [D, T])
    S['aT'] = dscr('s_aT', [DFF, T], BF16)
    S['pgT'] = dscr('s_pgT', [D, T], BF16)

    with ExitStack() as es:
        pg = PG(nc, es)
        xcur = I['xT0']
        for l in range(DEPTH):
            with ExitStack() as es2:
                cx = Ctx(pg, es2, T)
                gm = load_cols(pg, es2, I['norm_mix'][l], 32, 'g_mix')
                phase_norm(cx, xcur, gm, D, 1e-6)
                pg.barrier()
                W = I['w_in'][l]
                linear_A(cx, W, D, [(0, O1)], make_epi(cx, S['urwT'], 'urwT', rowoff=0))
                linear_A(cx, W, D, [(O1, O2 - O1)], make_epi(cx, S['ussmT'], 'ussmT', rowoff=O1))
                linear_A(cx, W, D, [(O2, O3 - O2)], make_epi(cx, S['uglaT'], 'uglaT', rowoff=O2))
                linear_A(cx, W, D, [(O3, 3 * D)], make_epi(cx, S['gateT'], 'gateT', func=AF.Sigmoid, rowoff=O3))
                pg.barrier()
            lo = L - 1 if L else 0
            pg.dma('sp', O['shift_out'][l][:, (0 if L else 1):NS + 1], S['urwT'][:, lo:L + NS], writes=[('shift_out', l)])
            cw = dict(mu=I['rw_mu'][l], w0=I['rw_w0'][l], a0=I['rw_a0'][l], kk=I['rw_kk'][l], ka=I['rw_ka'][l],
                      rk=I['rw_rk'][l], lnw=I['rw_lnw'][l], lnb=I['rw_lnb'][l])
            lw = dict(w2=I['rw_w2'][l], a2=I['rw_a2'][l], g2=I['rw_g2'][l])
            scr = {n: S['rw_' + n] for n in ('lw', 'an', 'bn', 'kh', 'r', 'v', 'g', 'bonus')}
            cst = dict(ident=I['k_ident'], maskSU=I['k_maskSU'], maskSL=I['k_maskSL'], maskU=I['k_maskT'])
            rwkv_prep(pg, L, NS, S['urwT'], I['shiftT'][l], cw, lw, scr)
            rwkv_scan2(pg, L, NS, scr, cst, I['rw_st'][l], O['rw_out_p'][l], O['rw_out_s'][l], S['oT'])
            rwkv_post(pg, L, NS, S['oT'], scr, cw, S['yrwT'])
            cw = dict(convw=I['convw'][l], convb=I['convb'][l], dtb=I['dtb'][l], alog=I['alog'][l],
                      dcol=I['ssm_dcol'][l], ssm_norm=I['ssm_norm'][l])
            cst = dict(ident=I['k_ident'], selh=I['k_selh'], negmask=I['k_negmask'], sellast=I['k_sellast'])
            ssd_conv(pg, L, NS, S['ussmT'], I['convT'][l], cw, S['xbcT'], O['conv_out_p'][l], O['conv_out_s'][l])
            ssd_main(pg, L, NS, S['ussmT'], S['xbcT'], cw, cst, I['ssm_st'][l], O['ssm_out_p'][l], O['ssm_out_s'][l],
                     S['yrawT'])
            ssd_post(pg, L, NS, S['yrawT'], S['xbcT'], S['ussmT'], cw, S['yssmT'])
            cw = dict(gla_fb=I['gla_fb'][l], gla_nwb=I['gla_nwb'][l])
            lw = dict(gla_fup=I['gla_fup'][l])
            scr = dict(qin=S['qin'], kin=S['kin'], kend=S['kend'], cdec=S['cdec'])
            cst = dict(identb=I['k_identb'], ident=I['k_ident'], maskT=I['k_maskT'])
            gla_prep(pg, L, NS, S['uglaT'], cw, lw, scr)
            gla_main(pg, L, NS, S['uglaT'], scr, cw, cst, I['gla_st'][l], O['gla_out_p'][l], O['gla_out_s'][l],
                     S['gyrawT'])
            gla_post(pg, L, NS, S['gyrawT'], S['uglaT'], S['yglaT'])
            with ExitStack() as es2:
                cx = Ctx(pg, es2, T)
                for j, yn in enumerate(('yrwT', 'yssmT', 'yglaT')):
                    pg.dma('sp', cx.XT[:, 0:16, :], S[yn].rearrange("(kc p) t -> p kc t", p=128), writes=[('XT',)])
                    gate = S['gateT'][j * D:(j + 1) * D, :]
                    linear_A(cx, I['w_branch'][l, j], 2048, [(0, D)],
                             make_epi(cx, S['mergedT'], 'mergedT', mul=gate, mulkey='gateT',
                                      add=(S['mergedT'] if j else None), addkey='mergedT'))
                    pg.barrier()
                for kc in range(32):
                    for (ca, cb_) in ((0, T // 2), (T // 2, T)):
                        pg.dma('pool', cx.XT[:, kc, ca:cb_], S['mergedT'][kc * 128:(kc + 1) * 128, ca:cb_], writes=[('XT',)])
                linear_A(cx, I['w_out'][l], D, [(0, D)], make_epi(cx, S['xT'], 'xT', add=xcur, addkey='xT'))
                pg.barrier()
                xcur = S['xT']
                gf = load_cols(pg, es2, I['norm_ffn'][l], 32, 'g_ffn')
                phase_norm(cx, xcur, gf, D, 1e-6)
                pg.barrier()
                linear_A(cx, I['w_ff1'][l], D, [(0, DFF)], make_epi(cx, S['aT'], 'aT', func=AF.Relu, square=True))
                pg.barrier()
                for qd in range(4):
                    pg.dma('sp', cx.XT[:, :, :], S['aT'][qd * D:(qd + 1) * D, :].rearrange("(kc p) t -> p kc t", p=128),
                           writes=[('XT',)])
                    linear_A(cx, I['w_ff2'][l][qd * D:(qd + 1) * D, :], D, [(0, D)],
                             make_epi(cx, S['xT'], 'xT', add=xcur, addkey='xT'))
                    pg.barrier()
                gp = load_cols(pg, es2, I['norm_ple'][l], 32, 'g_ple')
                phase_norm(cx, xcur, gp, D, 1e-6)
                pg.barrier()
                linear_A(cx, I['w_ple_gate'][l], D, [(0, D)], make_epi(cx, S['pgT'], 'pgT', func=AF.Sigmoid))
                pg.barrier()
                for kc in range(2):
                    for (ca, cb_) in ((0, T // 2), (T // 2, T)):
                        pg.dma('pool', cx.XT[:, kc, ca:cb_], I['pT'][l][kc * 128:(kc + 1) * 128, ca:cb_], writes=[('XT',)])
                linear_A(cx, I['w_ple_proj'][l], 256, [(0, D)],
                         make_epi(cx, S['xT'], 'xT', mul=S['pgT'], mulkey='pgT', add=xcur, addkey='xT'))
                pg.barrier()
                if l == DEPTH - 1:
                    gfin = load_cols(pg, es2, I['norm_final'], 32, 'g_fin')
                    phase_norm(cx, xcur, gfin, D, 1e-6, dst=O['yT'])
                    pg.barrier()
        pg.barrier()
    return pg


def _colvec(v):
    v = np.asarray(v, np.float32).reshape(-1)
    n = (v.size + 127) // 128
    p = np.zeros(n * 128, np.float32)
    p[:v.size] = v
    return np.ascontiguousarray(p.reshape(n, 128).T)


def _stack(fn, arr):
    return np.ascontiguousarray(np.stack([fn(arr[l]) for l in range(arr.shape[0])]))


def _mu_layout(mu):
    out = np.zeros((128, 52), np.float32)
    out[:, :48] = mu[:6144].reshape(48, 128).T
    out[:96, 48] = mu[6144:6240]
    out[:96, 49] = mu[6240:6336]
    out[:, 50] = mu[6336:6464]
    out[:, 51] = mu[6464:6592]
    return out


def _consts():
    import ml_dtypes
    bf = ml_dtypes.bfloat16
    bones = np.zeros((128, 128), np.float32)
    bones[:64, :64] = 1
    bones[64:, 64:] = 1
    ist = np.zeros((128, 64), np.float32)
    ist[np.arange(128), np.arange(128) % 64] = 1
    sel2 = np.zeros((128, 2), np.float32)
    sel2[:64, 0] = 1
    sel2[64:, 1] = 1
    selh = np.zeros((32, 32, 64), np.float32)
    for h in range(32):
        selh[h, h, :] = 1
    s = np.arange(64)[:, None]
    t = np.arange(64)[None, :]
    sellast = np.zeros((64, 128), np.float32)
    sellast[63, :] = 1
    return {
        'k_bones_bf': bones.astype(bf), 'k_istack': ist.astype(bf), 'k_sel2': sel2.astype(bf),
        'k_ident': np.eye(128, dtype=np.float32), 'k_identb': np.eye(128, dtype=np.float32).astype(bf),
        'k_selh': selh, 'k_negmask': np.where(s <= t, 0.0, -30000.0).astype(np.float32),
        'k_sellast': sellast, 'k_maskT': (s <= t).astype(np.float32),
        'k_maskSU': (s < t).astype(np.float32), 'k_maskSL': (s > t).astype(np.float32),
    }


def shared_inputs(inp):
    f = lambda a: np.ascontiguousarray(np.asarray(a, np.float32))
    DEPTH = inp['w_in'].shape[0]
    m = {}
    for n in ('w_in', 'w_branch', 'w_out', 'w_ff1', 'w_ff2', 'w_ple_gate', 'w_ple_proj', 'rw_w2', 'rw_a2', 'rw_g2'):
        m[n] = f(inp[n])
    m['gla_fup'] = f(inp['gla_f_up'])
    for n in ('norm_mix', 'norm_ffn', 'norm_ple'):
        m[n] = _stack(_colvec, f(inp[n]))
    m['norm_final'] = _colvec(inp['norm_final'])
    m['rw_mu'] = _stack(_mu_layout, f(inp['rw_mu']))
    for n, src in (('rw_w0', 'rw_w0'), ('rw_a0', 'rw_a0'), ('rw_kk', 'rw_kk'), ('rw_ka', 'rw_ka'), ('rw_rk', 'rw_rk'),
                   ('rw_lnw', 'rw_ln_w'), ('rw_lnb', 'rw_ln_b'), ('ssm_norm', 'ssm_norm')):
        m[n] = _stack(_colvec, f(inp[src]).reshape(DEPTH, -1))
    m['ssm_dcol'] = _stack(lambda d: _colvec(np.repeat(d, 64)), f(inp['ssm_d']))
    m['convw'] = _stack(lambda w: np.ascontiguousarray(w.reshape(4, 24, 128).transpose(2, 1, 0)), f(inp['ssm_conv_w']))
    m['convb'] = _stack(_colvec, f(inp['ssm_conv_b']))
    m['dtb'] = f(inp['ssm_dt_bias']).reshape(DEPTH, 32, 1)
    m['alog'] = f(inp['ssm_a_log']).reshape(DEPTH, 32, 1)
    m['gla_fb'] = _stack(_colvec, f(inp['gla_f_bias']))
    m['gla_nwb'] = _stack(lambda g: np.ascontiguousarray(np.tile(g, (64, 1))), f(inp['gla_norm']))
    m.update(_consts())
    return m


def core_inputs(inp, seq, s0, NS):
    f = lambda a: np.asarray(a, np.float32)
    DEPTH = inp['w_in'].shape[0]
    xp = f(inp['x_prompt'])[seq]
    xs = f(inp['x_sample'])[s0:s0 + NS, 0]
    m = {}
    m['xT0'] = np.ascontiguousarray(np.concatenate([xp, xs], 0).T)
    pp = f(inp['p_prompt'])[:, seq]
    ps = f(inp['p_sample'])[:, s0:s0 + NS, 0]
    m['pT'] = np.ascontiguousarray(np.concatenate([pp, ps], 1).transpose(0, 2, 1))
    S = f(inp['state_rwkv'])[:, s0:s0 + NS]
    m['rw_st'] = np.ascontiguousarray(
        S.reshape(DEPTH, NS, 16, 2, 64, 64).transpose(0, 1, 3, 5, 2, 4).reshape(DEPTH, NS, 128, 1024))
    m['shiftT'] = np.ascontiguousarray(f(inp['state_shift'])[:, s0:s0 + NS].transpose(0, 2, 1))
    H = f(inp['state_ssm'])[:, s0:s0 + NS]
    m['ssm_st'] = np.ascontiguousarray(H.transpose(0, 1, 4, 2, 3).reshape(DEPTH, NS, 128, 2048))
    C = f(inp['state_conv'])[:, s0:s0 + NS]
    m['convT'] = np.ascontiguousarray(C.transpose(0, 3, 1, 2))
    G = f(inp['state_gla'])[:, s0:s0 + NS]
    m['gla_st'] = np.ascontiguousarray(
        G.reshape(DEPTH, NS, 8, 128, 512).transpose(0, 1, 3, 2, 4).reshape(DEPTH, NS, 128, 4096))
    return m


def unpack_core(R, L, NS):
    DEPTH = R['rw_out_p'].shape[0]
    g = lambda k: np.asarray(R[k], np.float32)
    o = {}
    yT = g('yT')
    o['y_p'] = np.ascontiguousarray(yT[:, :L].T)
    o['y_s'] = np.ascontiguousarray(yT[:, L:].T)
    conv = lambda Dv: Dv.reshape(Dv.shape[0], -1, 2, 64, 16, 64).transpose(0, 1, 4, 2, 5, 3).reshape(Dv.shape[0], -1, 32, 64, 64)
    o['rw_p'] = conv(g('rw_out_p')[:, None])[:, 0]
    o['rw_s'] = conv(g('rw_out_s'))
    sh = g('shift_out')
    o['sh_p'] = sh[:, :, 0]
    o['sh_s'] = sh[:, :, 1:].transpose(0, 2, 1)
    o['ssm_p'] = g('ssm_out_p').reshape(DEPTH, 128, 32, 64).transpose(0, 2, 3, 1)
    o['ssm_s'] = g('ssm_out_s').reshape(DEPTH, -1, 128, 32, 64).transpose(0, 1, 3, 4, 2)
    o['conv_p'] = g('conv_out_p').transpose(0, 2, 1)
    o['conv_s'] = g('conv_out_s').transpose(0, 2, 3, 1)
    o['gla_p'] = g('gla_out_p').reshape(DEPTH, 128, 8, 512).transpose(0, 2, 1, 3).reshape(DEPTH, 4, 256, 512)
    o['gla_s'] = g('gla_out_s').reshape(DEPTH, -1, 128, 8, 512).transpose(0, 1, 3, 2, 4).reshape(DEPTH, -1, 4, 256, 512)
    return o


def kernel(**inputs):
    B, L = inputs['x_prompt'].shape[0], inputs['x_prompt'].shape[1]
    NSAMP = inputs['x_sample'].shape[0]
    n = 8
    NS = NSAMP // n
    DEPTH = inputs['w_in'].shape[0]
    nc = bass.Bass("TRN2", target_bir_lowering=False)
    build_program(nc, L, NS, DEPTH)
    shared = shared_inputs(inputs)
    in_maps = []
    for c in range(n):
        m = dict(shared)
        m.update(core_inputs(inputs, c % B, c * NS, NS))
        in_maps.append(m)
    res = run_bass_kernel_spmd(nc, in_maps, core_ids=list(range(n)))
    outs = [unpack_core(r, L, NS) for r in res.results]
    cat_p = lambda k, ax: np.ascontiguousarray(np.stack([outs[c][k] for c in range(B)], axis=ax)).astype(np.float32)
    cat_s = lambda k, ax: np.ascontiguousarray(np.concatenate([outs[c][k] for c in range(n)], axis=ax)).astype(np.float32)
    y_prompt = cat_p('y_p', 0)
    y_sample = cat_s('y_s', 0)[:, None, :]
    return (y_prompt, y_sample,
            cat_p('rw_p', 1), cat_s('rw_s', 1),
            cat_p('sh_p', 1), cat_s('sh_s', 1),
            cat_p('ssm_p', 1), cat_s('ssm_s', 1),
            cat_p('conv_p', 1), cat_s('conv_s', 1),
            cat_p('gla_p', 1), cat_s('gla_s', 1))


def rwkv_scan2(pg, L, NS, scr, cst, st_in, st_out_p, st_out_s, oT):
    names = ['lw', 'an', 'bn', 'kh', 'r', 'v']
    CK = 64
    OB = 128
    with ExitStack() as es:
        ident = pg.sb(es, [128, 128], F32, 'c2_id')
        mSU = pg.sb(es, [64, 64], F32, 'c2_mSU')
        mSL = pg.sb(es, [64, 64], F32, 'c2_mSL')
        mU = pg.sb(es, [64, 64], F32, 'c2_mU')
        pg.dma('sp', ident[:], cst['ident'], writes=[('c2_id',)])
        pg.dma('sp', mSU[:], cst['maskSU'], writes=[('c2_mSU',)])
        pg.dma('sp', mSL[:], cst['maskSL'], writes=[('c2_mSL',)])
        pg.dma('sp', mU[:], cst['maskU'], writes=[('c2_mU',)])
        KM = {'SU': (mSU, ('c2_mSU',)), 'SL': (mSL, ('c2_mSL',)), 'U': (mU, ('c2_mU',))}
        STb = [pg.sb(es, [64, 32, 64], F32, 'c2_ST%d' % i) for i in range(2)]
        sh = [64, 8, 64]
        inr = {n: Ring(pg, es, 2, sh, F32, 'c2i_' + n) for n in names}
        cumr = Ring(pg, es, 4, sh, F32, 'c2_cum')
        er = Ring(pg, es, 4, sh, F32, 'c2_e')
        tr_ = {n: Ring(pg, es, 2, sh, F32, 'c2t_' + n) for n in ('At', 'Bt', 'Kt', 'Be', 'Ke', 'N', 'NT', 'AakT', 'T', 'ATt')}
        Pr = Ring(pg, es, 4, sh, F32, 'c2_P')
        PTr = Ring(pg, es, 4, sh, F32, 'c2_PT')
        per_ = {n: Ring(pg, es, 4, sh, F32, 'c2p_' + n) for n in ('Rt', 'Abr', 'Akr', 'Ah', 'Wh', 'BeT', 'KeT', 'VT')}
        gCr = Ring(pg, es, 4, [64, 8], F32, 'c2_gC')
        UTr = Ring(pg, es, 2, sh, F32, 'c2_UT')
        OTr = Ring(pg, es, 2, sh, F32, 'c2_OT')
        osb = Ring(pg, es, 2, [128, 16, OB], F32, 'c2_osb')
        pb = PRing(pg, es, 8, [128, 512], F32, 'c2_pb')

        def v3(p, C):
            return p[0:C, 0:512].rearrange("p (h t) -> p h t", h=8)

        def pre(c0, C, hb, out):
            h0 = hb * 8
            q, qk = {}, {}
            for n in names:
                q[n], qk[n] = inr[n].next()
                with pg.nc.allow_non_contiguous_dma(reason="single-token sample columns"):
                    pg.dma('sp', q[n][:, :, 0:C], scr[n].rearrange("(h k) t -> k h t", k=64)[:, h0:h0 + 8, c0:c0 + C],
                           writes=[qk[n]])
            if C > 1:
                src, srck = q['lw'], qk['lw']
                s_ = 1
                while s_ < C:
                    dst, dstk = cumr.next()
                    pg.op('pool', lambda e, src=src, dst=dst, s_=s_: e.tensor_copy(out=dst[:, :, 0:s_], in_=src[:, :, 0:s_]),
                          reads=[srck], writes=[dstk])
                    pg.op('pool', lambda e, src=src, dst=dst, s_=s_: e.tensor_tensor(
                        out=dst[:, :, s_:C], in0=src[:, :, s_:C], in1=src[:, :, 0:C - s_], op=ALU.add),
                        reads=[srck, dstk], writes=[dstk])
                    src, srck = dst, dstk
                    s_ *= 2
                cum, cumk = src, srck
            else:
                cum, cumk = q['lw'], qk['lw']
            lw = q['lw']

            def ew(fn_pre, scale, dsts):
                e, ek = er.next()
                if fn_pre is not None:
                    fn_pre(e, ek)
                    src_, srck_ = e, ek
                else:
                    src_, srck_ = cum, cumk
                pg.op('act', lambda en: en.activation(out=e[:, :, 0:C], in_=src_[:, :, 0:C], func=AF.Exp, scale=scale),
                      reads=[srck_], writes=[ek])
                for (srcn, (d, dk), eng) in dsts:
                    pg.op(eng, lambda en, srcn=srcn, d=d: en.tensor_tensor(out=d[:, :, 0:C], in0=q[srcn][:, :, 0:C],
                                                                         in1=e[:, :, 0:C], op=ALU.mult),
                          reads=[qk[srcn], ek], writes=[dk])
                return e, ek

            At = tr_['At'].next()
            Bt = tr_['Bt'].next()
            Kt = tr_['Kt'].next()
            Be = tr_['Be'].next()
            Ke = tr_['Ke'].next()
            Rt = per_['Rt'].next()
            ew(lambda e, ek: pg.op('dve', lambda en: en.tensor_tensor(out=e[:, :, 0:C], in0=cum[:, :, 0:C],
                                                                      in1=lw[:, :, 0:C], op=ALU.subtract),
                                   reads=[cumk, qk['lw']], writes=[ek]), 1.0, [('an', At, 'dve')])
            ew(None, -1.0, [('bn', Bt, 'dve'), ('kh', Kt, 'pool')])
            e3, e3k = ew(None, 1.0, [('r', Rt, 'dve')])
            gC, gCk = gCr.next()
            pg.op('dve', lambda en: en.tensor_copy(out=gC[:, :], in_=e3[:, :, C - 1]), reads=[e3k], writes=[gCk])
            ew(lambda e, ek: pg.op('dve', lambda en: en.tensor_tensor(
                out=e[:, :, 0:C], in0=cum[:, :, C - 1:C].to_broadcast([64, 8, C]), in1=cum[:, :, 0:C], op=ALU.subtract),
                reads=[cumk], writes=[ek]), 1.0, [('bn', Be, 'dve'), ('kh', Ke, 'pool')])
            res = {}
            for (nm, X, Y, mk, ring) in (('N', Bt, At, 'SU', tr_['N']), ('NT', At, Bt, 'SL', tr_['NT']),
                                         ('AakT', At, Kt, 'SL', tr_['AakT']), ('Abr', Bt, Rt, 'U', per_['Abr']),
                                         ('Akr', Kt, Rt, 'U', per_['Akr'])):
                p, pk = pb.next()
                for hl in range(8):
                    pg.op('pe', lambda e, hl=hl, X=X, Y=Y, p=p: e.matmul(p[0:C, hl * 64:hl * 64 + C], X[0][:, hl, 0:C],
                                                                        Y[0][:, hl, 0:C], start=True, stop=True),
                          reads=[X[1], Y[1]], writes=[pk])
                d, dk = ring.next()
                m, mkey = KM[mk]
                pg.op('dve', lambda e, p=p, d=d, m=m: e.tensor_tensor(
                    out=d[0:C, :, 0:C], in0=v3(p, C)[:, :, 0:C], in1=m[0:C, 0:C].unsqueeze(1).to_broadcast([C, 8, C]),
                    op=ALU.mult), reads=[pk, mkey], writes=[dk])
                res[nm] = (d, dk)
            for (nm, X, ring) in (('ATt', At, tr_['ATt']), ('BeT', Be, per_['BeT']), ('KeT', Ke, per_['KeT']),
                                  ('VT', (q['v'], qk['v']), per_['VT'])):
                p, pk = pb.next()
                for hl in range(8):
                    pg.op('pe', lambda e, hl=hl, X=X, p=p: e.transpose(p[0:C, hl * 64:(hl + 1) * 64], X[0][:, hl, 0:C],
                                                                      ident[0:64, 0:64]),
                          reads=[X[1], ('c2_id',)], writes=[pk])
                d, dk = ring.next()
                pg.op('act', lambda e, p=p, d=d: e.activation(out=d[0:C, :, :], in_=v3(p, C), func=AF.Identity),
                      reads=[pk], writes=[dk])
                res[nm] = (d, dk)
            T, Tk = tr_['T'].next()
            N, Nk = res['N']
            pg.op('dve', lambda e: e.tensor_tensor(out=T[0:C, :, 0:C], in0=N[0:C, :, 0:C],
                                                   in1=ident[0:C, 0:C].unsqueeze(1).to_broadcast([C, 8, C]), op=ALU.add),
                  reads=[Nk, ('c2_id',)], writes=[Tk])
            yield
            P, Pk = res['N']
            PT, PTk = res['NT']
            rounds = 0
            s_ = 2
            while s_ < C:
                rounds += 1
                s_ *= 2
            for i in range(rounds):
                last = (i == rounds - 1)
                if not last:
                    p, pk = pb.next()
                    for hl in range(8):
                        pg.op('pe', lambda e, hl=hl, p=p, P=P, PT=PT: e.matmul(
                            p[0:C, hl * 64:hl * 64 + C], PT[0:C, hl, 0:C], P[0:C, hl, 0:C], start=True, stop=True),
                            reads=[Pk, PTk], writes=[pk])
                    Pn, Pnk = Pr.next()
                    pg.op('act', lambda e, p=p, Pn=Pn: e.activation(out=Pn[0:C, :, 0:C], in_=v3(p, C)[:, :, 0:C],
                                                                    func=AF.Identity), reads=[pk], writes=[Pnk])
                p2, p2k = pb.next()
                for hl in range(8):
                    pg.op('pe', lambda e, hl=hl, p2=p2, P=P, PT=PT: e.matmul(
                        p2[0:C, hl * 64:hl * 64 + C], P[0:C, hl, 0:C], PT[0:C, hl, 0:C], start=True, stop=True),
                        reads=[Pk, PTk], writes=[p2k])
                PTn, PTnk = PTr.next()
                pg.op('act', lambda e, p2=p2, PTn=PTn: e.activation(out=PTn[0:C, :, 0:C], in_=v3(p2, C)[:, :, 0:C],
                                                                    func=AF.Identity), reads=[p2k], writes=[PTnk])
                yield
                p3, p3k = pb.next()
                for hl in range(8):
                    pg.op('pe', lambda e, hl=hl, p3=p3, PTn=PTn: e.matmul(
                        p3[0:C, hl * 64:hl * 64 + C], PTn[0:C, hl, 0:C], T[0:C, hl, 0:C], start=True, stop=True),
                        reads=[PTnk, Tk], writes=[p3k])
                pg.op('dve', lambda e, p3=p3: e.tensor_tensor(out=T[0:C, :, 0:C], in0=T[0:C, :, 0:C],
                                                              in1=v3(p3, C)[:, :, 0:C], op=ALU.add),
                      reads=[Tk, p3k], writes=[Tk])
                yield
                if not last:
                    P, Pk = Pn, Pnk
                PT, PTk = PTn, PTnk
            ATt, ATtk = res['ATt']
            AakT, AakTk = res['AakT']
            p, pk = pb.next()
            for hl in range(8):
                pg.op('pe', lambda e, hl=hl, p=p: e.matmul(p[0:64, hl * 64:hl * 64 + C], ATt[0:C, hl, :], T[0:C, hl, 0:C],
                                                           start=True, stop=True), reads=[ATtk, Tk], writes=[pk])
            Ah, Ahk = per_['Ah'].next()
            pg.op('act', lambda e, p=p: e.activation(out=Ah[:, :, 0:C], in_=v3(p, 64)[:, :, 0:C], func=AF.Identity),
                  reads=[pk], writes=[Ahk])
            p, pk = pb.next()
            for hl in range(8):
                pg.op('pe', lambda e, hl=hl, p=p: e.matmul(p[0:C, hl * 64:hl * 64 + C], AakT[0:C, hl, 0:C], T[0:C, hl, 0:C],
                                                           start=True, stop=True), reads=[AakTk, Tk], writes=[pk])
            Wh, Whk = per_['Wh'].next()
            pg.op('act', lambda e, p=p: e.activation(out=Wh[0:C, :, 0:C], in_=v3(p, C)[:, :, 0:C], func=AF.Identity),
                  reads=[pk], writes=[Whk])
            out.update(dict(Rt=Rt, gC=(gC, gCk), Abr=res['Abr'], Akr=res['Akr'], Ah=(Ah, Ahk), Wh=(Wh, Whk),
                            BeT=res['BeT'], KeT=res['KeT'], VT=res['VT']))

        def seq(C, hb, pr, ST, stk, ob, obk, ooff):
            h0 = hb * 8
            Rt, Rtk = pr['Rt']
            gC, gCk = pr['gC']
            Abr, Abrk = pr['Abr']
            Akr, Akrk = pr['Akr']
            Ah, Ahk = pr['Ah']
            Wh, Whk = pr['Wh']
            BeT, BeTk = pr['BeT']
            KeT, KeTk = pr['KeT']
            VT, VTk = pr['VT']
            p, pk = pb.next()
            for hl in range(8):
                o_ = p[0:C, hl * 64:(hl + 1) * 64]
                pg.op('pe', lambda e, hl=hl, o_=o_: e.matmul(o_, Ah[:, hl, 0:C], ST[:, h0 + hl, :], start=True, stop=False),
                      reads=[Ahk, stk], writes=[pk])
                pg.op('pe', lambda e, hl=hl, o_=o_: e.matmul(o_, Wh[0:C, hl, 0:C], VT[0:C, hl, :], start=False, stop=True),
                      reads=[Whk, VTk], writes=[pk])
            UT, UTk = UTr.next()
            pg.op('act', lambda e: e.activation(out=UT[0:C, :, :], in_=v3(p, C), func=AF.Identity), reads=[pk], writes=[UTk])
            p2, p2k = pb.next()
            for hl in range(8):
                o_ = p2[0:C, hl * 64:(hl + 1) * 64]
                pg.op('pe', lambda e, hl=hl, o_=o_: e.matmul(o_, Rt[:, hl, 0:C], ST[:, h0 + hl, :], start=True, stop=False),
                      reads=[Rtk, stk], writes=[p2k])
                pg.op('pe', lambda e, hl=hl, o_=o_: e.matmul(o_, Abr[0:C, hl, 0:C], UT[0:C, hl, :], start=False, stop=False),
                      reads=[Abrk, UTk], writes=[p2k])
                pg.op('pe', lambda e, hl=hl, o_=o_: e.matmul(o_, Akr[0:C, hl, 0:C], VT[0:C, hl, :], start=False, stop=True),
                      reads=[Akrk, VTk], writes=[p2k])
            OT, OTk = OTr.next()
            pg.op('dve', lambda e: e.tensor_copy(out=OT[0:C, :, :], in_=v3(p2, C)), reads=[p2k], writes=[OTk])
            p3, p3k = pb.next()
            for j in range(4):
                pg.op('pe', lambda e, j=j: e.transpose(p3[:, j * 64:j * 64 + C],
                                                      OT[0:C, 2 * j:2 * j + 2, :].rearrange("p a b -> p (a b)"),
                                                      ident[0:C, 0:C]), reads=[OTk, ('c2_id',)], writes=[p3k])
            pg.op('act', lambda e: e.activation(out=ob[:, hb * 4:(hb + 1) * 4, ooff:ooff + C],
                                                in_=p3[:, 0:256].rearrange("p (a b) -> p a b", a=4)[:, :, 0:C],
                                                func=AF.Identity), reads=[p3k], writes=[obk])
            p4, p4k = pb.next()
            for hl in range(8):
                o_ = p4[0:64, hl * 64:(hl + 1) * 64]
                pg.op('pe', lambda e, hl=hl, o_=o_: e.matmul(o_, BeT[0:C, hl, :], UT[0:C, hl, :], start=True, stop=False),
                      reads=[BeTk, UTk], writes=[p4k])
                pg.op('pe', lambda e, hl=hl, o_=o_: e.matmul(o_, KeT[0:C, hl, :], VT[0:C, hl, :], start=False, stop=True),
                      reads=[KeTk, VTk], writes=[p4k])
            pg.op('dve', lambda e: e.tensor_tensor(out=ST[:, h0:h0 + 8, :], in0=ST[:, h0:h0 + 8, :],
                                                   in1=gC[:, :].unsqueeze(2).to_broadcast([64, 8, 64]), op=ALU.mult),
                  reads=[stk, gCk], writes=[stk])
            pg.op('dve', lambda e: e.tensor_tensor(out=ST[:, h0:h0 + 8, :], in0=ST[:, h0:h0 + 8, :], in1=v3(p4, 64),
                                                   op=ALU.add), reads=[stk, p4k], writes=[stk])

        def run_chunk(c0, C, ST, stk, ob, obk, ooff):
            for pair in range(2):
                outs = [{}, {}]
                gens = [pre(c0, C, pair * 2 + i, outs[i]) for i in range(2)]
                live = list(gens)
                while live:
                    for g in list(live):
                        try:
                            next(g)
                        except StopIteration:
                            live.remove(g)
                for i in range(2):
                    seq(C, pair * 2 + i, outs[i], ST, stk, ob, obk, ooff)

        def st_view(ap2d):
            return ap2d.rearrange("(hh k) (hp v) -> k hp hh v", hh=2, v=64)

        def sb_view(t):
            return t[:].rearrange("k (hp hh) v -> k hp hh v", hh=2)

        oTv = oT.rearrange("(c p) t -> p c t", p=128)
        if L:
            pg.op('dve', lambda e: e.memset(STb[0][:], 0.0), writes=[('c2_ST', 0)])
            for b0 in range(0, L, OB):
                bn_ = min(OB, L - b0)
                ob, obk = osb.next()
                for off in range(0, bn_, CK):
                    run_chunk(b0 + off, CK, STb[0], ('c2_ST', 0), ob, obk, off)
                pg.dma('sp', oTv[:, :, b0:b0 + bn_], ob[:, :, 0:bn_], reads=[obk], writes=[('oT', b0)])
            for hh in range(2):
                pg.dma('sp', st_view(st_out_p)[:, :, hh, :], sb_view(STb[0])[:, :, hh, :], reads=[('c2_ST', 0)],
                       writes=[('st_out_p', hh)])
        if NS:
            ob, obk = osb.next()
            for i in range(NS):
                b = i % 2
                for hh in range(2):
                    pg.dma('sp', sb_view(STb[b])[:, :, hh, :], st_view(st_in[i])[:, :, hh, :], writes=[('c2_ST', b)])
                run_chunk(L + i, 1, STb[b], ('c2_ST', b), ob, obk, i)
                for hh in range(2):
                    pg.dma('sp', st_view(st_out_s[i])[:, :, hh, :], sb_view(STb[b])[:, :, hh, :], reads=[('c2_ST', b)],
                           writes=[('st_out_s', i, hh)])
            pg.dma('sp', oTv[:, :, L:L + NS], ob[:, :, 0:NS], reads=[obk], writes=[('oT', L)])
        pg.barrier()
```
